# Optimizing a Trainium2 kernel written in Bass

```python
import math
import jax, jax.numpy as jnp
from jax import lax
import numpy as np

D_MODEL = 1024
BATCH = 8
SEQ = 2048
DEPTH = 4
DEC_BATCH = 128
DEC_SEQ = 8
PAST_LEN = 2048
PAGE_SIZE = 128

EXPAND = 2
D_INNER = EXPAND * D_MODEL
N_A_LAYERS = DEPTH // 2
N_B_LAYERS = DEPTH - N_A_LAYERS
RWKV_HEAD = 64
RWKV_HEADS = D_INNER // RWKV_HEAD
DECAY_LORA = 64
AAA_LORA = 64
DIFF_HEAD = 64
DIFF_HEADS = D_INNER // (2 * DIFF_HEAD)
DIFF_KV_HEADS = DIFF_HEADS // 2
DIFF_GROUP = DIFF_HEADS // DIFF_KV_HEADS
KV_WIDTH = DIFF_KV_HEADS * 2 * DIFF_HEAD
ROPE_DIMS = DIFF_HEAD // 4
ROPE_THETA = 500000.0
Q_BLOCK = 128
LN_EPS = 1e-5
GN_EPS = 64e-5
DEEPNORM_ALPHA = (2 * DEPTH) ** 0.25
DEEPNORM_BETA = (8 * DEPTH) ** -0.25
NEG_BIG = -1e30

kernel_name = "rwkv7_diffattn_yoco_step"


def _layer_norm(x, g, b):
    xf = x.astype(jnp.float32)
    mean = xf.mean(-1, keepdims=True)
    var = jnp.square(xf - mean).mean(-1, keepdims=True)
    return ((xf - mean) * lax.rsqrt(var + LN_EPS)).astype(x.dtype) * g + b


def _ada(c, w, b, n):
    m = jnp.einsum('bd,de->be', jax.nn.silu(c), w) + b
    return jnp.split(m[:, None, :], n, axis=-1)


def _rope(x, pos):
    half = ROPE_DIMS // 2
    inv = ROPE_THETA ** (-jnp.arange(half, dtype=jnp.float32) * 2.0 / ROPE_DIMS)
    ang = pos.astype(jnp.float32)[:, None] * inv[None, :]
    shape = (1, ang.shape[0]) + (1,) * (x.ndim - 3) + (half,)
    cos = jnp.cos(ang).reshape(shape).astype(x.dtype)
    sin = jnp.sin(ang).reshape(shape).astype(x.dtype)
    x1, x2 = x[..., :half], x[..., half:ROPE_DIMS]
    return jnp.concatenate([x1 * cos - x2 * sin, x2 * cos + x1 * sin, x[..., ROPE_DIMS:]], axis=-1)


def _rwkv7_mixer(h, shift0, wkv0, mu, w_rkvg, w0, w_decay1, w_decay2, a0, w_a1, w_a2,
                 k_k, k_a, r_k, gn_g, gn_b, w_o):
    B, T, _ = h.shape
    h_prev = jnp.concatenate([shift0[:, None, :].astype(h.dtype), h[:, :-1]], axis=1)
    xs = h[None] + (h_prev - h)[None] * mu[:, None, None, :]
    r, k, v, g = jnp.einsum('nbtd,nde->nbte', xs[:4], w_rkvg)
    w_log = -jax.nn.softplus(-(w0 + jnp.tanh(xs[4] @ w_decay1) @ w_decay2)) - 0.5
    decay = jnp.exp(-jnp.exp(w_log))
    a = jax.nn.sigmoid(a0 + (xs[5] @ w_a1) @ w_a2)
    heads = lambda t: t.reshape(B, T, RWKV_HEADS, RWKV_HEAD)
    kk = heads(k * k_k)
    kk = kk * lax.rsqrt(jnp.maximum(jnp.sum(jnp.square(kk.astype(jnp.float32)), -1, keepdims=True), 1e-24)).astype(kk.dtype)
    k = k * (1 + (a - 1) * k_a)
    r, decay, k, v, a = (heads(t) for t in (r, decay, k, v, a))
    seq = tuple(jnp.moveaxis(t, 1, 0) for t in (r, decay, k, v, -kk, kk * a))

    def step(S, inp):
        r_t, w_t, k_t, v_t, za_t, zb_t = inp
        sa = jnp.einsum('bhij,bhj->bhi', S, za_t)
        S = (S * w_t[:, :, None, :] + sa[..., None] * zb_t[:, :, None, :]
             + v_t[..., None] * k_t[:, :, None, :]).astype(S.dtype)
        return S, jnp.einsum('bhij,bhj->bhi', S, r_t)

    s_last, y = lax.scan(step, wkv0, seq)
    y = jnp.moveaxis(y, 0, 1).astype(jnp.float32)
    mean = y.mean(-1, keepdims=True)
    var = jnp.square(y - mean).mean(-1, keepdims=True)
    y = ((y - mean) * lax.rsqrt(var + GN_EPS)).astype(h.dtype).reshape(B, T, D_INNER) * gn_g + gn_b
    bonus = jnp.sum(r * k * r_k, axis=-1, keepdims=True) * v
    out = (y + bonus.reshape(B, T, D_INNER)) * jax.nn.silu(g)
    return out @ w_o, h[:, -1], s_last


def _shared_kv(x, c, ada_kv_w, ada_kv_b, w_kv, pos):
    B, T, _ = x.shape
    shift, scale = _ada(c, ada_kv_w, ada_kv_b, 2)
    k, v = jnp.split((x * (1 + scale) + shift) @ w_kv, 2, axis=-1)
    k = _rope(k.reshape(B, T, DIFF_KV_HEADS, 2, DIFF_HEAD), pos).reshape(B, T, DIFF_KV_HEADS, 2 * DIFF_HEAD)
    return k, v.reshape(B, T, DIFF_KV_HEADS, 2 * DIFF_HEAD)


def _diff_attn(q, k, v, q_pos, k_pos, lam):
    s = jnp.einsum('btkgmd,bskmd->bkgmts', q, k).astype(jnp.float32) * (DIFF_HEAD ** -0.5)
    mask = k_pos[None, :] <= q_pos[:, None]
    p = jax.nn.softmax(jnp.where(mask, s, NEG_BIG), axis=-1)
    amap = (p[:, :, :, 0] - lam * p[:, :, :, 1]).astype(v.dtype)
    return jnp.einsum('bkgts,bskv->btkgv', amap, v)


def _diff_mixer(h, k_all, v_all, q_pos, k_pos, blocked, lam_init, w_qg, lam_qk, subln_g, w_o):
    B, T, _ = h.shape
    q, gate = jnp.split(h @ w_qg, 2, axis=-1)
    q = _rope(q.reshape(B, T, DIFF_KV_HEADS, DIFF_GROUP, 2, DIFF_HEAD), q_pos)
    k = k_all.reshape(k_all.shape[0], k_all.shape[1], DIFF_KV_HEADS, 2, DIFF_HEAD)
    lq = lam_qk.astype(jnp.float32)
    lam = jnp.exp(jnp.sum(lq[0] * lq[1])) - jnp.exp(jnp.sum(lq[2] * lq[3])) + lam_init
    if blocked:
        outs = []
        for i in range(T // Q_BLOCK):
            lo, hi = i * Q_BLOCK, (i + 1) * Q_BLOCK
            outs.append(_diff_attn(q[:, lo:hi], k[:, :hi], v_all[:, :hi], q_pos[lo:hi], k_pos[:hi], lam))
        o = jnp.concatenate(outs, axis=1)
    else:
        o = _diff_attn(q, k, v_all, q_pos, k_pos, lam)
    of = o.astype(jnp.float32)
    o = (of * lax.rsqrt(jnp.mean(jnp.square(of), -1, keepdims=True) + LN_EPS)).astype(h.dtype) * subln_g
    o = (o * (1.0 - lam_init)).reshape(B, T, D_INNER) * jax.nn.silu(gate)
    return o @ w_o


def _run_group(x, c, pos, shift0, wkv0, past_k, past_v, blocked, P):
    new_shift, new_wkv = [], []
    k_new = v_new = k_all = v_all = k_pos = None
    for l in range(DEPTH):
        if l == N_A_LAYERS:
            k_new, v_new = _shared_kv(x, c, P['ada_kv_w'], P['ada_kv_b'], P['w_kv'], pos)
            if past_k is None:
                k_all, v_all, k_pos = k_new, v_new, pos
            else:
                k_all = jnp.concatenate([past_k.astype(k_new.dtype), k_new], axis=1)
                v_all = jnp.concatenate([past_v.astype(v_new.dtype), v_new], axis=1)
                k_pos = jnp.arange(k_all.shape[1], dtype=jnp.int32)
        shift, scale, gate = _ada(c, P['ada_w'][l], P['ada_b'][l], 3)
        h = x * (1 + scale) + shift
        if l < N_A_LAYERS:
            out, last_h, s_last = _rwkv7_mixer(
                h, shift0[l], wkv0[l], P['mu'][l], P['w_rkvg'][l], P['w0'][l], P['w_decay1'][l],
                P['w_decay2'][l], P['a0'][l], P['w_a1'][l], P['w_a2'][l], P['k_k'][l], P['k_a'][l],
                P['r_k'][l], P['gn_g'][l], P['gn_b'][l], P['w_o_a'][l])
            new_shift.append(last_h)
            new_wkv.append(s_last)
        else:
            j = l - N_A_LAYERS
            lam_init = 0.8 - 0.6 * math.exp(-0.3 * l)
            out = _diff_mixer(h, k_all, v_all, pos, k_pos, blocked, lam_init, P['w_qg'][j],
                              P['lam_qk'][j], P['subln_g'][j], P['w_o_b'][j])
        x = _layer_norm(DEEPNORM_ALPHA * x + (1 + gate) * out, P['ln_g'][l], P['ln_b'][l])
    return x, k_new, v_new, jnp.stack(new_wkv), jnp.stack(new_shift)


def setup_inputs(seed: int = 0) -> dict:
    key = jax.random.key(seed)
    ks = jax.random.split(key, 36)
    nrm = lambda i, shape, s: jax.random.normal(ks[i], shape, jnp.float32) * s
    n_pages = PAST_LEN // PAGE_SIZE
    n_used = DEC_BATCH * n_pages
    n_phys = n_used + max(1, n_used // 4)
    page_table = jax.random.permutation(ks[4], n_phys)[:n_used].reshape(DEC_BATCH, n_pages).astype(jnp.int32)
    D, E = D_MODEL, D_INNER
    ada_s = 0.3 * D ** -0.5
    v_scale = jnp.array([1.0, 1.0, DEEPNORM_BETA, 1.0], jnp.float32)[None, :, None, None]
    return {
        'x_prompt': nrm(0, (BATCH, SEQ, D), 1.0),
        'x_sample': nrm(1, (DEC_BATCH, DEC_SEQ, D), 1.0),
        'cache_k': nrm(2, (n_phys, PAGE_SIZE, DIFF_KV_HEADS, 2 * DIFF_HEAD), 1.0),
        'cache_v': nrm(3, (n_phys, PAGE_SIZE, DIFF_KV_HEADS, 2 * DIFF_HEAD), DEEPNORM_BETA),
        'page_table': page_table,
        'state_wkv': nrm(5, (N_A_LAYERS, DEC_BATCH, RWKV_HEADS, RWKV_HEAD, RWKV_HEAD), 0.3),
        'state_shift': nrm(6, (N_A_LAYERS, DEC_BATCH, D), 1.0),
        'c_prompt': nrm(7, (BATCH, D), 1.0),
        'c_sample': nrm(8, (DEC_BATCH, D), 1.0),
        'ada_w': nrm(9, (DEPTH, D, 3 * D), ada_s),
        'ada_b': nrm(10, (DEPTH, 3 * D), 0.02),
        'ln_g': 1.0 + nrm(11, (DEPTH, D), 0.05),
        'ln_b': nrm(12, (DEPTH, D), 0.02),
        'mu': jax.random.uniform(ks[13], (N_A_LAYERS, 6, D), jnp.float32),
        'w_rkvg': nrm(14, (N_A_LAYERS, 4, D, E), D ** -0.5) * v_scale,
        'w0': jax.random.uniform(ks[15], (N_A_LAYERS, E), jnp.float32, minval=-4.0, maxval=1.0),
        'w_decay1': nrm(16, (N_A_LAYERS, D, DECAY_LORA), D ** -0.5),
        'w_decay2': nrm(17, (N_A_LAYERS, DECAY_LORA, E), 0.1),
        'a0': nrm(18, (N_A_LAYERS, E), 0.5),
        'w_a1': nrm(19, (N_A_LAYERS, D, AAA_LORA), D ** -0.5),
        'w_a2': nrm(20, (N_A_LAYERS, AAA_LORA, E), 0.1),
        'k_k': 0.85 + nrm(21, (N_A_LAYERS, E), 0.05),
        'k_a': 1.0 + nrm(22, (N_A_LAYERS, E), 0.05),
        'r_k': nrm(23, (N_A_LAYERS, RWKV_HEADS, RWKV_HEAD), 0.1),
        'gn_g': 1.0 + nrm(24, (N_A_LAYERS, E), 0.05),
        'gn_b': nrm(25, (N_A_LAYERS, E), 0.02),
        'w_o_a': nrm(26, (N_A_LAYERS, E, D), E ** -0.5 * DEEPNORM_BETA),
        'ada_kv_w': nrm(27, (D, 2 * D), ada_s),
        'ada_kv_b': nrm(28, (2 * D,), 0.02),
        'w_kv': jnp.concatenate([nrm(29, (D, KV_WIDTH), D ** -0.5),
                                 nrm(30, (D, KV_WIDTH), D ** -0.5 * DEEPNORM_BETA)], axis=1),
        'w_qg': nrm(31, (N_B_LAYERS, D, 2 * E), D ** -0.5),
        'lam_qk': nrm(32, (N_B_LAYERS, 4, DIFF_HEAD), 0.1),
        'subln_g': 1.0 + nrm(33, (N_B_LAYERS, 2 * DIFF_HEAD), 0.05),
        'w_o_b': nrm(34, (N_B_LAYERS, E, D), E ** -0.5 * DEEPNORM_BETA),
    }


def reference(x_prompt, x_sample, cache_k, cache_v, page_table, state_wkv, state_shift, c_prompt, c_sample,
              ada_w, ada_b, ln_g, ln_b, mu, w_rkvg, w0, w_decay1, w_decay2, a0, w_a1, w_a2, k_k, k_a, r_k,
              gn_g, gn_b, w_o_a, ada_kv_w, ada_kv_b, w_kv, w_qg, lam_qk, subln_g, w_o_b):
    P = {'ada_w': ada_w, 'ada_b': ada_b, 'ln_g': ln_g, 'ln_b': ln_b, 'mu': mu, 'w_rkvg': w_rkvg, 'w0': w0,
         'w_decay1': w_decay1, 'w_decay2': w_decay2, 'a0': a0, 'w_a1': w_a1, 'w_a2': w_a2, 'k_k': k_k,
         'k_a': k_a, 'r_k': r_k, 'gn_g': gn_g, 'gn_b': gn_b, 'w_o_a': w_o_a, 'ada_kv_w': ada_kv_w,
         'ada_kv_b': ada_kv_b, 'w_kv': w_kv, 'w_qg': w_qg, 'lam_qk': lam_qk, 'subln_g': subln_g, 'w_o_b': w_o_b}
    B, T, _ = x_prompt.shape
    pos_p = jnp.arange(T, dtype=jnp.int32)
    shift0_p = jnp.zeros((N_A_LAYERS, B, D_MODEL), x_prompt.dtype)
    wkv0_p = jnp.zeros((N_A_LAYERS, B, RWKV_HEADS, RWKV_HEAD, RWKV_HEAD), x_prompt.dtype)
    y_prompt, k_prompt, v_prompt, wkv_prompt, shift_prompt = _run_group(
        x_prompt, c_prompt, pos_p, shift0_p, wkv0_p, None, None, True, P)
    DB, TS, _ = x_sample.shape
    past_len = page_table.shape[1] * cache_k.shape[1]
    past_k = cache_k[page_table].reshape(DB, past_len, DIFF_KV_HEADS, 2 * DIFF_HEAD)
    past_v = cache_v[page_table].reshape(DB, past_len, DIFF_KV_HEADS, 2 * DIFF_HEAD)
    pos_s = past_len + jnp.arange(TS, dtype=jnp.int32)
    y_sample, k_sample, v_sample, wkv_sample, shift_sample = _run_group(
        x_sample, c_sample, pos_s, state_shift, state_wkv, past_k, past_v, False, P)
    return (y_prompt, y_sample, k_prompt, v_prompt, k_sample, v_sample, wkv_prompt, shift_prompt, wkv_sample, shift_sample)
```

```python
import math
import numpy as np
import ml_dtypes
import concourse.bass as bass
import concourse.mybir as mybir
from concourse.bass_utils import run_bass_kernel_spmd

F32 = mybir.dt.float32
BF16 = mybir.dt.bfloat16
I32 = mybir.dt.int32
AF = mybir.ActivationFunctionType
ALU = mybir.AluOpType
AX = mybir.AxisListType

D = 1024
E = 2048
NCORE = 8
DEPTH = 4
NA = 2
ALPHA = (2 * DEPTH) ** 0.25
LN_EPS = 1e-5
GN_EPS = 64e-5
LWC = -math.exp(-0.5)
SEM_LIMIT = 30000


class Buf:
    __slots__ = ("w", "r", "const")

    def __init__(self, const=False):
        self.w = None
        self.r = []
        self.const = const


class Tile:
    def __init__(self, nc, name, shape, dtype, psum=False):
        if psum:
            self.t = nc.alloc_psum_tensor(name, list(shape), dtype)
        else:
            self.t = nc.alloc_sbuf_tensor(name, list(shape), dtype)
        self.b = Buf()
        self.shape = shape
        self.name = name

    def __getitem__(self, k):
        return self.t[k]


class Sched:
    ENGS = ("pe", "act", "dve", "pool", "sp")

    def __init__(self, nc, n_dma_sems=8):
        self.nc = nc
        self.prog = {e: [] for e in self.ENGS}
        self.sem = {}
        self.cnt = {}
        self.nsem = 0
        for e in ("pe", "act", "dve", "pool"):
            self._new_sem(e)
        self.seen = {e: {} for e in self.ENGS}
        self.dma_sems = {}
        self.dma_cnt = {}
        self.dma_rr = {}
        for q in ("sp", "act", "pool"):
            self.dma_sems[q] = [nc.alloc_semaphore(name=f"dma_{q}_{i}") for i in range(n_dma_sems)]
            self.dma_cnt[q] = [0] * n_dma_sems
            self.dma_rr[q] = 0
        self.n_instr = 0

    def _new_sem(self, e):
        self.sem[e] = self.nc.alloc_semaphore(name=f"sem_{e}_{self.nsem}")
        self.nsem += 1
        self.cnt[e] = 0

    def _need(self, e, toks):
        best = {}
        for tok in toks:
            if tok is None:
                continue
            s, v, src = tok
            if src == "pe" and e == "pe":
                continue
            if self.seen[e].get(id(s), 0) >= v:
                continue
            if id(s) not in best or best[id(s)][1] < v:
                best[id(s)] = (s, v)
        for s, v in best.values():
            self.seen[e][id(s)] = v
            self.prog[e].append(("wait", s, v))
            self.n_instr += 1

    @staticmethod
    def _deps(reads, writes):
        toks = []
        for b in reads:
            toks.append(b.w)
        for b in writes:
            toks.append(b.w)
            toks.extend(b.r)
        return toks

    @staticmethod
    def _commit(tok, reads, writes):
        for b in reads:
            if not b.const:
                b.r.append(tok)
        for b in writes:
            b.w = tok
            b.r = []

    def op(self, e, fn, r=(), w=()):
        r = [x.b if hasattr(x, "b") else x for x in r]
        w = [x.b if hasattr(x, "b") else x for x in w]
        self._need(e, self._deps(r, w))
        if self.cnt[e] >= SEM_LIMIT:
            self._new_sem(e)
        self.cnt[e] += 1
        tok = (self.sem[e], self.cnt[e], e)
        if e == "pe":
            self.last_pe = (self.sem[e], self.cnt[e])
        self.prog[e].append(("op", fn, self.sem[e]))
        self.n_instr += 1
        self._commit(tok, r, w)
        return tok

    def dma(self, q, fn, r=(), w=()):
        r = [x.b if hasattr(x, "b") else x for x in r]
        w = [x.b if hasattr(x, "b") else x for x in w]
        i = self.dma_rr[q]
        self.dma_rr[q] = (i + 1) % len(self.dma_sems[q])
        s = self.dma_sems[q][i]
        toks = self._deps(r, w)
        if self.dma_cnt[q][i] > 0:
            toks.append((s, self.dma_cnt[q][i], "dma"))
        self._need(q, toks)
        self.dma_cnt[q][i] += 16
        tok = (s, self.dma_cnt[q][i], "dma")
        self.prog[q].append(("dma", fn, s))
        self.n_instr += 1
        self._commit(tok, r, w)
        return tok

    def wait_all(self, e, bufs):
        self._need(e, [b.w for b in bufs])

    def _all_toks(self):
        toks = []
        for e in ("pe", "act", "dve", "pool"):
            if self.cnt[e] > 0:
                toks.append((self.sem[e], self.cnt[e], "x"))
        for q in self.dma_sems:
            for s, c in zip(self.dma_sems[q], self.dma_cnt[q]):
                if c > 0:
                    toks.append((s, c, "dma"))
        return toks

    def barrier(self):
        toks = self._all_toks()
        for e in self.ENGS:
            self._need(e, toks)

    def drain(self, e):
        self._need(e, self._all_toks())

    def emit(self):
        nc = self.nc
        with nc.Block() as block:
            def run(e):
                def body(eng):
                    for item in self.prog[e]:
                        if item[0] == "wait":
                            eng.wait_ge(item[1], item[2])
                        elif item[0] == "op":
                            item[1](eng).then_inc(item[2], 1)
                        else:
                            item[1](eng).then_inc(item[2], 16)
                return body
            block.tensor(run("pe"))
            block.scalar(run("act"))
            block.vector(run("dve"))
            block.gpsimd(run("pool"))
            block.sync(run("sp"))


class Rot:
    def __init__(self, items):
        self.items = items
        self.i = 0

    def next(self):
        x = self.items[self.i]
        self.i = (self.i + 1) % len(self.items)
        return x


def make_consts(T, PAST):
    c = {}
    i = np.arange(128)
    s, t = i[:, None], i[None, :]
    same8 = (s // 8) == (t // 8)
    ident = np.eye(128, dtype=np.float32)
    su = (s < t).astype(np.float32)
    ui = (s <= t).astype(np.float32)
    sl = (s > t).astype(np.float32)
    mats = np.stack([
        ident,
        su, ui, sl,
        LWC * ui,
        LWC * su,
        LWC * sl,
        LWC * ui * same8,
        LWC * su * same8,
        LWC * sl * same8,
    ], axis=1).astype(np.float32)
    c["mats"] = mats
    mask4 = np.concatenate([su, ui, su, ui], axis=1).astype(np.float32)
    c["mask4"] = mask4
    red = np.zeros((128, 17), np.float32)
    red[:, 0] = LWC
    for b in range(16):
        red[b * 8:(b + 1) * 8, 1 + b] = LWC
    c["red"] = red
    mb = np.where(s <= t, 0.0, -30000.0).astype(np.float32)
    c["maskb"] = np.concatenate([mb, mb, mb, mb], axis=1).astype(ml_dtypes.bfloat16)
    mb8 = mb[:8, :8]
    m8 = np.zeros((128, 16), np.float32)
    m8[:8, :8] = mb8
    m8[:8, 8:] = mb8
    c["maskb8"] = m8.astype(ml_dtypes.bfloat16)
    half = 8
    inv = (500000.0 ** (-np.arange(half, dtype=np.float32) * 2.0 / 16)).astype(np.float32)
    def tab(pos):
        ang = pos.astype(np.float32)[:, None] * inv[None, :]
        return np.concatenate([np.cos(ang), np.sin(ang)], axis=1).astype(np.float32)
    c["rope_p"] = tab(np.arange(T))
    c["rope_s"] = tab(PAST + (np.arange(128) % 8))
    return c


class StopBuild(Exception):
    pass


import os as _os
_STOPAT = _os.environ.get("STOPAT", "")


_DBG = [] if _os.environ.get("PEDBG") else None


def ck(name):
    if _STOPAT and name == _STOPAT:
        raise StopBuild()


def build_program(T, NPG, NPHYS, stage=99):
    NT = T // 128
    PAST = NPG * 128
    nc = bass.Bass("TRN2", target_bir_lowering=False)
    S = Sched(nc)

    def din(name, shape, dt=F32):
        return nc.dram_tensor(name, list(shape), dt, kind="ExternalInput").ap()

    def dout(name, shape, dt=F32):
        return nc.dram_tensor(name, list(shape), dt, kind="ExternalOutput").ap()

    def dscr(name, shape, dt=F32):
        return nc.dram_tensor(name, list(shape), dt, kind="Internal").ap()

    xp = din("xp", [T, D]); xs = din("xs", [128, D])
    cache_k = din("cache_k", [NPHYS * 128, 1024]); cache_v = din("cache_v", [NPHYS * 128, 1024])
    ptab = din("ptab", [1, 16 * NPG], I32)
    swkv = din("swkv", [NA, 16, 32, 64, 64]); sshift = din("sshift", [NA, 16, D])
    cvec = din("cvec", [17, D])
    ada_w = din("ada_w", [DEPTH, D, 3 * D]); ada_b = din("ada_b", [DEPTH, 3 * D])
    ln_g = din("ln_g", [DEPTH, D]); ln_b = din("ln_b", [DEPTH, D])
    mu = din("mu", [NA, 6, D])
    w_rkvg = din("w_rkvg", [NA, 4, D, E]); w0 = din("w0", [NA, E])
    w_d1 = din("w_decay1", [NA, D, 64]); w_d2 = din("w_decay2", [NA, 64, E])
    a0 = din("a0", [NA, E]); w_a1 = din("w_a1", [NA, D, 64]); w_a2 = din("w_a2", [NA, 64, E])
    k_k = din("k_k", [NA, E]); k_a = din("k_a", [NA, E]); r_k = din("r_k", [NA, E])
    gn_g = din("gn_g", [NA, E]); gn_b = din("gn_b", [NA, E])
    w_o_a = din("w_o_a", [NA, E, D])
    ada_kv_w = din("ada_kv_w", [D, 2 * D]); ada_kv_b = din("ada_kv_b", [1, 2 * D])
    w_kv = din("w_kv", [D, 2048]); w_qg = din("w_qg", [2, D, 4096])
    lam_qk = din("lam_qk", [2, 256]); subln_g = din("subln_g", [2, 128]); w_o_b = din("w_o_b", [2, E, D])
    c_mats = din("c_mats", [128, 10, 128]); c_mask4 = din("c_mask4", [128, 512]); c_red = din("c_red", [128, 17])
    c_maskb = din("c_maskb", [128, 512], BF16); c_maskb8 = din("c_maskb8", [128, 16], BF16)
    c_rope_p = din("c_rope_p", [T, 16]); c_rope_s = din("c_rope_s", [128, 16])
    y_p = dout("y_p", [T, D]); y_s = dout("y_s", [128, D])
    k_p = dout("k_p", [T, 1024]); v_p = dout("v_p", [T, 1024])
    k_s = dout("k_s", [128, 1024]); v_s = dout("v_s", [128, 1024])
    wkv_p = dout("wkv_p", [NA, 32, 64, 64]); shift_p = dout("shift_p", [NA, D])
    wkv_s = dout("wkv_s", [NA, 16, 32, 64, 64]); shift_s = dout("shift_s", [NA, 16, D])
    out_bufs = []
    xb_p = dscr("xb_p", [T, D]); xb_s = dscr("xb_s", [128, D])
    xbuf_p = Buf(); xbuf_s = Buf()
    wb_rkvg = dscr("wb_rkvg", [NA, 4, D, E], BF16); wb_oa = dscr("wb_oa", [NA, E, D], BF16)
    wb_d1 = dscr("wb_d1", [NA, D, 64], BF16); wb_d2 = dscr("wb_d2", [NA, 64, E], BF16)
    wb_a1 = dscr("wb_a1", [NA, D, 64], BF16); wb_a2 = dscr("wb_a2", [NA, 64, E], BF16)
    wb_kv = dscr("wb_kv", [D, 2048], BF16); wb_qg = dscr("wb_qg", [2, D, 4096], BF16)
    wb_ob = dscr("wb_ob", [2, E, D], BF16)
    wbufs = []

    def DMA(q, out_ap, in_ap, r=(), w=()):
        S.dma(q, lambda e: e.dma_start(out=out_ap, in_=in_ap), r=r, w=w)

    def conv(dst, src, rows, chunk=1024):
        for r0 in range(0, rows, chunk):
            r1 = min(rows, r0 + chunk)
            b_ = Buf()
            DMA("pool", dst[r0:r1], src[r0:r1], w=[b_])
            b_.const = True
            wbufs.append(b_)
    conv(wb_rkvg.rearrange("l n d e -> (l n d) e"), w_rkvg.rearrange("l n d e -> (l n d) e"), NA * 4 * D)
    conv(wb_oa.rearrange("l e d -> (l e) d"), w_o_a.rearrange("l e d -> (l e) d"), NA * E, 2048)
    conv(wb_d1.rearrange("l d e -> (l d) e"), w_d1.rearrange("l d e -> (l d) e"), NA * D, 2048)
    conv(wb_a1.rearrange("l d e -> (l d) e"), w_a1.rearrange("l d e -> (l d) e"), NA * D, 2048)
    conv(wb_d2.rearrange("l d e -> (l d) e"), w_d2.rearrange("l d e -> (l d) e"), NA * 64, 2048)
    conv(wb_a2.rearrange("l d e -> (l d) e"), w_a2.rearrange("l d e -> (l d) e"), NA * 64, 2048)
    conv(wb_kv, w_kv, D)
    conv(wb_qg.rearrange("l d e -> (l d) e"), w_qg.rearrange("l d e -> (l d) e"), 2 * D, 512)
    conv(wb_ob.rearrange("l e d -> (l e) d"), w_o_b.rearrange("l e d -> (l e) d"), 2 * E, 2048)

    mats = Tile(nc, "mats", [128, 10, 128], F32)
    S.dma("sp", lambda e: e.dma_start(out=mats[:, :, :], in_=c_mats), w=[mats])
    mask4 = Tile(nc, "mask4", [128, 512], F32)
    S.dma("sp", lambda e: e.dma_start(out=mask4[:, :], in_=c_mask4), w=[mask4])
    red = Tile(nc, "red", [128, 17], F32)
    S.dma("sp", lambda e: e.dma_start(out=red[:, :], in_=c_red), w=[red])
    maskb = Tile(nc, "maskb", [128, 512], BF16)
    S.dma("sp", lambda e: e.dma_start(out=maskb[:, :], in_=c_maskb), w=[maskb])
    maskb8 = Tile(nc, "maskb8", [128, 16], BF16)
    S.dma("sp", lambda e: e.dma_start(out=maskb8[:, :], in_=c_maskb8), w=[maskb8])
    identb = Tile(nc, "identb", [128, 128], BF16)
    S.op("dve", lambda e: e.tensor_copy(out=identb[:, :], in_=mats[:, 0, :]), r=[mats], w=[identb])
    for t_ in (mats, mask4, red, maskb, maskb8, identb):
        t_.b.const = True
    IDF = lambda n=128: mats[0:n, 0, 0:n]

    PB = [Tile(nc, f"pb{i}", [128, 512], F32, psum=True) for i in range(8)]
    pb_x0, pb_x1 = PB[0], PB[1]
    projR = Rot([PB[2], PB[3]])
    pb_tr = PB[4]
    scanR = Rot([PB[5], PB[6]])
    pb_y = PB[7]

    def ev_copy(eng, out_ap, in_ap, r, w):
        if eng == "act":
            S.op("act", lambda e: e.activation(out=out_ap, in_=in_ap, func=AF.Copy), r=r, w=w)
        else:
            S.op(eng, lambda e: e.tensor_copy(out=out_ap, in_=in_ap), r=r, w=w)

    def TTo(eng, out_ap, a, b_, op, r, w):
        S.op(eng, lambda e: e.tensor_tensor(out=out_ap, in0=a, in1=b_, op=op), r=r, w=w)

    def STT(eng, out_ap, a, sc, b_, op0, op1, r, w):
        S.op(eng, lambda e: e.scalar_tensor_tensor(out=out_ap, in0=a, scalar=sc, in1=b_, op0=op0, op1=op1), r=r, w=w)

    def TS(eng, out_ap, a, s1, s2, op0, op1, r, w):
        if op1 is None:
            S.op(eng, lambda e: e.tensor_scalar(out=out_ap, in0=a, scalar1=s1, scalar2=None, op0=op0), r=r, w=w)
        else:
            S.op(eng, lambda e: e.tensor_scalar(out=out_ap, in0=a, scalar1=s1, scalar2=s2, op0=op0, op1=op1), r=r, w=w)

    def ACT(out_ap, in_ap, func, r, w, scale=1.0, bias=0.0):
        S.op("act", lambda e: e.activation(out=out_ap, in_=in_ap, func=func, scale=scale, bias=bias), r=r, w=w)

    pe_inflight = {}

    def pe_guard(lhs_ap, bank):
        b0 = lhs_ap.base_partition(); k_ = lhs_ap.shape[0]
        rg = frozenset(range(b0 // 32, (b0 + k_ + 31) // 32))
        if len(rg) == 4:
            pe_inflight.clear()
            return
        hazard = any((not (rg & g2)) and bank in banks for g2, banks in pe_inflight.items())
        if hazard:
            if getattr(S, "last_pe", None) is not None:
                S.prog["pe"].append(("wait", S.last_pe[0], S.last_pe[1]))
                S.n_instr += 1
            pe_inflight.clear()
        pe_inflight.setdefault(rg, set()).add(bank)

    def MM(out_ap, lhsT, rhs, start, stop, r, w):
        if _DBG is not None:
            _DBG.append((w[0].name, start, stop, len(S.prog["pe"])))
        pe_guard(lhsT, w[0].name)
        S.op("pe", lambda e: e.matmul(out_ap, lhsT=lhsT, rhs=rhs, start=start, stop=stop), r=r, w=w)

    def TR(out_ap, in_ap, ident_ap, r, w):
        if _DBG is not None:
            _DBG.append((w[0].name, "T", "T", len(S.prog["pe"])))
        pe_guard(in_ap, w[0].name)
        S.op("pe", lambda e: e.transpose(out_ap, in_ap, ident_ap), r=r, w=w)

    evR = Rot(["act", "dve"])

    def RSQRT(t_, src_ap, dst_ap, eps, op):
        TS("dve", dst_ap, src_ap, eps, None, op, None, r=[t_], w=[t_])
        S.op("act", lambda e: e.activation(out=dst_ap, in_=dst_ap, func=AF.Sqrt), r=[t_], w=[t_])
        S.op("dve", lambda e: e.reciprocal(out=dst_ap, in_=dst_ap), r=[t_], w=[t_])

    csil = Tile(nc, "rows17", [17, D], F32)
    S.dma("sp", lambda e: e.dma_start(out=csil[:, :], in_=cvec), w=[csil])
    ACT(csil[:, :], csil[:, :], AF.Silu, r=[csil], w=[csil])
    scT = Tile(nc, "scT", [128, 8, 17], F32)
    for dc in range(8):
        pb = projR.next()
        TR(pb[:, 0:17], csil[0:17, dc * 128:(dc + 1) * 128], IDF(17), r=[csil, mats], w=[pb])
        ev_copy("dve", scT[:, dc, :], pb[:, 0:17], r=[pb], w=[scT])
    wpc_l = [Tile(nc, f"wpc{i}", [128, 8, 512], BF16) for i in range(3)]
    wpc = Rot(wpc_l)
    screp_mem = Tile(nc, "screp_mem", [128, 2, 8, 128], F32)

    class View:
        def __init__(self, base, fn):
            self.b = base.b
            self.fn = fn

        def __getitem__(self, k):
            return self.fn()[k]
    screp = {"p": View(screp_mem, lambda: screp_mem.t[:, 0, :, :]), "s": View(screp_mem, lambda: screp_mem.t[:, 1, :, :])}
    S.op("dve", lambda e: e.tensor_copy(out=screp["p"][:, :, :], in_=scT[:, :, 0:1].to_broadcast([128, 8, 128])),
         r=[scT], w=[screp["p"]])
    S.op("dve", lambda e: e.tensor_copy(out=screp["s"][:, :, :].rearrange("p c (b t) -> p c b t", t=8),
                                        in_=scT[:, :, 1:17].unsqueeze(3).to_broadcast([128, 8, 16, 8])),
         r=[scT], w=[screp["s"]])
    adaw_v = View(wpc_l[0], lambda: wpc_l[0].t[:, :, :].bitcast(F32))
    adab_t = Tile(nc, "adab", [128, 256], F32)

    def compute_mod(dst, grp, wsrc, bsrc, ncols, plus1_from):
        for cb in range(ncols // 256):
            wt = adaw_v; bt = adab_t
            csl = slice(cb * 256, (cb + 1) * 256)
            DMA("sp", wt[:, :, :], wsrc[:, csl].rearrange("(c p) n -> p c n", p=128), w=[wt])
            DMA("sp", bt[:, :], bsrc[:, csl].partition_broadcast(128), w=[bt])
            pb = projR.next()
            for dc in range(8):
                MM(pb[:, 0:256], screp[grp][:, dc, :], wt[:, dc, :], dc == 0, dc == 7, r=[screp[grp], wt], w=[pb])
            if cb * 256 >= plus1_from:
                STT("dve", dst[:, csl], pb[:, 0:256], 1.0, bt[:, :], ALU.add, ALU.add, r=[pb, bt], w=[dst])
            else:
                TTo("dve", dst[:, csl], pb[:, 0:256], bt[:, :], ALU.add, r=[pb, bt], w=[dst])

    x_t = Rot([Tile(nc, f"x{i}", [128, D], F32) for i in range(1)])
    h_t = Tile(nc, "h", [128, D], F32)
    lnv = {}
    lng_t = Tile(nc, "lng", [128, D], F32); lnb_t = Tile(nc, "lnb", [128, D], F32)
    st6 = Tile(nc, "st6", [128, 2, 6], F32); mv = Tile(nc, "mv", [128, 2], F32); rstd = Tile(nc, "rstd", [128, 1], F32)
    xn_t = h_t
    tmpD = Tile(nc, "tmpD", [128, D], F32)
    xo_t = Rot([Tile(nc, "xo0", [128, D], F32)])

    def load_ln(l):
        S.dma("sp", lambda e: e.dma_start(out=lng_t[:, :], in_=ln_g[l:l + 1, :].partition_broadcast(128)), w=[lng_t])
        S.dma("sp", lambda e: e.dma_start(out=lnb_t[:, :], in_=ln_b[l:l + 1, :].partition_broadcast(128)), w=[lnb_t])

    def residual_ln(xt, gate1_ap, gate_tile, dst_ap, dst_buf, also=None):
        for hf, pb in enumerate((pb_x0, pb_x1)):
            sl = slice(hf * 512, (hf + 1) * 512)
            TTo("dve", tmpD[:, sl], pb[:, :], gate1_ap[:, sl], ALU.mult, r=[pb, gate_tile], w=[tmpD])
        STT("dve", xn_t[:, :], xt[:, :], ALPHA, tmpD[:, :], ALU.mult, ALU.add, r=[xt, tmpD], w=[xn_t])
        for hf in range(2):
            S.op("dve", lambda e, hf=hf: e.bn_stats(out=st6[:, hf, :], in_=xn_t[:, hf * 512:(hf + 1) * 512]), r=[xn_t], w=[st6])
        S.op("dve", lambda e: e.bn_aggr(out=mv[:, :], in_=st6[:, :, :]), r=[st6], w=[mv])
        ev_copy("dve", rstd[:, :], mv[:, 1:2], r=[mv], w=[rstd])
        RSQRT(rstd, rstd[:, :], rstd[:, :], LN_EPS, ALU.add)
        TS("dve", xn_t[:, :], xn_t[:, :], mv[:, 0:1], rstd[:, 0:1], ALU.subtract, ALU.mult, r=[xn_t, mv, rstd], w=[xn_t])
        xo = xo_t.next()
        TTo("dve", xn_t[:, :], xn_t[:, :], lng_t[:, :], ALU.mult, r=[xn_t, lng_t], w=[xn_t])
        TTo("dve", xo[:, :], xn_t[:, :], lnb_t[:, :], ALU.add, r=[xn_t, lnb_t], w=[xo])
        if dst_ap is not None:
            S.dma("sp", lambda e: e.dma_start(out=dst_ap, in_=xo[:, :]), r=[xo], w=[dst_buf])
        return xo

    def new_out_buf():
        b = Buf()
        out_bufs.append(b)
        return b

    mod_one = Tile(nc, "mod_one", [128, 3 * D], F32)
    mod = {"p": mod_one, "s": mod_one}
    onesf = Tile(nc, "onesf", [128, 128], F32); onesb = Tile(nc, "onesb", [128, 128], BF16)
    S.op("pool", lambda e: e.memset(onesf[:, :], 1.0), w=[onesf])
    S.op("dve", lambda e: e.tensor_copy(out=onesb[:, :], in_=onesf[:, :]), r=[onesf], w=[onesb])
    zerosb = Tile(nc, "zerosb", [128, 128], BF16)
    S.op("pool", lambda e: e.memset(zerosb[:, :], 0.0), w=[zerosb])
    onesf.b.const = True; onesb.b.const = True; zerosb.b.const = True
    sbuf_mark = nc.sbuf_base
    vec = {nm: Tile(nc, f"vec_{nm}", [128, 512], F32) for nm in ("k_k", "k_a", "r_k", "gn_g", "gn_b", "w0", "a0")}
    vec_src = {"k_k": k_k, "k_a": k_a, "r_k": r_k, "gn_g": gn_g, "gn_b": gn_b, "w0": w0, "a0": a0}
    mu6 = csil
    muT = Tile(nc, "muT", [128, 6, 8], F32)
    hT = Tile(nc, "hT", [128, 8, 129], F32)
    dT = Tile(nc, "dT", [128, 8, 128], F32)
    xsn = [Tile(nc, f"xs{n}", [128, 8, 128], BF16) for n in range(6)]
    stsh = csil; stT = Tile(nc, "stT", [128, 8, 16], F32)
    wd1 = Tile(nc, "wd1", [128, 8, 64], BF16); wa1 = Tile(nc, "wa1", [128, 8, 64], BF16)
    wd2 = Tile(nc, "wd2", [64, 512], BF16); wa2 = Tile(nc, "wa2", [64, 512], BF16)
    t1T = Tile(nc, "t1T", [64, 128], BF16); t2T = Tile(nc, "t2T", [64, 128], BF16)
    wo_pc = Rot([Tile(nc, f"wopc{i}", [128, 4, D], BF16) for i in range(1)])
    W = {nm: Tile(nc, f"w_{nm}", [128, 512], F32) for nm in
         ("r", "k", "v", "sg", "sig", "asig", "kk", "km", "zb", "t0", "e0", "e1", "y", "y2")}
    ss8 = Tile(nc, "ss8", [128, 8], F32); rn8 = Tile(nc, "rn8", [128, 8], F32); bs8 = Tile(nc, "bs8", [128, 8], F32)
    s1 = Tile(nc, "s1", [128, 8], F32); s2 = Tile(nc, "s2", [128, 8], F32); m8 = Tile(nc, "m8", [128, 8], F32); r8 = Tile(nc, "r8", [128, 8], F32)
    Bq = {nm: Tile(nc, f"b_{nm}", [128, 512], BF16) for nm in ("Rh", "Ah", "Bc", "Kc", "Kt", "Bt", "V", "out")}
    ART = Tile(nc, "ART", [128, 4, 256], BF16); BcT = Tile(nc, "BcT", [128, 4, 128], BF16); KcT = Tile(nc, "KcT", [128, 4, 128], BF16)
    outT = Tile(nc, "outT", [128, 4, 128], BF16)
    gam = Tile(nc, "gam", [128, 16, 17], F32)
    ST = Tile(nc, "ST", [128, 16, 64], F32); STb = Tile(nc, "STb", [128, 16, 64], BF16)
    MMq = [Tile(nc, f"MMq{i}", [128, 512], BF16) for i in range(4)]
    Xq = Rot([Tile(nc, f"Xq{i}", [128, 4, 128], BF16) for i in range(2)])
    XTq = Rot([Tile(nc, f"XTq{i}", [128, 4, 128], BF16) for i in range(2)])
    Pq = Tile(nc, "Pq", [128, 4, 128], F32); Pbq = Tile(nc, "Pbq", [128, 4, 128], BF16)
    Wbq = Tile(nc, "Wbq", [128, 4, 64], BF16); Ubq = Tile(nc, "Ubq", [128, 4, 64], BF16)
    smp = Tile(nc, "smp", [8, 3, 512], BF16)
    s0ld = Tile(nc, "s0ld", [64, 8, 64], F32)
    sout = Tile(nc, "sout", [64, 8, 64], F32)
    src3 = Tile(nc, "src3", [128, 3, 512], BF16)

    def scan_chunk(n, cols, lvls, rows_fn, cbi, st_tile, stb_tile, gam_col, ybank, ART_, BcT_, KcT_):
        ART4 = ART_[:, :, :].rearrange("p s (q t) -> p s q t", q=2)
        for quad in range(2):
            heads = [quad * 4 + i for i in range(4)]
            hord = [(0, heads[0]), (2, heads[2]), (1, heads[1]), (3, heads[3])]
            X0 = Xq.next(); XT0 = XTq.next()
            pbT = scanR.next()
            for qi, hl in hord:
                sb, h2 = hl // 2, hl % 2
                pr = slice(h2 * 64, h2 * 64 + 64)
                MM(pbT[0:n, qi * 128:qi * 128 + n], ART4[pr, sb, 0, cols], BcT_[pr, sb, cols], True, True, r=[ART_, BcT_], w=[pbT])
            ck("sa")
            TTo("dve", XT0[0:n, :, 0:n], pbT[0:n, :].rearrange("p (q t) -> p q t", q=4)[:, :, 0:n],
                mats[0:n, 3:4, 0:n].to_broadcast([n, 4, n]), ALU.mult, r=[pbT, mats], w=[XT0])
            ck("sb")
            for qi, hl in hord:
                sb, h2 = hl // 2, hl % 2
                pr = slice(h2 * 64, h2 * 64 + 64)
                arh = ART4[pr, sb, :, cols]
                pb = scanR.next()
                MM(pb[0:n, 0:2 * n], BcT_[pr, sb, cols], arh, True, True, r=[BcT_, ART_], w=[pb])
                MM(pb[0:n, 2 * n:4 * n], KcT_[pr, sb, cols], arh, True, True, r=[KcT_, ART_], w=[pb])
                if n == 128:
                    TTo("dve", MMq[qi][:, :], pb[:, :], mask4[:, :], ALU.mult, r=[pb, mask4], w=[MMq[qi]])
                else:
                    m4 = mask4[0:n, :].rearrange("p (k t) -> p k t", k=4)[:, :, 0:n]
                    TTo("dve", MMq[qi][0:n, 0:4 * n].rearrange("p (k t) -> p k t", k=4),
                        pb[0:n, 0:4 * n].rearrange("p (k t) -> p k t", k=4), m4, ALU.mult, r=[pb, mask4], w=[MMq[qi]])
            ck("sc1")
            for qi in range(4):
                ev_copy("act", X0[0:n, qi, 0:n], MMq[qi][0:n, 0:n], r=[MMq[qi]], w=[X0])
            TTo("dve", Pq[0:n, :, 0:n], X0[0:n, :, 0:n], mats[0:n, 0:1, 0:n].to_broadcast([n, 4, n]), ALU.add, r=[X0, mats], w=[Pq])
            ev_copy("act", Pbq[0:n, :, 0:n], Pq[0:n, :, 0:n], r=[Pq], w=[Pbq])
            Xc, XTc = X0, XT0
            for lv in range(1, lvls):
                last = lv == lvls - 1
                pbXT = scanR.next()
                for qi in range(4):
                    MM(pbXT[0:n, qi * 128:qi * 128 + n], Xc[0:n, qi, 0:n], XTc[0:n, qi, 0:n], True, True, r=[Xc, XTc], w=[pbXT])
                XTn = XTq.next()
                ev_copy("act", XTn[0:n, :, 0:n], pbXT[0:n, :].rearrange("p (q t) -> p q t", q=4)[:, :, 0:n], r=[pbXT], w=[XTn])
                if not last:
                    pbX = scanR.next()
                    for qi in range(4):
                        MM(pbX[0:n, qi * 128:qi * 128 + n], XTc[0:n, qi, 0:n], Xc[0:n, qi, 0:n], True, True, r=[Xc, XTc], w=[pbX])
                    Xn = Xq.next()
                    ev_copy("dve", Xn[0:n, :, 0:n], pbX[0:n, :].rearrange("p (q t) -> p q t", q=4)[:, :, 0:n], r=[pbX], w=[Xn])
                pbP = scanR.next()
                for qi in range(4):
                    MM(pbP[0:n, qi * 128:qi * 128 + n], XTn[0:n, qi, 0:n], Pbq[0:n, qi, 0:n], True, True, r=[XTn, Pbq], w=[pbP])
                TTo("dve", Pq[0:n, :, 0:n], Pq[0:n, :, 0:n], pbP[0:n, :].rearrange("p (q t) -> p q t", q=4)[:, :, 0:n], ALU.add, r=[Pq, pbP], w=[Pq])
                ev_copy("act", Pbq[0:n, :, 0:n], Pq[0:n, :, 0:n], r=[Pq], w=[Pbq])
                XTc = XTn
                if not last:
                    Xc = Xn
            ck("sc2")
            pbW = scanR.next()
            for qi, hl in hord:
                sb, h2 = hl // 2, hl % 2
                pr = slice(h2 * 64, h2 * 64 + 64)
                hh = cbi * 4 + sb
                MM(pbW[0:n, qi * 64:(qi + 1) * 64], ART4[pr, sb, 0, cols], stb_tile[pr, hh, :], True, False, r=[ART_, stb_tile], w=[pbW])
                MM(pbW[0:n, qi * 64:(qi + 1) * 64], MMq[qi][0:n, 2 * n:3 * n], rows_fn("V", hl), False, True, r=[MMq[qi], src3, smp], w=[pbW])
            ev_copy("act", Wbq[0:n, :, :], pbW[0:n, 0:256].rearrange("p (q i) -> p q i", q=4), r=[pbW], w=[Wbq])
            pbU = scanR.next()
            for qi in range(4):
                MM(pbU[0:n, qi * 64:(qi + 1) * 64], Pbq[0:n, qi, 0:n], Wbq[0:n, qi, :], True, True, r=[Pbq, Wbq], w=[pbU])
            ev_copy("dve", Ubq[0:n, :, :], pbU[0:n, 0:256].rearrange("p (q i) -> p q i", q=4), r=[pbU], w=[Ubq])
            ck("sc3")
            for qi, hl in hord:
                sb, h2 = hl // 2, hl % 2
                pr = slice(h2 * 64, h2 * 64 + 64)
                hh = cbi * 4 + sb
                yo = ybank[0:n, hl * 64:(hl + 1) * 64]
                MM(yo, ART4[pr, sb, 1, cols], stb_tile[pr, hh, :], True, False, r=[ART_, stb_tile], w=[ybank])
                MM(yo, MMq[qi][0:n, n:2 * n], Ubq[0:n, qi, :], False, False, r=[MMq[qi], Ubq], w=[ybank])
                MM(yo, MMq[qi][0:n, 3 * n:4 * n], rows_fn("V", hl), False, True, r=[MMq[qi], src3, smp], w=[ybank])
            ck("sc4")
            pbS = scanR.next()
            for qi, hl in hord:
                sb, h2 = hl // 2, hl % 2
                pr = slice(h2 * 64, h2 * 64 + 64)
                so = pbS[pr, (qi // 2) * 64:(qi // 2) * 64 + 64]
                MM(so, rows_fn("Bt", hl), Ubq[0:n, qi, :], True, False, r=[src3, smp, Ubq], w=[pbS])
                MM(so, rows_fn("Kt", hl), rows_fn("V", hl), False, True, r=[src3, smp], w=[pbS])
            for sbl in range(2):
                sb = quad * 2 + sbl
                hh = cbi * 4 + sb
                STT("dve", st_tile[:, hh, :], st_tile[:, hh, :], gam_col(hh), pbS[:, sbl * 64:(sbl + 1) * 64],
                    ALU.mult, ALU.add, r=[st_tile, gam, pbS, stb_tile], w=[st_tile])
        ev_copy("act", stb_tile[:, cbi * 4:(cbi + 1) * 4, :], st_tile[:, cbi * 4:(cbi + 1) * 4, :], r=[st_tile], w=[stb_tile])

    def RED(out_ap, in_ap, r, w):
        S.op("dve", lambda e: e.tensor_reduce(out=out_ap, in_=in_ap, axis=AX.X, op=ALU.add), r=r, w=w)

    def MEMSET(ap, val, w):
        S.op("pool", lambda e: e.memset(ap, val), w=w)

    def v3(t_):
        return t_[:, :].rearrange("p (h j) -> p h j", j=64)

    def bc3(t_):
        return t_[:, :].unsqueeze(2).to_broadcast([128, 8, 64])

    def state_out(dst_fn, cbs):
        for cbo in cbs:
            pbo = scanR.next()
            for sb in range(4):
                TR(pbo[0:64, sb * 128:(sb + 1) * 128], ST[:, cbo * 4 + sb, :], IDF(), r=[ST, mats], w=[pbo])
            ev_copy("dve", sout[:, :, :], pbo[0:64, :].rearrange("i (h j) -> i h j", j=64), r=[pbo], w=[sout])
            DMA("sp", dst_fn(cbo), sout[:, :, :], r=[sout], w=[new_out_buf()])

    def rwkv_layer(l):
        compute_mod(mod["p"], "p", ada_w[l], ada_b[l:l + 1, :], 3 * D, D)
        load_ln(l)
        DMA("sp", mu6[0:6, :], mu[l], w=[mu6])
        for dc in range(8):
            pb = projR.next()
            TR(pb[:, 0:6], mu6[0:6, dc * 128:(dc + 1) * 128], IDF(6), r=[mu6, mats], w=[pb])
            ev_copy("dve", muT[:, :, dc], pb[:, 0:6], r=[pb], w=[muT])
        S.wait_all("sp", wbufs)
        DMA("sp", wd1[:, :, :], wb_d1[l].rearrange("(c p) n -> p c n", p=128), w=[wd1])
        DMA("sp", wa1[:, :, :], wb_a1[l].rearrange("(c p) n -> p c n", p=128), w=[wa1])
        MEMSET(ST[:, :, :], 0.0, [ST])
        MEMSET(STb[:, :, :], 0.0, [STb])
        MEMSET(hT[:, :, 0:1], 0.0, [hT])
        ck("setup")
        tiles = [("p", i) for i in range(NT)] + [("s", 0)]
        for grp, ti in tiles:
            smp_mode = grp == "s"
            if smp_mode:
                compute_mod(mod["s"], "s", ada_w[l], ada_b[l:l + 1, :], 3 * D, D)
            xt = x_t.next()
            if l == 0:
                DMA("sp", xt[:, :], xs if smp_mode else xp[ti * 128:(ti + 1) * 128, :], w=[xt])
            else:
                DMA("sp", xt[:, :], xb_s if smp_mode else xb_p[ti * 128:(ti + 1) * 128, :], r=[xbuf_s if smp_mode else xbuf_p], w=[xt])
            m_ = mod[grp]
            TTo("dve", h_t[:, :], xt[:, :], m_[:, D:2 * D], ALU.mult, r=[xt, m_], w=[h_t])
            TTo("dve", h_t[:, :], h_t[:, :], m_[:, 0:D], ALU.add, r=[h_t, m_], w=[h_t])
            if grp == "p" and ti == NT - 1:
                DMA("sp", shift_p[l:l + 1, :], h_t[127:128, :], r=[h_t], w=[new_out_buf()])
            if smp_mode:
                for b in range(16):
                    DMA("sp", shift_s[l, b:b + 1, :], h_t[b * 8 + 7:b * 8 + 8, :], r=[h_t], w=[new_out_buf()])
            ck("h")
            for half in range(2):
                pb = projR.next()
                for q in range(4):
                    dc = half * 4 + q
                    TR(pb[:, q * 128:(q + 1) * 128], h_t[:, dc * 128:(dc + 1) * 128], IDF(), r=[h_t, mats], w=[pb])
                ev_copy("act", hT[:, half * 4:(half + 1) * 4, 1:129], pb[:, :].rearrange("p (q t) -> p q t", q=4), r=[pb], w=[hT])
            if not smp_mode:
                TTo("dve", dT[:, :, :], hT[:, :, 0:128], hT[:, :, 1:129], ALU.subtract, r=[hT], w=[dT])
            else:
                DMA("sp", stsh[0:16, :], sshift[l], w=[stsh])
                for dc in range(8):
                    pb = projR.next()
                    TR(pb[:, 0:16], stsh[0:16, dc * 128:(dc + 1) * 128], IDF(16), r=[stsh, mats], w=[pb])
                    ev_copy("dve", stT[:, dc, :], pb[:, 0:16], r=[pb], w=[stT])
                h4 = hT[:, :, 1:129].rearrange("p c (b t) -> p c b t", t=8)
                d4 = dT[:, :, :].rearrange("p c (b t) -> p c b t", t=8)
                TTo("dve", d4[:, :, :, 1:8], h4[:, :, :, 0:7], h4[:, :, :, 1:8], ALU.subtract, r=[hT], w=[dT])
                TTo("dve", d4[:, :, :, 0:1], stT[:, :, :].unsqueeze(3), h4[:, :, :, 0:1], ALU.subtract, r=[hT, stT], w=[dT])
            for n_ in range(6):
                for dc in range(8):
                    STT("dve", xsn[n_][:, dc, :], dT[:, dc, :], muT[:, n_, dc:dc + 1], hT[:, dc, 1:129], ALU.mult, ALU.add,
                        r=[dT, muT, hT], w=[xsn[n_]])
            if not smp_mode:
                ev_copy("act", hT[:, :, 0:1], hT[:, :, 128:129], r=[hT], w=[hT])
            ck("xs")
            pb = projR.next()
            for dc in range(8):
                MM(pb[0:64, 0:128], wd1[:, dc, :], xsn[4][:, dc, :], dc == 0, dc == 7, r=[wd1, xsn[4]], w=[pb])
            ACT(t1T[:, :], pb[0:64, 0:128], AF.Tanh, r=[pb], w=[t1T])
            pb = projR.next()
            for dc in range(8):
                MM(pb[0:64, 0:128], wa1[:, dc, :], xsn[5][:, dc, :], dc == 0, dc == 7, r=[wa1, xsn[5]], w=[pb])
            ev_copy("act", t2T[:, :], pb[0:64, 0:128], r=[pb], w=[t2T])
            Lm = (7, 8, 9) if smp_mode else (4, 5, 6)
            for cb in range(4):
                cs = slice(cb * 512, (cb + 1) * 512)
                for nm, t_ in vec.items():
                    DMA("sp", t_[:, :], vec_src[nm][l:l + 1, cs].partition_broadcast(128), w=[t_])
                DMA("sp", wd2[:, :], wb_d2[l, :, cs], w=[wd2])
                DMA("sp", wa2[:, :], wb_a2[l, :, cs], w=[wa2])

                def proj(n_):
                    wt = wpc.next()
                    DMA("sp", wt[:, :, :], wb_rkvg[l, n_, :, cs].rearrange("(c p) n -> p c n", p=128), w=[wt])
                    pb_ = projR.next()
                    for dc in range(8):
                        MM(pb_[:, :], xsn[n_][:, dc, :], wt[:, dc, :], dc == 0, dc == 7, r=[xsn[n_], wt], w=[pb_])
                    return pb_
                pb = proj(0); ev_copy("act", W["r"][:, :], pb[:, :], r=[pb], w=[W["r"]])
                pb = proj(1); ev_copy("dve", W["k"][:, :], pb[:, :], r=[pb], w=[W["k"]])
                pb = proj(2); ev_copy("act", W["v"][:, :], pb[:, :], r=[pb], w=[W["v"]])
                pb = proj(3); ACT(W["sg"][:, :], pb[:, :], AF.Silu, r=[pb], w=[W["sg"]])
                pb = projR.next()
                MM(pb[:, :], t1T[:, :], wd2[:, :], True, True, r=[t1T, wd2], w=[pb])
                TTo("dve", W["t0"][:, :], pb[:, :], vec["w0"][:, :], ALU.add, r=[pb, vec["w0"]], w=[W["t0"]])
                ACT(W["sig"][:, :], W["t0"][:, :], AF.Sigmoid, r=[W["t0"]], w=[W["sig"]])
                pb = projR.next()
                MM(pb[:, :], t2T[:, :], wa2[:, :], True, True, r=[t2T, wa2], w=[pb])
                TTo("dve", W["t0"][:, :], pb[:, :], vec["a0"][:, :], ALU.add, r=[pb, vec["a0"]], w=[W["t0"]])
                ACT(W["asig"][:, :], W["t0"][:, :], AF.Sigmoid, r=[W["t0"]], w=[W["asig"]])
                ck("proj")
                pb = projR.next()
                ncol = 17 if smp_mode else 1
                for sb in range(4):
                    MM(pb[:, sb * 32:sb * 32 + ncol], W["sig"][:, sb * 128:(sb + 1) * 128], red[:, 0:ncol], True, True, r=[W["sig"], red], w=[pb])
                ACT(gam[:, cb * 4:(cb + 1) * 4, 0:ncol], pb[:, 0:128].rearrange("p (s c) -> p s c", s=4)[:, :, 0:ncol], AF.Exp, r=[pb], w=[gam])
                TTo("dve", W["kk"][:, :], W["k"][:, :], vec["k_k"][:, :], ALU.mult, r=[W["k"], vec["k_k"]], w=[W["kk"]])
                TTo("dve", W["t0"][:, :], W["kk"][:, :], W["kk"][:, :], ALU.mult, r=[W["kk"]], w=[W["t0"]])
                RED(ss8[:, :], v3(W["t0"]), r=[W["t0"]], w=[ss8])
                ev_copy("dve", rn8[:, :], ss8[:, :], r=[ss8], w=[rn8])
                RSQRT(rn8, rn8[:, :], rn8[:, :], 1e-24, ALU.max)
                TTo("dve", v3(W["kk"]), v3(W["kk"]), bc3(rn8), ALU.mult, r=[W["kk"], rn8], w=[W["kk"]])
                STT("dve", W["t0"][:, :], W["asig"][:, :], -1.0, vec["k_a"][:, :], ALU.add, ALU.mult, r=[W["asig"], vec["k_a"]], w=[W["t0"]])
                STT("dve", W["km"][:, :], W["t0"][:, :], 1.0, W["k"][:, :], ALU.add, ALU.mult, r=[W["t0"], W["k"]], w=[W["km"]])
                TTo("dve", W["zb"][:, :], W["kk"][:, :], W["asig"][:, :], ALU.mult, r=[W["kk"], W["asig"]], w=[W["zb"]])
                TTo("dve", W["t0"][:, :], W["r"][:, :], W["km"][:, :], ALU.mult, r=[W["r"], W["km"]], w=[W["t0"]])
                TTo("dve", W["t0"][:, :], W["t0"][:, :], vec["r_k"][:, :], ALU.mult, r=[W["t0"], vec["r_k"]], w=[W["t0"]])
                RED(bs8[:, :], v3(W["t0"]), r=[W["t0"]], w=[bs8])
                def cexp(dst_nm, mi, sc):
                    pb_ = projR.next()
                    MM(pb_[:, :], mats[:, mi, :], W["sig"][:, :], True, True, r=[mats, W["sig"]], w=[pb_])
                    ACT(W[dst_nm][:, :], pb_[:, :], AF.Exp, r=[pb_], w=[W[dst_nm]], scale=sc)
                cexp("e0", Lm[0], 1.0)
                TTo("dve", Bq["Rh"][:, :], W["r"][:, :], W["e0"][:, :], ALU.mult, r=[W["r"], W["e0"]], w=[Bq["Rh"]])
                cexp("e1", Lm[1], 1.0)
                STT("dve", Bq["Ah"][:, :], W["kk"][:, :], -1.0, W["e1"][:, :], ALU.mult, ALU.mult, r=[W["kk"], W["e1"]], w=[Bq["Ah"]])
                cexp("e0", Lm[0], -1.0)
                TTo("dve", Bq["Bc"][:, :], W["zb"][:, :], W["e0"][:, :], ALU.mult, r=[W["zb"], W["e0"]], w=[Bq["Bc"]])
                TTo("dve", Bq["Kc"][:, :], W["km"][:, :], W["e0"][:, :], ALU.mult, r=[W["km"], W["e0"]], w=[Bq["Kc"]])
                cexp("e1", Lm[2], 1.0)
                TTo("dve", src3[:, 0, :], W["km"][:, :], W["e1"][:, :], ALU.mult, r=[W["km"], W["e1"]], w=[src3])
                TTo("dve", src3[:, 1, :], W["zb"][:, :], W["e1"][:, :], ALU.mult, r=[W["zb"], W["e1"]], w=[src3])
                ev_copy("act", src3[:, 2, :], W["v"][:, :], r=[W["v"]], w=[src3])
                ck("elem")
                pbb = pb_tr.t[:, :].bitcast(BF16)
                for qn, nm in enumerate(("Ah", "Rh", "Bc", "Kc")):
                    for sb in range(4):
                        TR(pbb[:, sb * 128:(sb + 1) * 128], Bq[nm][:, sb * 128:(sb + 1) * 128], identb[:, :], r=[Bq[nm], identb], w=[pb_tr])
                    dst_t = ART if qn < 2 else (BcT if qn == 2 else KcT)
                    lo = 128 if qn == 1 else 0
                    ev_copy(evR.next(), dst_t[:, :, lo:lo + 128], pbb[:, 0:512].rearrange("p (s t) -> p s t", s=4), r=[pb_tr], w=[dst_t])
                ck("tr")
                rows = {"Kt": 0, "Bt": 1, "V": 2}
                if not smp_mode:
                    scan_chunk(128, slice(0, 128), 7, lambda nm, hl: src3[:, rows[nm], hl * 64:(hl + 1) * 64], cb, ST, STb,
                               lambda hh: gam[:, hh, 0:1], pb_y, ART, BcT, KcT)
                    ev_copy("act", W["y"][:, :], pb_y[:, :], r=[pb_y], w=[W["y"]])
                else:
                    for b in range(16):
                        DMA("sp", smp[:, :, :], src3[b * 8:(b + 1) * 8, :, :], r=[src3], w=[smp])
                        DMA("sp", s0ld[:, :, :], swkv[l, b, cb * 8:(cb + 1) * 8].rearrange("h i j -> i h j"), w=[s0ld])
                        pbs = scanR.next()
                        for sb in range(4):
                            TR(pbs[:, sb * 64:(sb + 1) * 64], s0ld[:, 2 * sb:2 * sb + 2, :].rearrange("i h j -> i (h j)"),
                               IDF(64), r=[s0ld, mats], w=[pbs])
                        ev_copy("dve", ST[:, cb * 4:(cb + 1) * 4, :], pbs[:, 0:256].rearrange("p (s i) -> p s i", s=4), r=[pbs], w=[ST])
                        ev_copy("act", STb[:, cb * 4:(cb + 1) * 4, :], ST[:, cb * 4:(cb + 1) * 4, :], r=[ST], w=[STb])
                        scan_chunk(8, slice(b * 8, (b + 1) * 8), 3, lambda nm, hl: smp[0:8, rows[nm], hl * 64:(hl + 1) * 64], cb, ST, STb,
                                   lambda hh, b=b: gam[:, hh, 1 + b:2 + b], pb_y, ART, BcT, KcT)
                        ev_copy("act", W["y2"][0:8, :], pb_y[0:8, :], r=[pb_y], w=[W["y2"]])
                        DMA("sp", W["y"][b * 8:(b + 1) * 8, :], W["y2"][0:8, :], r=[W["y2"]], w=[W["y"]])
                        state_out(lambda c_, b=b: wkv_s[l, b, c_ * 8:(c_ + 1) * 8].rearrange("h i j -> i h j"), [cb])
                ck("scan")
                TTo("dve", W["y2"][:, :], W["y"][:, :], W["y"][:, :], ALU.mult, r=[W["y"]], w=[W["y2"]])
                RED(s1[:, :], v3(W["y"]), r=[W["y"]], w=[s1])
                RED(s2[:, :], v3(W["y2"]), r=[W["y2"]], w=[s2])
                TS("dve", m8[:, :], s1[:, :], 1.0 / 64, None, ALU.mult, None, r=[s1], w=[m8])
                TTo("dve", r8[:, :], m8[:, :], m8[:, :], ALU.mult, r=[m8], w=[r8])
                STT("dve", r8[:, :], s2[:, :], 1.0 / 64, r8[:, :], ALU.mult, ALU.subtract, r=[s2, r8], w=[r8])
                RSQRT(r8, r8[:, :], r8[:, :], GN_EPS, ALU.add)
                TTo("dve", v3(W["y"]), v3(W["y"]), bc3(m8), ALU.subtract, r=[W["y"], m8], w=[W["y"]])
                TTo("dve", v3(W["y"]), v3(W["y"]), bc3(r8), ALU.mult, r=[W["y"], r8], w=[W["y"]])
                TTo("dve", W["y"][:, :], W["y"][:, :], vec["gn_g"][:, :], ALU.mult, r=[W["y"], vec["gn_g"]], w=[W["y"]])
                TTo("dve", W["y"][:, :], W["y"][:, :], vec["gn_b"][:, :], ALU.add, r=[W["y"], vec["gn_b"]], w=[W["y"]])
                TTo("dve", v3(W["t0"]), v3(W["v"]), bc3(bs8), ALU.mult, r=[W["v"], bs8], w=[W["t0"]])
                TTo("dve", W["y"][:, :], W["y"][:, :], W["t0"][:, :], ALU.add, r=[W["y"], W["t0"]], w=[W["y"]])
                TTo("dve", Bq["out"][:, :], W["y"][:, :], W["sg"][:, :], ALU.mult, r=[W["y"], W["sg"]], w=[Bq["out"]])
                for sb in range(4):
                    TR(pbb[:, sb * 128:(sb + 1) * 128], Bq["out"][:, sb * 128:(sb + 1) * 128], identb[:, :], r=[Bq["out"], identb], w=[pb_tr])
                ev_copy(evR.next(), outT[:, :, :], pbb[:, 0:512].rearrange("p (s t) -> p s t", s=4), r=[pb_tr], w=[outT])
                wo = wo_pc.next()
                DMA("sp", wo[:, :, :], wb_oa[l, cb * 512:(cb + 1) * 512, :].rearrange("(c p) n -> p c n", p=128), w=[wo])
                for sb in range(4):
                    for hf, pbx in enumerate((pb_x0, pb_x1)):
                        MM(pbx[:, :], outT[:, sb, :], wo[:, sb, hf * 512:(hf + 1) * 512], cb == 0 and sb == 0, cb == 3 and sb == 3,
                           r=[outT, wo], w=[pbx])
            ck("cbs")
            if not smp_mode:
                residual_ln(xt, m_[:, 2 * D:3 * D], m_, xb_p[ti * 128:(ti + 1) * 128, :], xbuf_p)
            else:
                residual_ln(xt, m_[:, 2 * D:3 * D], m_, xb_s, xbuf_s)
            ck("tile")
            if grp == "p" and ti == NT - 1:
                state_out(lambda c_: wkv_p[l, c_ * 8:(c_ + 1) * 8].rearrange("h i j -> i h j"), [0, 1, 2, 3])

    def phase_b():
        S.barrier()
        nc.sbuf_base = sbuf_mark
        QT = Tile(nc, "QT", [128, 16, 128], BF16)
        ETr = Rot([Tile(nc, f"ET{i}", [128, 512], BF16) for i in range(2)])
        OTf = Tile(nc, "OTf", [128, 16, 128], F32)
        sqt = Tile(nc, "sqt", [128, 16, 128], F32)
        OTb = Tile(nc, "OTb", [128, 16, 128], BF16)
        SG = Tile(nc, "SG", [128, 2048], F32)
        Wq = Rot([Tile(nc, f"Wq{i}", [128, 512], F32) for i in range(2)])
        hTb = Tile(nc, "hTb", [128, 8, 128], BF16)
        rLt = Tile(nc, "rLt", [128, 512], F32); Ont = Tile(nc, "Ont", [128, 512], F32)
        wo_b = Tile(nc, "wo_b", [128, 4, D], BF16)
        rtab = Tile(nc, "rtab", [128, 2, 8], F32)
        rp = [Tile(nc, f"rp{i}", [128, 8, 8], F32) for i in range(4)]
        lq = Tile(nc, "lq", [128, 256], F32); lq2 = Tile(nc, "lq2", [128, 2, 64], F32)
        le = Tile(nc, "le", [128, 2], F32); neglam = Tile(nc, "neglam", [128, 1], F32)
        subg = Tile(nc, "subg", [128, 1], F32)
        accO = Rot([PB[7], PB[4]]); accL = Rot([PB[2], PB[3]]); xR = Rot([PB[0], PB[1]])
        mark2 = nc.sbuf_base

        def load_x(xt, src_ap, buf):
            DMA("sp", xt[:, :], src_ap, r=[buf], w=[xt])

        def modulate(xt, m_):
            TTo("dve", h_t[:, :], xt[:, :], m_[:, D:2 * D], ALU.mult, r=[xt, m_], w=[h_t])
            TTo("dve", h_t[:, :], h_t[:, :], m_[:, 0:D], ALU.add, r=[h_t, m_], w=[h_t])

        def make_hT():
            for half in range(2):
                pb = projR.next()
                for q in range(4):
                    dc = half * 4 + q
                    TR(pb[:, q * 128:(q + 1) * 128], h_t[:, dc * 128:(dc + 1) * 128], IDF(), r=[h_t, mats], w=[pb])
                ev_copy("act", hTb[:, half * 4:(half + 1) * 4, :], pb[:, :].rearrange("p (q t) -> p q t", q=4), r=[pb], w=[hTb])

        def projB(wdram):
            wt = wpc.next()
            DMA("sp", wt[:, :, :], wdram.rearrange("(c p) n -> p c n", p=128), w=[wt])
            pb_ = projR.next()
            for dc in range(8):
                MM(pb_[:, :], hTb[:, dc, :], wt[:, dc, :], dc == 0, dc == 7, r=[hTb, wt], w=[pb_])
            return pb_

        def rope(Wt):
            v = Wt[:, :].rearrange("p (g d) -> p g d", d=64)
            x1, x2 = v[:, :, 0:8], v[:, :, 8:16]
            cos = rtab[:, 0:1, :].to_broadcast([128, 8, 8]); sin = rtab[:, 1:2, :].to_broadcast([128, 8, 8])
            TTo("dve", rp[0][:, :, :], x1, cos, ALU.mult, r=[Wt, rtab], w=[rp[0]])
            TTo("dve", rp[1][:, :, :], x2, sin, ALU.mult, r=[Wt, rtab], w=[rp[1]])
            TTo("dve", rp[2][:, :, :], x1, sin, ALU.mult, r=[Wt, rtab], w=[rp[2]])
            TTo("dve", rp[3][:, :, :], x2, cos, ALU.mult, r=[Wt, rtab], w=[rp[3]])
            TTo("dve", x1, rp[0][:, :, :], rp[1][:, :, :], ALU.subtract, r=[rp[0], rp[1]], w=[Wt])
            TTo("dve", x2, rp[3][:, :, :], rp[2][:, :, :], ALU.add, r=[rp[3], rp[2]], w=[Wt])

        def kv_tile(xt, rows_k, rows_v, KT_dst_fn, V_dst_fn):
            make_hT()
            for c in range(4):
                pb = projB(wb_kv[:, c * 512:(c + 1) * 512])
                Wt = Wq.next()
                ev_copy("act", Wt[:, :], pb[:, :], r=[pb], w=[Wt])
                if c < 2:
                    rope(Wt)
                    DMA("sp", rows_k[:, c * 512:(c + 1) * 512], Wt[:, :], r=[Wt], w=[new_out_buf()])
                    pbt = projR.next()
                    for q in range(4):
                        TR(pbt[:, q * 128:(q + 1) * 128], Wt[:, q * 128:(q + 1) * 128], IDF(), r=[Wt, mats], w=[pbt])
                    kd, kb = KT_dst_fn(c)
                    ev_copy("dve", kd, pbt[:, :].rearrange("p (q t) -> p q t", q=4), r=[pbt], w=[kb])
                else:
                    DMA("sp", rows_v[:, (c - 2) * 512:(c - 1) * 512], Wt[:, :], r=[Wt], w=[new_out_buf()])
                    vd, vb = V_dst_fn(c - 2)
                    ev_copy("dve", vd, Wt[:, :], r=[Wt], w=[vb])

        def qg_tile(j):
            make_hT()
            for c in range(4):
                pb = projB(wb_qg[j, :, c * 512:(c + 1) * 512])
                Wt = Wq.next()
                ev_copy("act", Wt[:, :], pb[:, :], r=[pb], w=[Wt])
                rope(Wt)
                pbt = projR.next()
                for q in range(4):
                    TR(pbt[:, q * 128:(q + 1) * 128], Wt[:, q * 128:(q + 1) * 128], IDF(), r=[Wt, mats], w=[pbt])
                ev_copy("dve", QT[:, c * 4:(c + 1) * 4, :], pbt[:, :].rearrange("p (q t) -> p q t", q=4), r=[pbt], w=[QT])
            for c in range(4):
                pb = projB(wb_qg[j, :, 2048 + c * 512:2048 + (c + 1) * 512])
                ACT(SG[:, c * 512:(c + 1) * 512], pb[:, :], AF.Silu, r=[pb], w=[SG])

        def layer_consts(l):
            j = l - NA
            lam_init = 0.8 - 0.6 * math.exp(-0.3 * l)
            load_ln(l)
            DMA("sp", lq[:, :], lam_qk[j:j + 1, :].partition_broadcast(128), w=[lq])
            TTo("dve", lq2[:, 0, :], lq[:, 0:64], lq[:, 64:128], ALU.mult, r=[lq], w=[lq2])
            TTo("dve", lq2[:, 1, :], lq[:, 128:192], lq[:, 192:256], ALU.mult, r=[lq], w=[lq2])
            RED(le[:, :], lq2[:, :, :], r=[lq2], w=[le])
            ACT(le[:, :], le[:, :], AF.Exp, r=[le], w=[le])
            TTo("dve", neglam[:, :], le[:, 1:2], le[:, 0:1], ALU.subtract, r=[le], w=[neglam])
            TS("dve", neglam[:, :], neglam[:, :], -lam_init, None, ALU.add, None, r=[neglam], w=[neglam])
            DMA("sp", subg[:, :], subln_g[j:j + 1, :].rearrange("o v -> v o"), w=[subg])
            return lam_init

        def attn_post(xt, m_, lam_init, j, dst_ap, dst_buf):
            OT2 = OTf[:, :, :].rearrange("p a t -> p (a t)")
            sq2 = sqt[:, :, :].rearrange("p a t -> p (a t)")
            S.op("act", lambda e: e.activation(out=sq2, in_=OT2, func=AF.Square), r=[OTf], w=[sqt])
            for c in range(4):
                cs = slice(c * 512, (c + 1) * 512)
                pb = projR.next()
                MM(pb[:, :], onesf[:, :], sq2[:, cs], True, True, r=[onesf, sqt], w=[pb])
                TS("dve", sq2[:, cs], pb[:, :], 1.0 / 128, LN_EPS, ALU.mult, ALU.add, r=[pb], w=[sqt])
                S.op("act", lambda e, cs=cs: e.activation(out=sq2[:, cs], in_=sq2[:, cs], func=AF.Sqrt), r=[sqt], w=[sqt])
                S.op("dve", lambda e, cs=cs: e.reciprocal(out=sq2[:, cs], in_=sq2[:, cs]), r=[sqt], w=[sqt])
                TTo("dve", OT2[:, cs], OT2[:, cs], sq2[:, cs], ALU.mult, r=[OTf, sqt], w=[OTf])
                pbt = projR.next()
                for q in range(4):
                    kg = c * 4 + q
                    TR(pbt[:, q * 128:(q + 1) * 128], SG[:, kg * 128:(kg + 1) * 128], IDF(), r=[SG, mats], w=[pbt])
                TTo("dve", OT2[:, cs], OT2[:, cs], pbt[:, :], ALU.mult, r=[OTf, pbt], w=[OTf])
            TS("dve", OTb[:, :, :], OTf[:, :, :], subg[:, 0:1], 1.0 - lam_init, ALU.mult, ALU.mult, r=[OTf, subg], w=[OTb])
            for piece in range(4):
                DMA("sp", wo_b[:, :, :], wb_ob[j, piece * 512:(piece + 1) * 512, :].rearrange("(c p) n -> p c n", p=128), w=[wo_b])
                for sb in range(4):
                    for hf, pbx in enumerate((pb_x0, pb_x1)):
                        MM(pbx[:, :], OTb[:, piece * 4 + sb, :], wo_b[:, sb, hf * 512:(hf + 1) * 512],
                           piece == 0 and sb == 0, piece == 3 and sb == 3, r=[OTb, wo_b], w=[pbx])
            return residual_ln(xt, m_[:, 2 * D:3 * D], m_, dst_ap, dst_buf)

        def finish_head(aO, aL, ncol, out_ap, in_m0, in_m1):
            S.op("dve", lambda e: e.reciprocal(out=rLt[:, 0:ncol], in_=aL[:, 0:ncol]), r=[aL], w=[rLt])
            TTo("dve", Ont[:, 0:ncol], aO[:, 0:ncol], rLt[:, 0:ncol], ALU.mult, r=[aO, rLt], w=[Ont])
            STT("dve", out_ap, in_m1, neglam[:, 0:1], in_m0, ALU.mult, ALU.add, r=[Ont, neglam], w=[OTf])

        KT = Tile(nc, "KT", [128, 8, T], BF16)
        Vb = Tile(nc, "Vb", [128, NT, 1024], BF16)
        m_ = mod["p"]
        compute_mod(m_, "p", ada_kv_w, ada_kv_b, 2 * D, D)
        for ti in range(0 if _os.environ.get("ONLYS") else NT):
            rs = slice(ti * 128, (ti + 1) * 128)
            xt = x_t.next()
            load_x(xt, xb_p[rs, :], xbuf_p)
            modulate(xt, m_)
            DMA("sp", rtab[:, :, :], c_rope_p[rs, :].rearrange("p (a c) -> p a c", a=2), w=[rtab])
            kv_tile(xt, k_p[rs, :], v_p[rs, :],
                    lambda c, ti=ti: (KT[:, c * 4:(c + 1) * 4, ti * 128:(ti + 1) * 128], KT),
                    lambda c, ti=ti: (Vb[:, ti, c * 512:(c + 1) * 512], Vb))
        ck("kvp")
        for l in (NA, NA + 1):
            if stage < l + 1 + 1 or _os.environ.get("ONLYS"):
                break
            j = l - NA
            compute_mod(m_, "p", ada_w[l], ada_b[l:l + 1, :], 3 * D, D)
            lam_init = layer_consts(l)
            for qt in range(NT):
                rs = slice(qt * 128, (qt + 1) * 128)
                xt = x_t.next()
                load_x(xt, xb_p[rs, :], xbuf_p)
                modulate(xt, m_)
                DMA("sp", rtab[:, :, :], c_rope_p[rs, :].rearrange("p (a c) -> p a c", a=2), w=[rtab])
                qg_tile(j)
                for k in range(8):
                    aO = accO.next(); aL = accL.next()
                    for kt in range(qt + 1):
                        ps = scanR.next()
                        diag = kt == qt
                        if diag:
                            MM(ps[:, :], identb[:, :], maskb[:, :], True, False, r=[identb, maskb], w=[ps])
                        for m in range(2):
                            pr = slice(m * 64, (m + 1) * 64)
                            MM(ps[:, m * 256:(m + 1) * 256], KT[pr, k, kt * 128:(kt + 1) * 128], QT[pr, 2 * k:2 * k + 2, :],
                               not diag, (not diag) or m == 1, r=[KT, QT], w=[ps])
                        et = ETr.next()
                        ACT(et[:, :], ps[:, :], AF.Exp, r=[ps], w=[et], scale=0.125)
                        MM(aO[:, :], Vb[:, kt, k * 128:(k + 1) * 128], et[:, :], kt == 0, kt == qt, r=[Vb, et], w=[aO])
                        MM(aL[:, :], onesb[:, :], et[:, :], kt == 0, kt == qt, r=[onesb, et], w=[aL])
                    finish_head(aO, aL, 512, OTf[:, 2 * k:2 * k + 2, :],
                                Ont[:, 0:256].rearrange("p (g t) -> p g t", g=2),
                                Ont[:, 256:512].rearrange("p (g t) -> p g t", g=2))
                if l == DEPTH - 1:
                    attn_post(xt, m_, lam_init, j, y_p[rs, :], new_out_buf())
                else:
                    attn_post(xt, m_, lam_init, j, xb_p[rs, :], xbuf_p)
            ck(f"attp{l}")

        S.barrier()
        nc.sbuf_base = mark2
        KTs = Tile(nc, "KTs", [128, 8, 128], BF16)
        Vs = Tile(nc, "Vs", [128, 1024], BF16)
        vnew = Tile(nc, "vnew", [8, 1024], BF16)
        Qbd = Tile(nc, "Qbd", [128, 16, 8, 32], BF16)
        KpgR = Rot([Tile(nc, f"Kpg{i}", [128, 1024], F32) for i in range(2)])
        VpgR = Rot([Tile(nc, f"Vpg{i}", [128, 1024], F32) for i in range(2)])
        KTbR = Rot([Tile(nc, f"KTb{i}", [128, 8, 128], BF16) for i in range(2)])
        VpbR = Rot([Tile(nc, f"Vpb{i}", [128, 1024], BF16) for i in range(2)])
        idxR = Rot([Tile(nc, f"idx{i}", [128, 1], I32) for i in range(4)])
        NPT = 16 * NPG
        pti = Tile(nc, "pti", [128, NPT], I32); ptf = Tile(nc, "ptf", [128, NPT], F32)
        iot = Tile(nc, "iot", [128, 1], F32); idxall = Tile(nc, "idxall", [128, NPT], I32)
        DMA("sp", pti[:, :], ptab.partition_broadcast(128), w=[pti])
        S.op("pool", lambda e: e.iota(iot[:, :], pattern=[[0, 1]], base=0, channel_multiplier=1,
                                      allow_small_or_imprecise_dtypes=True), w=[iot])
        ev_copy("dve", ptf[:, :], pti[:, :], r=[pti], w=[ptf])
        STT("dve", ptf[:, :], ptf[:, :], 128.0, iot[:, 0:1].to_broadcast([128, NPT]), ALU.mult, ALU.add, r=[ptf, iot], w=[ptf])
        ev_copy("dve", idxall[:, :], ptf[:, :], r=[ptf], w=[idxall])
        MEMSET(Qbd[:, :, :, :], 0.0, [Qbd])
        if _os.environ.get("DBGIDX"):
            dbg_idx = nc.dram_tensor("dbg_idx", [128, NPT], I32, kind="ExternalOutput").ap()
            DMA("sp", dbg_idx, idxall[:, :], r=[idxall], w=[new_out_buf()])

        m_ = mod["s"]
        compute_mod(m_, "s", ada_kv_w, ada_kv_b, 2 * D, D)
        xt = x_t.next()
        load_x(xt, xb_s, xbuf_s)
        modulate(xt, m_)
        DMA("sp", rtab[:, :, :], c_rope_s.rearrange("p (a c) -> p a c", a=2), w=[rtab])
        kv_tile(xt, k_s, v_s,
                lambda c: (KTs[:, c * 4:(c + 1) * 4, :], KTs),
                lambda c: (Vs[:, c * 512:(c + 1) * 512], Vs))
        ck("kvs")
        for l in (NA, NA + 1):
            if stage < l + 1 + 1:
                break
            j = l - NA
            compute_mod(m_, "s", ada_w[l], ada_b[l:l + 1, :], 3 * D, D)
            lam_init = layer_consts(l)
            xt = x_t.next()
            load_x(xt, xb_s, xbuf_s)
            modulate(xt, m_)
            qg_tile(j)
            for b in range(16):
                for m in range(2):
                    pr = slice(m * 64, (m + 1) * 64)
                    ev_copy("dve", Qbd[pr, b, :, m * 16:(m + 1) * 16].rearrange("p k (g t) -> p k g t", g=2),
                            QT[pr, :, b * 8:(b + 1) * 8].rearrange("p (k g) t -> p k g t", g=2), r=[QT], w=[Qbd])
            ck("g0")
            for b in range(16):
                aO = accO.next(); aL = accL.next()
                for jp in range(NPG + 1):
                    new = jp == NPG
                    ps = scanR.next()
                    et = ETr.next()
                    if not new:
                        it = idxR.next()
                        ev_copy("dve", it[:, :], idxall[:, b * NPG + jp:b * NPG + jp + 1], r=[idxall], w=[it])
                        kp = KpgR.next(); vp = VpgR.next()
                        S.dma("pool", lambda e, kp=kp, it=it: e.indirect_dma_start(
                            out=kp[:, :], out_offset=None, in_=cache_k[:, :],
                            in_offset=bass.IndirectOffsetOnAxis(ap=it[:, :], axis=0)), r=[it], w=[kp])
                        S.dma("pool", lambda e, vp=vp, it=it: e.indirect_dma_start(
                            out=vp[:, :], out_offset=None, in_=cache_v[:, :],
                            in_offset=bass.IndirectOffsetOnAxis(ap=it[:, :], axis=0)), r=[it], w=[vp])
                        ck("g1")
                        ktb = KTbR.next(); vpb = VpbR.next()
                        for half in range(2):
                            pbt = xR.next()
                            for q in range(4):
                                hh = half * 4 + q
                                TR(pbt[:, q * 128:(q + 1) * 128], kp[:, hh * 128:(hh + 1) * 128], IDF(), r=[kp, mats], w=[pbt])
                            ev_copy(evR.next(), ktb[:, half * 4:(half + 1) * 4, :], pbt[:, :].rearrange("p (q t) -> p q t", q=4), r=[pbt], w=[ktb])
                        ev_copy(evR.next(), vpb[:, :], vp[:, :], r=[vp], w=[vpb])
                        n = 128
                        for k in range(8):
                            MM(ps[:, k * 32:(k + 1) * 32], ktb[:, k, :], Qbd[:, b, k, :], True, True, r=[ktb, Qbd], w=[ps])
                        ACT(et[:, 0:256], ps[:, 0:256], AF.Exp, r=[ps], w=[et], scale=0.125)
                        vsrc = vpb
                    else:
                        n = 8
                        DMA("sp", vnew[:, :], Vs[b * 8:(b + 1) * 8, :], r=[Vs], w=[vnew])
                        for k in range(8):
                            MM(ps[0:8, k * 32:(k + 1) * 32], KTs[:, k, b * 8:(b + 1) * 8], Qbd[:, b, k, :], True, True, r=[KTs, Qbd], w=[ps])
                        ACT(et[0:8, 0:256], ps[0:8, 0:256], AF.Exp, r=[ps], w=[et], scale=0.125)
                        TTo("dve", et[0:8, 0:256].rearrange("p (a t) -> p a t", t=8), et[0:8, 0:256].rearrange("p (a t) -> p a t", t=8),
                            mats[0:8, 2:3, 0:8].to_broadcast([8, 32, 8]), ALU.mult, r=[et, mats], w=[et])
                        vsrc = vnew
                    ck("g2")
                    MM(aL[:, 0:256], onesb[0:n, :], et[0:n, 0:256], jp == 0, new, r=[onesb, et], w=[aL])
                    if jp == 0:
                        MM(aO[:, 0:256], zerosb[0:n, :], et[0:n, 0:256], True, False, r=[zerosb, et], w=[aO])
                    for k in range(8):
                        MM(aO[:, k * 32:(k + 1) * 32], vsrc[0:n, k * 128:(k + 1) * 128], et[0:n, k * 32:(k + 1) * 32], False, new and k == 7,
                           r=[vsrc, et], w=[aO])
                ck("g3")
                On5 = Ont[:, 0:256].rearrange("p (k m g t) -> p k m g t", k=8, m=2, g=2)
                finish_head(aO, aL, 256, OTf[:, :, b * 8:(b + 1) * 8].rearrange("p (k g) t -> p k g t", g=2),
                            On5[:, :, 0, :, :], On5[:, :, 1, :, :])
            if l == DEPTH - 1:
                attn_post(xt, m_, lam_init, j, y_s, new_out_buf())
            else:
                attn_post(xt, m_, lam_init, j, xb_s, xbuf_s)
            ck(f"atts{l}")

    try:
        for l in range(NA):
            if stage >= 1 + l and not _os.environ.get("ONLYS"):
                rwkv_layer(l)
        if stage >= 3:
            phase_b()
    except StopBuild:
        pass

    S.drain("sp")
    S.emit()
    return nc


_STAGE = 99


def kernel(x_prompt, x_sample, cache_k, cache_v, page_table, state_wkv, state_shift, c_prompt, c_sample,
           ada_w, ada_b, ln_g, ln_b, mu, w_rkvg, w0, w_decay1, w_decay2, a0, w_a1, w_a2, k_k, k_a, r_k,
           gn_g, gn_b, w_o_a, ada_kv_w, ada_kv_b, w_kv, w_qg, lam_qk, subln_g, w_o_b):
    f = lambda a: np.ascontiguousarray(np.asarray(a, dtype=np.float32))
    B, T, _ = x_prompt.shape
    DB, TS, _ = x_sample.shape
    NPHYS = cache_k.shape[0]
    NPG = page_table.shape[1]
    assert B == NCORE and DB == 16 * NCORE and TS == 8, (B, NCORE, DB)
    nc = build_program(T, NPG, NPHYS, stage=_STAGE)
    consts = make_consts(T, NPG * 128)
    shared = {
        "cache_k": f(cache_k).reshape(NPHYS * 128, 1024), "cache_v": f(cache_v).reshape(NPHYS * 128, 1024),
        "ada_w": f(ada_w), "ada_b": f(ada_b), "ln_g": f(ln_g), "ln_b": f(ln_b), "mu": f(mu),
        "w_rkvg": f(w_rkvg), "w0": f(w0), "w_decay1": f(w_decay1), "w_decay2": f(w_decay2), "a0": f(a0),
        "w_a1": f(w_a1), "w_a2": f(w_a2), "k_k": f(k_k), "k_a": f(k_a), "r_k": f(r_k).reshape(NA, E),
        "gn_g": f(gn_g), "gn_b": f(gn_b), "w_o_a": f(w_o_a), "ada_kv_w": f(ada_kv_w),
        "ada_kv_b": f(ada_kv_b).reshape(1, 2 * D), "w_kv": f(w_kv), "w_qg": f(w_qg),
        "lam_qk": f(lam_qk).reshape(2, 256), "subln_g": f(subln_g), "w_o_b": f(w_o_b),
        "c_mats": consts["mats"], "c_mask4": consts["mask4"], "c_red": consts["red"],
        "c_maskb": consts["maskb"], "c_maskb8": consts["maskb8"],
        "c_rope_p": consts["rope_p"], "c_rope_s": consts["rope_s"],
    }
    xpf, xsf = f(x_prompt), f(x_sample)
    swf, ssf = f(state_wkv), f(state_shift)
    cpf, csf = f(c_prompt), f(c_sample)
    pt = np.ascontiguousarray(np.asarray(page_table, dtype=np.int32))
    in_maps = []
    for c in range(NCORE):
        bs = slice(16 * c, 16 * (c + 1))
        m = dict(shared)
        m["xp"] = xpf[c]
        m["xs"] = xsf[bs].reshape(128, D)
        m["ptab"] = pt[bs].reshape(1, 16 * NPG)
        m["swkv"] = np.ascontiguousarray(swf[:, bs])
        m["sshift"] = np.ascontiguousarray(ssf[:, bs])
        m["cvec"] = np.concatenate([cpf[c:c + 1], csf[bs]], axis=0)
        in_maps.append(m)
    res = run_bass_kernel_spmd(nc, in_maps, core_ids=list(range(NCORE))).results
    g = lambda name: [np.asarray(r[name]) for r in res]
    if _os.environ.get("DBGIDX"):
        global _LAST_RES
        _LAST_RES = res
    y_prompt = np.stack(g("y_p"), 0).reshape(B, T, D)
    y_sample = np.concatenate(g("y_s"), 0).reshape(DB, TS, D)
    k_prompt = np.stack(g("k_p"), 0).reshape(B, T, 8, 128)
    v_prompt = np.stack(g("v_p"), 0).reshape(B, T, 8, 128)
    k_sample = np.concatenate(g("k_s"), 0).reshape(DB, TS, 8, 128)
    v_sample = np.concatenate(g("v_s"), 0).reshape(DB, TS, 8, 128)
    wkv_prompt = np.stack(g("wkv_p"), 1).reshape(NA, B, 32, 64, 64)
    shift_prompt = np.stack(g("shift_p"), 1).reshape(NA, B, D)
    wkv_sample = np.concatenate(g("wkv_s"), 1).reshape(NA, DB, 32, 64, 64)
    shift_sample = np.concatenate(g("shift_s"), 1).reshape(NA, DB, D)
    return (y_prompt.astype(np.float32), y_sample.astype(np.float32), k_prompt.astype(np.float32),
            v_prompt.astype(np.float32), k_sample.astype(np.float32), v_sample.astype(np.float32),
            wkv_prompt.astype(np.float32), shift_prompt.astype(np.float32), wkv_sample.astype(np.float32),
            shift_sample.astype(np.float32))
```

```python
import math
import numpy as np
import ml_dtypes
import concourse.bass as bass
import concourse.mybir as mybir
from concourse.bass_utils import run_bass_kernel_spmd

F32 = mybir.dt.float32
BF16 = mybir.dt.bfloat16
I32 = mybir.dt.int32
AF = mybir.ActivationFunctionType
ALU = mybir.AluOpType
AX = mybir.AxisListType

D = 1024
E = 2048
NCORE = 8
DEPTH = 4
NA = 2
ALPHA = (2 * DEPTH) ** 0.25
LN_EPS = 1e-5
GN_EPS = 64e-5
LWC = -math.exp(-0.5)
SEM_LIMIT = 30000


class Buf:
    __slots__ = ("w", "r", "const")

    def __init__(self, const=False):
        self.w = None
        self.r = []
        self.const = const


class Tile:
    def __init__(self, nc, name, shape, dtype, psum=False):
        if psum:
            self.t = nc.alloc_psum_tensor(name, list(shape), dtype)
        else:
            self.t = nc.alloc_sbuf_tensor(name, list(shape), dtype)
        self.b = Buf()
        self.shape = shape
        self.name = name

    def __getitem__(self, k):
        return self.t[k]


class Sched:
    ENGS = ("pe", "act", "dve", "pool", "sp")

    def __init__(self, nc, n_dma_sems=8):
        self.nc = nc
        self.prog = {e: [] for e in self.ENGS}
        self.sem = {}
        self.cnt = {}
        self.nsem = 0
        for e in ("pe", "act", "dve", "pool"):
            self._new_sem(e)
        self.seen = {e: {} for e in self.ENGS}
        self.dma_sems = {}
        self.dma_cnt = {}
        self.dma_rr = {}
        for q in ("sp", "act", "pool"):
            self.dma_sems[q] = [nc.alloc_semaphore(name=f"dma_{q}_{i}") for i in range(n_dma_sems)]
            self.dma_cnt[q] = [0] * n_dma_sems
            self.dma_rr[q] = 0
        self.n_instr = 0

    def _new_sem(self, e):
        self.sem[e] = self.nc.alloc_semaphore(name=f"sem_{e}_{self.nsem}")
        self.nsem += 1
        self.cnt[e] = 0

    def _need(self, e, toks):
        best = {}
        for tok in toks:
            if tok is None:
                continue
            s, v, src = tok
            if src == "pe" and e == "pe":
                continue
            if self.seen[e].get(id(s), 0) >= v:
                continue
            if id(s) not in best or best[id(s)][1] < v:
                best[id(s)] = (s, v)
        for s, v in best.values():
            self.seen[e][id(s)] = v
            self.prog[e].append(("wait", s, v))
            self.n_instr += 1

    @staticmethod
    def _deps(reads, writes):
        toks = []
        for b in reads:
            toks.append(b.w)
        for b in writes:
            toks.append(b.w)
            toks.extend(b.r)
        return toks

    @staticmethod
    def _commit(tok, reads, writes):
        for b in reads:
            if not b.const:
                b.r.append(tok)
        for b in writes:
            b.w = tok
            b.r = []

    def op(self, e, fn, r=(), w=()):
        r = [x.b if hasattr(x, "b") else x for x in r]
        w = [x.b if hasattr(x, "b") else x for x in w]
        self._need(e, self._deps(r, w))
        if self.cnt[e] >= SEM_LIMIT:
            self._new_sem(e)
        self.cnt[e] += 1
        tok = (self.sem[e], self.cnt[e], e)
        if e == "pe":
            self.last_pe = (self.sem[e], self.cnt[e])
        self.prog[e].append(("op", fn, self.sem[e]))
        self.n_instr += 1
        self._commit(tok, r, w)
        return tok

    def dma(self, q, fn, r=(), w=()):
        r = [x.b if hasattr(x, "b") else x for x in r]
        w = [x.b if hasattr(x, "b") else x for x in w]
        i = self.dma_rr[q]
        self.dma_rr[q] = (i + 1) % len(self.dma_sems[q])
        s = self.dma_sems[q][i]
        toks = self._deps(r, w)
        if self.dma_cnt[q][i] > 0:
            toks.append((s, self.dma_cnt[q][i], "dma"))
        self._need(q, toks)
        self.dma_cnt[q][i] += 16
        tok = (s, self.dma_cnt[q][i], "dma")
        self.prog[q].append(("dma", fn, s))
        self.n_instr += 1
        self._commit(tok, r, w)
        return tok

    def wait_all(self, e, bufs):
        self._need(e, [b.w for b in bufs])

    def _all_toks(self):
        toks = []
        for e in ("pe", "act", "dve", "pool"):
            if self.cnt[e] > 0:
                toks.append((self.sem[e], self.cnt[e], "x"))
        for q in self.dma_sems:
            for s, c in zip(self.dma_sems[q], self.dma_cnt[q]):
                if c > 0:
                    toks.append((s, c, "dma"))
        return toks

    def barrier(self):
        toks = self._all_toks()
        for e in self.ENGS:
            self._need(e, toks)

    def drain(self, e):
        self._need(e, self._all_toks())

    def emit(self):
        nc = self.nc
        with nc.Block() as block:
            def run(e):
                def body(eng):
                    for item in self.prog[e]:
                        if item[0] == "wait":
                            eng.wait_ge(item[1], item[2])
                        elif item[0] == "op":
                            item[1](eng).then_inc(item[2], 1)
                        else:
                            item[1](eng).then_inc(item[2], 16)
                return body
            block.tensor(run("pe"))
            block.scalar(run("act"))
            block.vector(run("dve"))
            block.gpsimd(run("pool"))
            block.sync(run("sp"))


class Rot:
    def __init__(self, items):
        self.items = items
        self.i = 0

    def next(self):
        x = self.items[self.i]
        self.i = (self.i + 1) % len(self.items)
        return x


def make_consts(T, PAST):
    c = {}
    i = np.arange(128)
    s, t = i[:, None], i[None, :]
    same8 = (s // 8) == (t // 8)
    ident = np.eye(128, dtype=np.float32)
    su = (s < t).astype(np.float32)
    ui = (s <= t).astype(np.float32)
    sl = (s > t).astype(np.float32)
    mats = np.stack([
        ident,
        su, ui, sl,
        LWC * ui,
        LWC * su,
        LWC * sl,
        LWC * ui * same8,
        LWC * su * same8,
        LWC * sl * same8,
    ], axis=1).astype(np.float32)
    c["mats"] = mats
    mask4 = np.concatenate([su, ui, su, ui], axis=1).astype(np.float32)
    c["mask4"] = mask4
    red = np.zeros((128, 17), np.float32)
    red[:, 0] = LWC
    for b in range(16):
        red[b * 8:(b + 1) * 8, 1 + b] = LWC
    c["red"] = red
    mb = np.where(s <= t, 0.0, -30000.0).astype(np.float32)
    c["maskb"] = np.concatenate([mb, mb, mb, mb], axis=1).astype(ml_dtypes.bfloat16)
    mb8 = mb[:8, :8]
    m8 = np.zeros((128, 16), np.float32)
    m8[:8, :8] = mb8
    m8[:8, 8:] = mb8
    c["maskb8"] = m8.astype(ml_dtypes.bfloat16)
    half = 8
    inv = (500000.0 ** (-np.arange(half, dtype=np.float32) * 2.0 / 16)).astype(np.float32)
    def tab(pos):
        ang = pos.astype(np.float32)[:, None] * inv[None, :]
        return np.concatenate([np.cos(ang), np.sin(ang)], axis=1).astype(np.float32)
    c["rope_p"] = tab(np.arange(T))
    c["rope_s"] = tab(PAST + (np.arange(128) % 8))
    return c


class StopBuild(Exception):
    pass


import os as _os
_STOPAT = _os.environ.get("STOPAT", "")


_DBG = [] if _os.environ.get("PEDBG") else None


def ck(name):
    if _STOPAT and name == _STOPAT:
        raise StopBuild()


def build_program(T, NPG, NPHYS, stage=99):
    NT = T // 128
    PAST = NPG * 128
    nc = bass.Bass("TRN2", target_bir_lowering=False)
    S = Sched(nc)

    def din(name, shape, dt=F32):
        return nc.dram_tensor(name, list(shape), dt, kind="ExternalInput").ap()

    def dout(name, shape, dt=F32):
        return nc.dram_tensor(name, list(shape), dt, kind="ExternalOutput").ap()

    def dscr(name, shape, dt=F32):
        return nc.dram_tensor(name, list(shape), dt, kind="Internal").ap()

    xp = din("xp", [T, D]); xs = din("xs", [128, D])
    cache_k = din("cache_k", [NPHYS * 128, 1024]); cache_v = din("cache_v", [NPHYS * 128, 1024])
    ptab = din("ptab", [1, 16 * NPG], I32)
    swkv = din("swkv", [NA, 16, 32, 64, 64]); sshift = din("sshift", [NA, 16, D])
    cvec = din("cvec", [17, D])
    ada_w = din("ada_w", [DEPTH, D, 3 * D]); ada_b = din("ada_b", [DEPTH, 3 * D])
    ln_g = din("ln_g", [DEPTH, D]); ln_b = din("ln_b", [DEPTH, D])
    mu = din("mu", [NA, 6, D])
    w_rkvg = din("w_rkvg", [NA, 4, D, E]); w0 = din("w0", [NA, E])
    w_d1 = din("w_decay1", [NA, D, 64]); w_d2 = din("w_decay2", [NA, 64, E])
    a0 = din("a0", [NA, E]); w_a1 = din("w_a1", [NA, D, 64]); w_a2 = din("w_a2", [NA, 64, E])
    k_k = din("k_k", [NA, E]); k_a = din("k_a", [NA, E]); r_k = din("r_k", [NA, E])
    gn_g = din("gn_g", [NA, E]); gn_b = din("gn_b", [NA, E])
    w_o_a = din("w_o_a", [NA, E, D])
    ada_kv_w = din("ada_kv_w", [D, 2 * D]); ada_kv_b = din("ada_kv_b", [1, 2 * D])
    w_kv = din("w_kv", [D, 2048]); w_qg = din("w_qg", [2, D, 4096])
    lam_qk = din("lam_qk", [2, 256]); subln_g = din("subln_g", [2, 128]); w_o_b = din("w_o_b", [2, E, D])
    c_mats = din("c_mats", [128, 10, 128]); c_mask4 = din("c_mask4", [128, 512]); c_red = din("c_red", [128, 17])
    c_maskb = din("c_maskb", [128, 512], BF16); c_maskb8 = din("c_maskb8", [128, 16], BF16)
    c_rope_p = din("c_rope_p", [T, 16]); c_rope_s = din("c_rope_s", [128, 16])
    y_p = dout("y_p", [T, D]); y_s = dout("y_s", [128, D])
    k_p = dout("k_p", [T, 1024]); v_p = dout("v_p", [T, 1024])
    k_s = dout("k_s", [128, 1024]); v_s = dout("v_s", [128, 1024])
    wkv_p = dout("wkv_p", [NA, 32, 64, 64]); shift_p = dout("shift_p", [NA, D])
    wkv_s = dout("wkv_s", [NA, 16, 32, 64, 64]); shift_s = dout("shift_s", [NA, 16, D])
    out_bufs = []
    xb_p = dscr("xb_p", [T, D]); xb_s = dscr("xb_s", [128, D])
    xbuf_p = Buf(); xbuf_s = Buf()
    wb_rkvg = dscr("wb_rkvg", [NA, 4, D, E], BF16); wb_oa = dscr("wb_oa", [NA, E, D], BF16)
    wb_d1 = dscr("wb_d1", [NA, D, 64], BF16); wb_d2 = dscr("wb_d2", [NA, 64, E], BF16)
    wb_a1 = dscr("wb_a1", [NA, D, 64], BF16); wb_a2 = dscr("wb_a2", [NA, 64, E], BF16)
    wb_kv = dscr("wb_kv", [D, 2048], BF16); wb_qg = dscr("wb_qg", [2, D, 4096], BF16)
    wb_ob = dscr("wb_ob", [2, E, D], BF16)
    wbufs = []

    def DMA(q, out_ap, in_ap, r=(), w=()):
        S.dma(q, lambda e: e.dma_start(out=out_ap, in_=in_ap), r=r, w=w)

    def conv(dst, src, rows, chunk=1024):
        for r0 in range(0, rows, chunk):
            r1 = min(rows, r0 + chunk)
            b_ = Buf()
            DMA("pool", dst[r0:r1], src[r0:r1], w=[b_])
            b_.const = True
            wbufs.append(b_)
    conv(wb_rkvg.rearrange("l n d e -> (l n d) e"), w_rkvg.rearrange("l n d e -> (l n d) e"), NA * 4 * D)
    conv(wb_oa.rearrange("l e d -> (l e) d"), w_o_a.rearrange("l e d -> (l e) d"), NA * E, 2048)
    conv(wb_d1.rearrange("l d e -> (l d) e"), w_d1.rearrange("l d e -> (l d) e"), NA * D, 2048)
    conv(wb_a1.rearrange("l d e -> (l d) e"), w_a1.rearrange("l d e -> (l d) e"), NA * D, 2048)
    conv(wb_d2.rearrange("l d e -> (l d) e"), w_d2.rearrange("l d e -> (l d) e"), NA * 64, 2048)
    conv(wb_a2.rearrange("l d e -> (l d) e"), w_a2.rearrange("l d e -> (l d) e"), NA * 64, 2048)
    conv(wb_kv, w_kv, D)
    conv(wb_qg.rearrange("l d e -> (l d) e"), w_qg.rearrange("l d e -> (l d) e"), 2 * D, 512)
    conv(wb_ob.rearrange("l e d -> (l e) d"), w_o_b.rearrange("l e d -> (l e) d"), 2 * E, 2048)

    mats = Tile(nc, "mats", [128, 10, 128], F32)
    S.dma("sp", lambda e: e.dma_start(out=mats[:, :, :], in_=c_mats), w=[mats])
    mask4 = Tile(nc, "mask4", [128, 512], F32)
    S.dma("sp", lambda e: e.dma_start(out=mask4[:, :], in_=c_mask4), w=[mask4])
    red = Tile(nc, "red", [128, 17], F32)
    S.dma("sp", lambda e: e.dma_start(out=red[:, :], in_=c_red), w=[red])
    maskb = Tile(nc, "maskb", [128, 512], BF16)
    S.dma("sp", lambda e: e.dma_start(out=maskb[:, :], in_=c_maskb), w=[maskb])
    maskb8 = Tile(nc, "maskb8", [128, 16], BF16)
    S.dma("sp", lambda e: e.dma_start(out=maskb8[:, :], in_=c_maskb8), w=[maskb8])
    identb = Tile(nc, "identb", [128, 128], BF16)
    S.op("dve", lambda e: e.tensor_copy(out=identb[:, :], in_=mats[:, 0, :]), r=[mats], w=[identb])
    for t_ in (mats, mask4, red, maskb, maskb8, identb):
        t_.b.const = True
    IDF = lambda n=128: mats[0:n, 0, 0:n]

    PB = [Tile(nc, f"pb{i}", [128, 512], F32, psum=True) for i in range(8)]
    pb_x0, pb_x1 = PB[0], PB[1]
    projR = Rot([PB[2], PB[3]])
    pb_tr = PB[4]
    scanR = Rot([PB[5], PB[6]])
    pb_y = PB[7]

    def ev_copy(eng, out_ap, in_ap, r, w):
        if eng == "act":
            S.op("act", lambda e: e.activation(out=out_ap, in_=in_ap, func=AF.Copy), r=r, w=w)
        else:
            S.op(eng, lambda e: e.tensor_copy(out=out_ap, in_=in_ap), r=r, w=w)

    def TTo(eng, out_ap, a, b_, op, r, w):
        S.op(eng, lambda e: e.tensor_tensor(out=out_ap, in0=a, in1=b_, op=op), r=r, w=w)

    def STT(eng, out_ap, a, sc, b_, op0, op1, r, w):
        S.op(eng, lambda e: e.scalar_tensor_tensor(out=out_ap, in0=a, scalar=sc, in1=b_, op0=op0, op1=op1), r=r, w=w)

    def TS(eng, out_ap, a, s1, s2, op0, op1, r, w):
        if op1 is None:
            S.op(eng, lambda e: e.tensor_scalar(out=out_ap, in0=a, scalar1=s1, scalar2=None, op0=op0), r=r, w=w)
        else:
            S.op(eng, lambda e: e.tensor_scalar(out=out_ap, in0=a, scalar1=s1, scalar2=s2, op0=op0, op1=op1), r=r, w=w)

    def ACT(out_ap, in_ap, func, r, w, scale=1.0, bias=0.0):
        S.op("act", lambda e: e.activation(out=out_ap, in_=in_ap, func=func, scale=scale, bias=bias), r=r, w=w)

    pe_inflight = {}

    def pe_guard(lhs_ap, bank):
        b0 = lhs_ap.base_partition(); k_ = lhs_ap.shape[0]
        rg = frozenset(range(b0 // 32, (b0 + k_ + 31) // 32))
        if len(rg) == 4:
            pe_inflight.clear()
            return
        hazard = any((not (rg & g2)) and bank in banks for g2, banks in pe_inflight.items())
        if hazard:
            if getattr(S, "last_pe", None) is not None:
                S.prog["pe"].append(("wait", S.last_pe[0], S.last_pe[1]))
                S.n_instr += 1
            pe_inflight.clear()
        pe_inflight.setdefault(rg, set()).add(bank)

    def MM(out_ap, lhsT, rhs, start, stop, r, w):
        if _DBG is not None:
            _DBG.append((w[0].name, start, stop, len(S.prog["pe"])))
        pe_guard(lhsT, w[0].name)
        S.op("pe", lambda e: e.matmul(out_ap, lhsT=lhsT, rhs=rhs, start=start, stop=stop), r=r, w=w)

    def TR(out_ap, in_ap, ident_ap, r, w):
        if _DBG is not None:
            _DBG.append((w[0].name, "T", "T", len(S.prog["pe"])))
        pe_guard(in_ap, w[0].name)
        S.op("pe", lambda e: e.transpose(out_ap, in_ap, ident_ap), r=r, w=w)

    evR = Rot(["act", "dve"])

    def RSQRT(t_, src_ap, dst_ap, eps, op):
        TS("dve", dst_ap, src_ap, eps, None, op, None, r=[t_], w=[t_])
        S.op("act", lambda e: e.activation(out=dst_ap, in_=dst_ap, func=AF.Sqrt), r=[t_], w=[t_])
        S.op("dve", lambda e: e.reciprocal(out=dst_ap, in_=dst_ap), r=[t_], w=[t_])

    csil = Tile(nc, "rows17", [17, D], F32)
    S.dma("sp", lambda e: e.dma_start(out=csil[:, :], in_=cvec), w=[csil])
    ACT(csil[:, :], csil[:, :], AF.Silu, r=[csil], w=[csil])
    scT = Tile(nc, "scT", [128, 8, 17], F32)
    for dc in range(8):
        pb = projR.next()
        TR(pb[:, 0:17], csil[0:17, dc * 128:(dc + 1) * 128], IDF(17), r=[csil, mats], w=[pb])
        ev_copy("dve", scT[:, dc, :], pb[:, 0:17], r=[pb], w=[scT])
    wpc_l = [Tile(nc, f"wpc{i}", [128, 8, 512], BF16) for i in range(3)]
    wpc = Rot(wpc_l)
    screp_mem = Tile(nc, "screp_mem", [128, 2, 8, 128], F32)

    class View:
        def __init__(self, base, fn):
            self.b = base.b
            self.fn = fn

        def __getitem__(self, k):
            return self.fn()[k]
    screp = {"p": View(screp_mem, lambda: screp_mem.t[:, 0, :, :]), "s": View(screp_mem, lambda: screp_mem.t[:, 1, :, :])}
    S.op("dve", lambda e: e.tensor_copy(out=screp["p"][:, :, :], in_=scT[:, :, 0:1].to_broadcast([128, 8, 128])),
         r=[scT], w=[screp["p"]])
    S.op("dve", lambda e: e.tensor_copy(out=screp["s"][:, :, :].rearrange("p c (b t) -> p c b t", t=8),
                                        in_=scT[:, :, 1:17].unsqueeze(3).to_broadcast([128, 8, 16, 8])),
         r=[scT], w=[screp["s"]])
    adaw_v = View(wpc_l[0], lambda: wpc_l[0].t[:, :, :].bitcast(F32))
    adab_t = Tile(nc, "adab", [128, 256], F32)

    def compute_mod(dst, grp, wsrc, bsrc, ncols, plus1_from):
        for cb in range(ncols // 256):
            wt = adaw_v; bt = adab_t
            csl = slice(cb * 256, (cb + 1) * 256)
            DMA("sp", wt[:, :, :], wsrc[:, csl].rearrange("(c p) n -> p c n", p=128), w=[wt])
            DMA("sp", bt[:, :], bsrc[:, csl].partition_broadcast(128), w=[bt])
            pb = projR.next()
            for dc in range(8):
                MM(pb[:, 0:256], screp[grp][:, dc, :], wt[:, dc, :], dc == 0, dc == 7, r=[screp[grp], wt], w=[pb])
            if cb * 256 >= plus1_from:
                STT("dve", dst[:, csl], pb[:, 0:256], 1.0, bt[:, :], ALU.add, ALU.add, r=[pb, bt], w=[dst])
            else:
                TTo("dve", dst[:, csl], pb[:, 0:256], bt[:, :], ALU.add, r=[pb, bt], w=[dst])

    x_t = Rot([Tile(nc, f"x{i}", [128, D], F32) for i in range(1)])
    h_t = Tile(nc, "h", [128, D], F32)
    lnv = {}
    lng_t = Tile(nc, "lng", [128, D], F32); lnb_t = Tile(nc, "lnb", [128, D], F32)
    st6 = Tile(nc, "st6", [128, 2, 6], F32); mv = Tile(nc, "mv", [128, 2], F32); rstd = Tile(nc, "rstd", [128, 1], F32)
    xn_t = h_t
    xo_t = Rot([Tile(nc, "xo0", [128, D], F32)])

    def load_ln(l):
        S.dma("sp", lambda e: e.dma_start(out=lng_t[:, :], in_=ln_g[l:l + 1, :].partition_broadcast(128)), w=[lng_t])
        S.dma("sp", lambda e: e.dma_start(out=lnb_t[:, :], in_=ln_b[l:l + 1, :].partition_broadcast(128)), w=[lnb_t])

    def residual_ln(xt, gate1_ap, gate_tile, dst_ap, dst_buf, also=None):
        for hf, pb in enumerate((pb_x0, pb_x1)):
            sl = slice(hf * 512, (hf + 1) * 512)
            TTo("dve", xn_t[:, sl], pb[:, :], gate1_ap[:, sl], ALU.mult, r=[pb, gate_tile], w=[xn_t])
        STT("dve", xn_t[:, :], xt[:, :], ALPHA, xn_t[:, :], ALU.mult, ALU.add, r=[xt, xn_t], w=[xn_t])
        for hf in range(2):
            S.op("dve", lambda e, hf=hf: e.bn_stats(out=st6[:, hf, :], in_=xn_t[:, hf * 512:(hf + 1) * 512]), r=[xn_t], w=[st6])
        S.op("dve", lambda e: e.bn_aggr(out=mv[:, :], in_=st6[:, :, :]), r=[st6], w=[mv])
        ev_copy("dve", rstd[:, :], mv[:, 1:2], r=[mv], w=[rstd])
        RSQRT(rstd, rstd[:, :], rstd[:, :], LN_EPS, ALU.add)
        TS("dve", xn_t[:, :], xn_t[:, :], mv[:, 0:1], rstd[:, 0:1], ALU.subtract, ALU.mult, r=[xn_t, mv, rstd], w=[xn_t])
        xo = xo_t.next()
        TTo("dve", xn_t[:, :], xn_t[:, :], lng_t[:, :], ALU.mult, r=[xn_t, lng_t], w=[xn_t])
        TTo("dve", xo[:, :], xn_t[:, :], lnb_t[:, :], ALU.add, r=[xn_t, lnb_t], w=[xo])
        if dst_ap is not None:
            S.dma("sp", lambda e: e.dma_start(out=dst_ap, in_=xo[:, :]), r=[xo], w=[dst_buf])
        return xo

    def new_out_buf():
        b = Buf()
        out_bufs.append(b)
        return b

    mod_one = Tile(nc, "mod_one", [128, 3 * D], F32)
    mod = {"p": mod_one, "s": mod_one}
    onesf = Tile(nc, "onesf", [128, 128], F32); onesb = Tile(nc, "onesb", [128, 128], BF16)
    S.op("pool", lambda e: e.memset(onesf[:, :], 1.0), w=[onesf])
    S.op("dve", lambda e: e.tensor_copy(out=onesb[:, :], in_=onesf[:, :]), r=[onesf], w=[onesb])
    zerosb = Tile(nc, "zerosb", [128, 128], BF16)
    S.op("pool", lambda e: e.memset(zerosb[:, :], 0.0), w=[zerosb])
    onesf.b.const = True; onesb.b.const = True; zerosb.b.const = True
    sbuf_mark = nc.sbuf_base
    vec = {nm: Tile(nc, f"vec_{nm}", [128, 512], F32) for nm in ("k_k", "k_a", "r_k", "gn_g", "gn_b", "w0", "a0")}
    vec_src = {"k_k": k_k, "k_a": k_a, "r_k": r_k, "gn_g": gn_g, "gn_b": gn_b, "w0": w0, "a0": a0}
    mu6 = csil
    muT = Tile(nc, "muT", [128, 6, 8], F32)
    hT = Tile(nc, "hT", [128, 8, 129], F32)
    dT = Tile(nc, "dT", [128, 8, 128], F32)
    xsn = [Tile(nc, f"xs{n}", [128, 8, 128], BF16) for n in range(6)]
    stsh = csil; stT = Tile(nc, "stT", [128, 8, 16], F32)
    wd1 = Tile(nc, "wd1", [128, 8, 64], BF16); wa1 = Tile(nc, "wa1", [128, 8, 64], BF16)
    wd2 = Tile(nc, "wd2", [64, 512], BF16); wa2 = Tile(nc, "wa2", [64, 512], BF16)
    t1T = Tile(nc, "t1T", [64, 128], BF16); t2T = Tile(nc, "t2T", [64, 128], BF16)
    wo_pc = Rot([Tile(nc, f"wopc{i}", [128, 4, D], BF16) for i in range(1)])
    W = {nm: Tile(nc, f"w_{nm}", [128, 512], F32) for nm in
         ("r", "k", "v", "sg", "sig", "asig", "kk", "km", "zb", "t0", "e0", "e1")}
    W["y"] = W["e1"]; W["y2"] = W["e0"]
    ss8 = Tile(nc, "ss8", [128, 8], F32); rn8 = Tile(nc, "rn8", [128, 8], F32); bs8 = Tile(nc, "bs8", [128, 8], F32)
    s1 = Tile(nc, "s1", [128, 8], F32); s2 = Tile(nc, "s2", [128, 8], F32); m8 = Tile(nc, "m8", [128, 8], F32); r8 = Tile(nc, "r8", [128, 8], F32)
    Bq = {nm: Tile(nc, f"b_{nm}", [128, 512], BF16) for nm in ("Rh", "Ah", "Bc", "Kc", "Kt", "Bt", "V", "out")}
    ART = Tile(nc, "ART", [128, 4, 256], BF16); BcT = Tile(nc, "BcT", [128, 4, 128], BF16); KcT = Tile(nc, "KcT", [128, 4, 128], BF16)
    outT = Tile(nc, "outT", [128, 4, 128], BF16)
    gam = Tile(nc, "gam", [128, 16, 17], F32)
    ST = Tile(nc, "ST", [128, 16, 64], F32); STb = Tile(nc, "STb", [128, 16, 64], BF16)
    QRES = []
    for qd in range(2):
        QRES.append({
            "MMq": [Tile(nc, f"MMq{qd}_{i}", [128, 512], BF16) for i in range(4)],
            "Xq": Rot([Tile(nc, f"Xq{qd}_{i}", [128, 4, 128], BF16) for i in range(2)]),
            "XTq": Rot([Tile(nc, f"XTq{qd}_{i}", [128, 4, 128], BF16) for i in range(2)]),
            "Pq": Tile(nc, f"Pq{qd}", [128, 4, 128], F32), "Pbq": Tile(nc, f"Pbq{qd}", [128, 4, 128], BF16),
            "Wbq": Tile(nc, f"Wbq{qd}", [128, 4, 64], BF16), "Ubq": Tile(nc, f"Ubq{qd}", [128, 4, 64], BF16),
            "R": Rot([PB[5], PB[6]]) if qd == 0 else Rot([PB[2], PB[3]]),
        })
    smp = Tile(nc, "smp", [8, 3, 512], BF16)
    s0ld = Tile(nc, "s0ld", [64, 8, 64], F32)
    sout = Tile(nc, "sout", [64, 8, 64], F32)
    src3 = Tile(nc, "src3", [128, 3, 512], BF16)

    def scan_chunk(n, cols, lvls, rows_fn, cbi, st_tile, stb_tile, gam_col, ybank, ART_, BcT_, KcT_):
        ART4 = ART_[:, :, :].rearrange("p s (q t) -> p s q t", q=2)

        def quad_body(quad, Rq):
            MMq, Xq, XTq, Pq, Pbq, Wbq, Ubq, scanR = (Rq[k_] for k_ in ("MMq", "Xq", "XTq", "Pq", "Pbq", "Wbq", "Ubq", "R"))
            heads = [quad * 4 + i for i in range(4)]
            hord = [(0, heads[0]), (2, heads[2]), (1, heads[1]), (3, heads[3])]
            X0 = Xq.next(); XT0 = XTq.next()
            pbT = scanR.next()
            for qi, hl in hord:
                sb, h2 = hl // 2, hl % 2
                pr = slice(h2 * 64, h2 * 64 + 64)
                MM(pbT[0:n, qi * 128:qi * 128 + n], ART4[pr, sb, 0, cols], BcT_[pr, sb, cols], True, True, r=[ART_, BcT_], w=[pbT])
            TTo("dve", XT0[0:n, :, 0:n], pbT[0:n, :].rearrange("p (q t) -> p q t", q=4)[:, :, 0:n],
                mats[0:n, 3:4, 0:n].to_broadcast([n, 4, n]), ALU.mult, r=[pbT, mats], w=[XT0])
            yield
            for qi, hl in hord:
                sb, h2 = hl // 2, hl % 2
                pr = slice(h2 * 64, h2 * 64 + 64)
                arh = ART4[pr, sb, :, cols]
                pb = scanR.next()
                MM(pb[0:n, 0:2 * n], BcT_[pr, sb, cols], arh, True, True, r=[BcT_, ART_], w=[pb])
                MM(pb[0:n, 2 * n:4 * n], KcT_[pr, sb, cols], arh, True, True, r=[KcT_, ART_], w=[pb])
                if n == 128:
                    TTo("dve", MMq[qi][:, :], pb[:, :], mask4[:, :], ALU.mult, r=[pb, mask4], w=[MMq[qi]])
                else:
                    m4 = mask4[0:n, :].rearrange("p (k t) -> p k t", k=4)[:, :, 0:n]
                    TTo("dve", MMq[qi][0:n, 0:4 * n].rearrange("p (k t) -> p k t", k=4),
                        pb[0:n, 0:4 * n].rearrange("p (k t) -> p k t", k=4), m4, ALU.mult, r=[pb, mask4], w=[MMq[qi]])
                if qi == 2:
                    yield
            for qi in range(4):
                ev_copy("act", X0[0:n, qi, 0:n], MMq[qi][0:n, 0:n], r=[MMq[qi]], w=[X0])
            TTo("dve", Pq[0:n, :, 0:n], X0[0:n, :, 0:n], mats[0:n, 0:1, 0:n].to_broadcast([n, 4, n]), ALU.add, r=[X0, mats], w=[Pq])
            ev_copy("act", Pbq[0:n, :, 0:n], Pq[0:n, :, 0:n], r=[Pq], w=[Pbq])
            yield
            Xc, XTc = X0, XT0
            for lv in range(1, lvls):
                last = lv == lvls - 1
                pbXT = scanR.next()
                for qi in range(4):
                    MM(pbXT[0:n, qi * 128:qi * 128 + n], Xc[0:n, qi, 0:n], XTc[0:n, qi, 0:n], True, True, r=[Xc, XTc], w=[pbXT])
                XTn = XTq.next()
                ev_copy("act", XTn[0:n, :, 0:n], pbXT[0:n, :].rearrange("p (q t) -> p q t", q=4)[:, :, 0:n], r=[pbXT], w=[XTn])
                if not last:
                    pbX = scanR.next()
                    for qi in range(4):
                        MM(pbX[0:n, qi * 128:qi * 128 + n], XTc[0:n, qi, 0:n], Xc[0:n, qi, 0:n], True, True, r=[Xc, XTc], w=[pbX])
                    Xn = Xq.next()
                    ev_copy("dve", Xn[0:n, :, 0:n], pbX[0:n, :].rearrange("p (q t) -> p q t", q=4)[:, :, 0:n], r=[pbX], w=[Xn])
                yield
                pbP = scanR.next()
                for qi in range(4):
                    MM(pbP[0:n, qi * 128:qi * 128 + n], XTn[0:n, qi, 0:n], Pbq[0:n, qi, 0:n], True, True, r=[XTn, Pbq], w=[pbP])
                TTo("dve", Pq[0:n, :, 0:n], Pq[0:n, :, 0:n], pbP[0:n, :].rearrange("p (q t) -> p q t", q=4)[:, :, 0:n], ALU.add, r=[Pq, pbP], w=[Pq])
                ev_copy("act", Pbq[0:n, :, 0:n], Pq[0:n, :, 0:n], r=[Pq], w=[Pbq])
                XTc = XTn
                if not last:
                    Xc = Xn
                yield
            pbW = scanR.next()
            for qi, hl in hord:
                sb, h2 = hl // 2, hl % 2
                pr = slice(h2 * 64, h2 * 64 + 64)
                hh = cbi * 4 + sb
                MM(pbW[0:n, qi * 64:(qi + 1) * 64], ART4[pr, sb, 0, cols], stb_tile[pr, hh, :], True, False, r=[ART_, stb_tile], w=[pbW])
                MM(pbW[0:n, qi * 64:(qi + 1) * 64], MMq[qi][0:n, 2 * n:3 * n], rows_fn("V", hl), False, True, r=[MMq[qi], src3, smp], w=[pbW])
            ev_copy("act", Wbq[0:n, :, :], pbW[0:n, 0:256].rearrange("p (q i) -> p q i", q=4), r=[pbW], w=[Wbq])
            yield
            pbU = scanR.next()
            for qi in range(4):
                MM(pbU[0:n, qi * 64:(qi + 1) * 64], Pbq[0:n, qi, 0:n], Wbq[0:n, qi, :], True, True, r=[Pbq, Wbq], w=[pbU])
            ev_copy("dve", Ubq[0:n, :, :], pbU[0:n, 0:256].rearrange("p (q i) -> p q i", q=4), r=[pbU], w=[Ubq])
            yield
            for qi, hl in hord:
                sb, h2 = hl // 2, hl % 2
                pr = slice(h2 * 64, h2 * 64 + 64)
                hh = cbi * 4 + sb
                yo = ybank[0:n, hl * 64:(hl + 1) * 64]
                MM(yo, ART4[pr, sb, 1, cols], stb_tile[pr, hh, :], True, False, r=[ART_, stb_tile], w=[ybank])
                MM(yo, MMq[qi][0:n, n:2 * n], Ubq[0:n, qi, :], False, False, r=[MMq[qi], Ubq], w=[ybank])
                MM(yo, MMq[qi][0:n, 3 * n:4 * n], rows_fn("V", hl), False, True, r=[MMq[qi], src3, smp], w=[ybank])
            pbS = scanR.next()
            for qi, hl in hord:
                sb, h2 = hl // 2, hl % 2
                pr = slice(h2 * 64, h2 * 64 + 64)
                so = pbS[pr, (qi // 2) * 64:(qi // 2) * 64 + 64]
                MM(so, rows_fn("Bt", hl), Ubq[0:n, qi, :], True, False, r=[src3, smp, Ubq], w=[pbS])
                MM(so, rows_fn("Kt", hl), rows_fn("V", hl), False, True, r=[src3, smp], w=[pbS])
            for sbl in range(2):
                sb = quad * 2 + sbl
                hh = cbi * 4 + sb
                STT("dve", st_tile[:, hh, :], st_tile[:, hh, :], gam_col(hh), pbS[:, sbl * 64:(sbl + 1) * 64],
                    ALU.mult, ALU.add, r=[st_tile, gam, pbS, stb_tile], w=[st_tile])

        for qd_ in range(2):
            Rq_ = dict(QRES[qd_]); Rq_["R"] = scanR
            for _ in quad_body(qd_, Rq_):
                pass
        ev_copy("act", stb_tile[:, cbi * 4:(cbi + 1) * 4, :], st_tile[:, cbi * 4:(cbi + 1) * 4, :], r=[st_tile], w=[stb_tile])

    def RED(out_ap, in_ap, r, w):
        S.op("dve", lambda e: e.tensor_reduce(out=out_ap, in_=in_ap, axis=AX.X, op=ALU.add), r=r, w=w)

    def MEMSET(ap, val, w):
        S.op("pool", lambda e: e.memset(ap, val), w=w)

    def v3(t_):
        return t_[:, :].rearrange("p (h j) -> p h j", j=64)

    def bc3(t_):
        return t_[:, :].unsqueeze(2).to_broadcast([128, 8, 64])

    def state_out(dst_fn, cbs):
        for cbo in cbs:
            pbo = scanR.next()
            for sb in range(4):
                TR(pbo[0:64, sb * 128:(sb + 1) * 128], ST[:, cbo * 4 + sb, :], IDF(), r=[ST, mats], w=[pbo])
            ev_copy("dve", sout[:, :, :], pbo[0:64, :].rearrange("i (h j) -> i h j", j=64), r=[pbo], w=[sout])
            DMA("sp", dst_fn(cbo), sout[:, :, :], r=[sout], w=[new_out_buf()])

    def rwkv_layer(l):
        compute_mod(mod["p"], "p", ada_w[l], ada_b[l:l + 1, :], 3 * D, D)
        load_ln(l)
        DMA("sp", mu6[0:6, :], mu[l], w=[mu6])
        for dc in range(8):
            pb = projR.next()
            TR(pb[:, 0:6], mu6[0:6, dc * 128:(dc + 1) * 128], IDF(6), r=[mu6, mats], w=[pb])
            ev_copy("dve", muT[:, :, dc], pb[:, 0:6], r=[pb], w=[muT])
        S.wait_all("sp", wbufs)
        DMA("sp", wd1[:, :, :], wb_d1[l].rearrange("(c p) n -> p c n", p=128), w=[wd1])
        DMA("sp", wa1[:, :, :], wb_a1[l].rearrange("(c p) n -> p c n", p=128), w=[wa1])
        MEMSET(ST[:, :, :], 0.0, [ST])
        MEMSET(STb[:, :, :], 0.0, [STb])
        MEMSET(hT[:, :, 0:1], 0.0, [hT])
        ck("setup")
        tiles = [("p", i) for i in range(NT)] + [("s", 0)]
        for grp, ti in tiles:
            smp_mode = grp == "s"
            if smp_mode:
                compute_mod(mod["s"], "s", ada_w[l], ada_b[l:l + 1, :], 3 * D, D)
            xt = x_t.next()
            if l == 0:
                DMA("sp", xt[:, :], xs if smp_mode else xp[ti * 128:(ti + 1) * 128, :], w=[xt])
            else:
                DMA("sp", xt[:, :], xb_s if smp_mode else xb_p[ti * 128:(ti + 1) * 128, :], r=[xbuf_s if smp_mode else xbuf_p], w=[xt])
            m_ = mod[grp]
            TTo("dve", h_t[:, :], xt[:, :], m_[:, D:2 * D], ALU.mult, r=[xt, m_], w=[h_t])
            TTo("dve", h_t[:, :], h_t[:, :], m_[:, 0:D], ALU.add, r=[h_t, m_], w=[h_t])
            if grp == "p" and ti == NT - 1:
                DMA("sp", shift_p[l:l + 1, :], h_t[127:128, :], r=[h_t], w=[new_out_buf()])
            if smp_mode:
                for b in range(16):
                    DMA("sp", shift_s[l, b:b + 1, :], h_t[b * 8 + 7:b * 8 + 8, :], r=[h_t], w=[new_out_buf()])
            ck("h")
            for half in range(2):
                pb = projR.next()
                for q in range(4):
                    dc = half * 4 + q
                    TR(pb[:, q * 128:(q + 1) * 128], h_t[:, dc * 128:(dc + 1) * 128], IDF(), r=[h_t, mats], w=[pb])
                ev_copy("act", hT[:, half * 4:(half + 1) * 4, 1:129], pb[:, :].rearrange("p (q t) -> p q t", q=4), r=[pb], w=[hT])
            if not smp_mode:
                TTo("dve", dT[:, :, :], hT[:, :, 0:128], hT[:, :, 1:129], ALU.subtract, r=[hT], w=[dT])
            else:
                DMA("sp", stsh[0:16, :], sshift[l], w=[stsh])
                for dc in range(8):
                    pb = projR.next()
                    TR(pb[:, 0:16], stsh[0:16, dc * 128:(dc + 1) * 128], IDF(16), r=[stsh, mats], w=[pb])
                    ev_copy("dve", stT[:, dc, :], pb[:, 0:16], r=[pb], w=[stT])
                h4 = hT[:, :, 1:129].rearrange("p c (b t) -> p c b t", t=8)
                d4 = dT[:, :, :].rearrange("p c (b t) -> p c b t", t=8)
                TTo("dve", d4[:, :, :, 1:8], h4[:, :, :, 0:7], h4[:, :, :, 1:8], ALU.subtract, r=[hT], w=[dT])
                TTo("dve", d4[:, :, :, 0:1], stT[:, :, :].unsqueeze(3), h4[:, :, :, 0:1], ALU.subtract, r=[hT, stT], w=[dT])
            for n_ in range(6):
                for dc in range(8):
                    STT("dve", xsn[n_][:, dc, :], dT[:, dc, :], muT[:, n_, dc:dc + 1], hT[:, dc, 1:129], ALU.mult, ALU.add,
                        r=[dT, muT, hT], w=[xsn[n_]])
            if not smp_mode:
                ev_copy("act", hT[:, :, 0:1], hT[:, :, 128:129], r=[hT], w=[hT])
            ck("xs")
            pb = projR.next()
            for dc in range(8):
                MM(pb[0:64, 0:128], wd1[:, dc, :], xsn[4][:, dc, :], dc == 0, dc == 7, r=[wd1, xsn[4]], w=[pb])
            ACT(t1T[:, :], pb[0:64, 0:128], AF.Tanh, r=[pb], w=[t1T])
            pb = projR.next()
            for dc in range(8):
                MM(pb[0:64, 0:128], wa1[:, dc, :], xsn[5][:, dc, :], dc == 0, dc == 7, r=[wa1, xsn[5]], w=[pb])
            ev_copy("act", t2T[:, :], pb[0:64, 0:128], r=[pb], w=[t2T])
            Lm = (7, 8, 9) if smp_mode else (4, 5, 6)
            for cb in range(4):
                cs = slice(cb * 512, (cb + 1) * 512)
                for nm, t_ in vec.items():
                    DMA("pool", t_[:, :], vec_src[nm][l:l + 1, cs].partition_broadcast(128), w=[t_])
                DMA("pool", wd2[:, :], wb_d2[l, :, cs], w=[wd2])
                DMA("pool", wa2[:, :], wb_a2[l, :, cs], w=[wa2])

                def proj(n_):
                    wt = wpc.next()
                    DMA("sp", wt[:, :, :], wb_rkvg[l, n_, :, cs].rearrange("(c p) n -> p c n", p=128), w=[wt])
                    pb_ = projR.next()
                    for dc in range(8):
                        MM(pb_[:, :], xsn[n_][:, dc, :], wt[:, dc, :], dc == 0, dc == 7, r=[xsn[n_], wt], w=[pb_])
                    return pb_
                pb = proj(0); ev_copy("act", W["r"][:, :], pb[:, :], r=[pb], w=[W["r"]])
                pb = proj(1); ev_copy("dve", W["k"][:, :], pb[:, :], r=[pb], w=[W["k"]])
                pb = proj(2); ev_copy("act", W["v"][:, :], pb[:, :], r=[pb], w=[W["v"]])
                pb = proj(3); ACT(W["sg"][:, :], pb[:, :], AF.Silu, r=[pb], w=[W["sg"]])
                pb = projR.next()
                MM(pb[:, :], t1T[:, :], wd2[:, :], True, True, r=[t1T, wd2], w=[pb])
                TTo("dve", W["t0"][:, :], pb[:, :], vec["w0"][:, :], ALU.add, r=[pb, vec["w0"]], w=[W["t0"]])
                ACT(W["sig"][:, :], W["t0"][:, :], AF.Sigmoid, r=[W["t0"]], w=[W["sig"]])
                pb = projR.next()
                MM(pb[:, :], t2T[:, :], wa2[:, :], True, True, r=[t2T, wa2], w=[pb])
                TTo("dve", W["t0"][:, :], pb[:, :], vec["a0"][:, :], ALU.add, r=[pb, vec["a0"]], w=[W["t0"]])
                ACT(W["asig"][:, :], W["t0"][:, :], AF.Sigmoid, r=[W["t0"]], w=[W["asig"]])
                ck("proj")
                pb = projR.next()
                ncol = 17 if smp_mode else 1
                for sb in range(4):
                    MM(pb[:, sb * 32:sb * 32 + ncol], W["sig"][:, sb * 128:(sb + 1) * 128], red[:, 0:ncol], True, True, r=[W["sig"], red], w=[pb])
                ACT(gam[:, cb * 4:(cb + 1) * 4, 0:ncol], pb[:, 0:128].rearrange("p (s c) -> p s c", s=4)[:, :, 0:ncol], AF.Exp, r=[pb], w=[gam])
                TTo("dve", W["kk"][:, :], W["k"][:, :], vec["k_k"][:, :], ALU.mult, r=[W["k"], vec["k_k"]], w=[W["kk"]])
                TTo("dve", W["t0"][:, :], W["kk"][:, :], W["kk"][:, :], ALU.mult, r=[W["kk"]], w=[W["t0"]])
                RED(ss8[:, :], v3(W["t0"]), r=[W["t0"]], w=[ss8])
                ev_copy("dve", rn8[:, :], ss8[:, :], r=[ss8], w=[rn8])
                RSQRT(rn8, rn8[:, :], rn8[:, :], 1e-24, ALU.max)
                TTo("dve", v3(W["kk"]), v3(W["kk"]), bc3(rn8), ALU.mult, r=[W["kk"], rn8], w=[W["kk"]])
                STT("dve", W["t0"][:, :], W["asig"][:, :], -1.0, vec["k_a"][:, :], ALU.add, ALU.mult, r=[W["asig"], vec["k_a"]], w=[W["t0"]])
                STT("dve", W["km"][:, :], W["t0"][:, :], 1.0, W["k"][:, :], ALU.add, ALU.mult, r=[W["t0"], W["k"]], w=[W["km"]])
                TTo("dve", W["zb"][:, :], W["kk"][:, :], W["asig"][:, :], ALU.mult, r=[W["kk"], W["asig"]], w=[W["zb"]])
                TTo("dve", W["t0"][:, :], W["r"][:, :], W["km"][:, :], ALU.mult, r=[W["r"], W["km"]], w=[W["t0"]])
                TTo("dve", W["t0"][:, :], W["t0"][:, :], vec["r_k"][:, :], ALU.mult, r=[W["t0"], vec["r_k"]], w=[W["t0"]])
                RED(bs8[:, :], v3(W["t0"]), r=[W["t0"]], w=[bs8])
                def cexp(dst_nm, mi, sc):
                    pb_ = projR.next()
                    MM(pb_[:, :], mats[:, mi, :], W["sig"][:, :], True, True, r=[mats, W["sig"]], w=[pb_])
                    ACT(W[dst_nm][:, :], pb_[:, :], AF.Exp, r=[pb_], w=[W[dst_nm]], scale=sc)
                cexp("e0", Lm[0], 1.0)
                TTo("dve", Bq["Rh"][:, :], W["r"][:, :], W["e0"][:, :], ALU.mult, r=[W["r"], W["e0"]], w=[Bq["Rh"]])
                cexp("e1", Lm[1], 1.0)
                STT("dve", Bq["Ah"][:, :], W["kk"][:, :], -1.0, W["e1"][:, :], ALU.mult, ALU.mult, r=[W["kk"], W["e1"]], w=[Bq["Ah"]])
                cexp("e0", Lm[0], -1.0)
                TTo("dve", Bq["Bc"][:, :], W["zb"][:, :], W["e0"][:, :], ALU.mult, r=[W["zb"], W["e0"]], w=[Bq["Bc"]])
                TTo("dve", Bq["Kc"][:, :], W["km"][:, :], W["e0"][:, :], ALU.mult, r=[W["km"], W["e0"]], w=[Bq["Kc"]])
                cexp("e1", Lm[2], 1.0)
                TTo("dve", src3[:, 0, :], W["km"][:, :], W["e1"][:, :], ALU.mult, r=[W["km"], W["e1"]], w=[src3])
                TTo("dve", src3[:, 1, :], W["zb"][:, :], W["e1"][:, :], ALU.mult, r=[W["zb"], W["e1"]], w=[src3])
                ev_copy("act", src3[:, 2, :], W["v"][:, :], r=[W["v"]], w=[src3])
                ck("elem")
                pbb = pb_tr.t[:, :].bitcast(BF16)
                for qn, nm in enumerate(("Ah", "Rh", "Bc", "Kc")):
                    for sb in range(4):
                        TR(pbb[:, sb * 128:(sb + 1) * 128], Bq[nm][:, sb * 128:(sb + 1) * 128], identb[:, :], r=[Bq[nm], identb], w=[pb_tr])
                    dst_t = ART if qn < 2 else (BcT if qn == 2 else KcT)
                    lo = 128 if qn == 1 else 0
                    ev_copy(evR.next(), dst_t[:, :, lo:lo + 128], pbb[:, 0:512].rearrange("p (s t) -> p s t", s=4), r=[pb_tr], w=[dst_t])
                ck("tr")
                rows = {"Kt": 0, "Bt": 1, "V": 2}
                if not smp_mode:
                    scan_chunk(128, slice(0, 128), 7, lambda nm, hl: src3[:, rows[nm], hl * 64:(hl + 1) * 64], cb, ST, STb,
                               lambda hh: gam[:, hh, 0:1], pb_y, ART, BcT, KcT)
                    ev_copy("act", W["y"][:, :], pb_y[:, :], r=[pb_y], w=[W["y"]])
                else:
                    for b in range(16):
                        DMA("sp", smp[:, :, :], src3[b * 8:(b + 1) * 8, :, :], r=[src3], w=[smp])
                        DMA("sp", s0ld[:, :, :], swkv[l, b, cb * 8:(cb + 1) * 8].rearrange("h i j -> i h j"), w=[s0ld])
                        pbs = scanR.next()
                        for sb in range(4):
                            TR(pbs[:, sb * 64:(sb + 1) * 64], s0ld[:, 2 * sb:2 * sb + 2, :].rearrange("i h j -> i (h j)"),
                               IDF(64), r=[s0ld, mats], w=[pbs])
                        ev_copy("dve", ST[:, cb * 4:(cb + 1) * 4, :], pbs[:, 0:256].rearrange("p (s i) -> p s i", s=4), r=[pbs], w=[ST])
                        ev_copy("act", STb[:, cb * 4:(cb + 1) * 4, :], ST[:, cb * 4:(cb + 1) * 4, :], r=[ST], w=[STb])
                        scan_chunk(8, slice(b * 8, (b + 1) * 8), 3, lambda nm, hl: smp[0:8, rows[nm], hl * 64:(hl + 1) * 64], cb, ST, STb,
                                   lambda hh, b=b: gam[:, hh, 1 + b:2 + b], pb_y, ART, BcT, KcT)
                        ev_copy("act", W["y2"][0:8, :], pb_y[0:8, :], r=[pb_y], w=[W["y2"]])
                        DMA("sp", W["y"][b * 8:(b + 1) * 8, :], W["y2"][0:8, :], r=[W["y2"]], w=[W["y"]])
                        state_out(lambda c_, b=b: wkv_s[l, b, c_ * 8:(c_ + 1) * 8].rearrange("h i j -> i h j"), [cb])
                ck("scan")
                TTo("dve", W["y2"][:, :], W["y"][:, :], W["y"][:, :], ALU.mult, r=[W["y"]], w=[W["y2"]])
                RED(s1[:, :], v3(W["y"]), r=[W["y"]], w=[s1])
                RED(s2[:, :], v3(W["y2"]), r=[W["y2"]], w=[s2])
                TS("dve", m8[:, :], s1[:, :], 1.0 / 64, None, ALU.mult, None, r=[s1], w=[m8])
                TTo("dve", r8[:, :], m8[:, :], m8[:, :], ALU.mult, r=[m8], w=[r8])
                STT("dve", r8[:, :], s2[:, :], 1.0 / 64, r8[:, :], ALU.mult, ALU.subtract, r=[s2, r8], w=[r8])
                RSQRT(r8, r8[:, :], r8[:, :], GN_EPS, ALU.add)
                TTo("dve", v3(W["y"]), v3(W["y"]), bc3(m8), ALU.subtract, r=[W["y"], m8], w=[W["y"]])
                TTo("dve", v3(W["y"]), v3(W["y"]), bc3(r8), ALU.mult, r=[W["y"], r8], w=[W["y"]])
                TTo("dve", W["y"][:, :], W["y"][:, :], vec["gn_g"][:, :], ALU.mult, r=[W["y"], vec["gn_g"]], w=[W["y"]])
                TTo("dve", W["y"][:, :], W["y"][:, :], vec["gn_b"][:, :], ALU.add, r=[W["y"], vec["gn_b"]], w=[W["y"]])
                TTo("dve", v3(W["t0"]), v3(W["v"]), bc3(bs8), ALU.mult, r=[W["v"], bs8], w=[W["t0"]])
                TTo("dve", W["y"][:, :], W["y"][:, :], W["t0"][:, :], ALU.add, r=[W["y"], W["t0"]], w=[W["y"]])
                TTo("dve", Bq["out"][:, :], W["y"][:, :], W["sg"][:, :], ALU.mult, r=[W["y"], W["sg"]], w=[Bq["out"]])
                for sb in range(4):
                    TR(pbb[:, sb * 128:(sb + 1) * 128], Bq["out"][:, sb * 128:(sb + 1) * 128], identb[:, :], r=[Bq["out"], identb], w=[pb_tr])
                ev_copy(evR.next(), outT[:, :, :], pbb[:, 0:512].rearrange("p (s t) -> p s t", s=4), r=[pb_tr], w=[outT])
                wo = wo_pc.next()
                DMA("sp", wo[:, :, :], wb_oa[l, cb * 512:(cb + 1) * 512, :].rearrange("(c p) n -> p c n", p=128), w=[wo])
                for sb in range(4):
                    for hf, pbx in enumerate((pb_x0, pb_x1)):
                        MM(pbx[:, :], outT[:, sb, :], wo[:, sb, hf * 512:(hf + 1) * 512], cb == 0 and sb == 0, cb == 3 and sb == 3,
                           r=[outT, wo], w=[pbx])
            ck("cbs")
            if not smp_mode:
                residual_ln(xt, m_[:, 2 * D:3 * D], m_, xb_p[ti * 128:(ti + 1) * 128, :], xbuf_p)
            else:
                residual_ln(xt, m_[:, 2 * D:3 * D], m_, xb_s, xbuf_s)
            ck("tile")
            if grp == "p" and ti == NT - 1:
                state_out(lambda c_: wkv_p[l, c_ * 8:(c_ + 1) * 8].rearrange("h i j -> i h j"), [0, 1, 2, 3])

    def phase_b():
        S.barrier()
        nc.sbuf_base = sbuf_mark
        QT = Tile(nc, "QT", [128, 16, 128], BF16)
        ETr = Rot([Tile(nc, f"ET{i}", [128, 512], BF16) for i in range(2)])
        OTf = Tile(nc, "OTf", [128, 16, 128], F32)
        sqt = Tile(nc, "sqt", [128, 16, 128], F32)
        OTb = Tile(nc, "OTb", [128, 16, 128], BF16)
        SG = Tile(nc, "SG", [128, 2048], F32)
        Wq = Rot([Tile(nc, f"Wq{i}", [128, 512], F32) for i in range(2)])
        hTb = Tile(nc, "hTb", [128, 8, 128], BF16)
        rLt = Tile(nc, "rLt", [128, 512], F32); Ont = Tile(nc, "Ont", [128, 512], F32)
        wo_bR = Rot([Tile(nc, f"wo_b{i}", [128, 2, D], BF16) for i in range(2)])
        rtab = Tile(nc, "rtab", [128, 2, 8], F32)
        rp = [Tile(nc, f"rp{i}", [128, 8, 8], F32) for i in range(4)]
        lq = Tile(nc, "lq", [128, 256], F32); lq2 = Tile(nc, "lq2", [128, 2, 64], F32)
        le = Tile(nc, "le", [128, 2], F32); neglam = Tile(nc, "neglam", [128, 1], F32)
        subg = Tile(nc, "subg", [128, 1], F32)
        accO = Rot([PB[7], PB[4]]); accL = Rot([PB[2], PB[3]]); xR = Rot([PB[0], PB[1]])
        mark2 = nc.sbuf_base

        def load_x(xt, src_ap, buf):
            DMA("sp", xt[:, :], src_ap, r=[buf], w=[xt])

        def modulate(xt, m_):
            TTo("dve", h_t[:, :], xt[:, :], m_[:, D:2 * D], ALU.mult, r=[xt, m_], w=[h_t])
            TTo("dve", h_t[:, :], h_t[:, :], m_[:, 0:D], ALU.add, r=[h_t, m_], w=[h_t])

        def make_hT():
            for half in range(2):
                pb = projR.next()
                for q in range(4):
                    dc = half * 4 + q
                    TR(pb[:, q * 128:(q + 1) * 128], h_t[:, dc * 128:(dc + 1) * 128], IDF(), r=[h_t, mats], w=[pb])
                ev_copy("act", hTb[:, half * 4:(half + 1) * 4, :], pb[:, :].rearrange("p (q t) -> p q t", q=4), r=[pb], w=[hTb])

        def projB(wdram):
            wt = wpc.next()
            DMA("sp", wt[:, :, :], wdram.rearrange("(c p) n -> p c n", p=128), w=[wt])
            pb_ = projR.next()
            for dc in range(8):
                MM(pb_[:, :], hTb[:, dc, :], wt[:, dc, :], dc == 0, dc == 7, r=[hTb, wt], w=[pb_])
            return pb_

        def rope(Wt):
            v = Wt[:, :].rearrange("p (g d) -> p g d", d=64)
            x1, x2 = v[:, :, 0:8], v[:, :, 8:16]
            cos = rtab[:, 0:1, :].to_broadcast([128, 8, 8]); sin = rtab[:, 1:2, :].to_broadcast([128, 8, 8])
            TTo("dve", rp[0][:, :, :], x1, cos, ALU.mult, r=[Wt, rtab], w=[rp[0]])
            TTo("dve", rp[1][:, :, :], x2, sin, ALU.mult, r=[Wt, rtab], w=[rp[1]])
            TTo("dve", rp[2][:, :, :], x1, sin, ALU.mult, r=[Wt, rtab], w=[rp[2]])
            TTo("dve", rp[3][:, :, :], x2, cos, ALU.mult, r=[Wt, rtab], w=[rp[3]])
            TTo("dve", x1, rp[0][:, :, :], rp[1][:, :, :], ALU.subtract, r=[rp[0], rp[1]], w=[Wt])
            TTo("dve", x2, rp[3][:, :, :], rp[2][:, :, :], ALU.add, r=[rp[3], rp[2]], w=[Wt])

        def kv_tile(xt, rows_k, rows_v, KT_dst_fn, V_dst_fn):
            make_hT()
            for c in range(4):
                pb = projB(wb_kv[:, c * 512:(c + 1) * 512])
                Wt = Wq.next()
                ev_copy("act", Wt[:, :], pb[:, :], r=[pb], w=[Wt])
                if c < 2:
                    rope(Wt)
                    DMA("sp", rows_k[:, c * 512:(c + 1) * 512], Wt[:, :], r=[Wt], w=[new_out_buf()])
                    pbt = projR.next()
                    for q in range(4):
                        TR(pbt[:, q * 128:(q + 1) * 128], Wt[:, q * 128:(q + 1) * 128], IDF(), r=[Wt, mats], w=[pbt])
                    kd, kb = KT_dst_fn(c)
                    ev_copy("dve", kd, pbt[:, :].rearrange("p (q t) -> p q t", q=4), r=[pbt], w=[kb])
                else:
                    DMA("sp", rows_v[:, (c - 2) * 512:(c - 1) * 512], Wt[:, :], r=[Wt], w=[new_out_buf()])
                    vd, vb = V_dst_fn(c - 2)
                    ev_copy("dve", vd, Wt[:, :], r=[Wt], w=[vb])

        def qg_tile(j):
            make_hT()
            for c in range(4):
                pb = projB(wb_qg[j, :, c * 512:(c + 1) * 512])
                Wt = Wq.next()
                ev_copy("act", Wt[:, :], pb[:, :], r=[pb], w=[Wt])
                rope(Wt)
                pbt = projR.next()
                for q in range(4):
                    TR(pbt[:, q * 128:(q + 1) * 128], Wt[:, q * 128:(q + 1) * 128], IDF(), r=[Wt, mats], w=[pbt])
                ev_copy("dve", QT[:, c * 4:(c + 1) * 4, :], pbt[:, :].rearrange("p (q t) -> p q t", q=4), r=[pbt], w=[QT])
            for c in range(4):
                pb = projB(wb_qg[j, :, 2048 + c * 512:2048 + (c + 1) * 512])
                ACT(SG[:, c * 512:(c + 1) * 512], pb[:, :], AF.Silu, r=[pb], w=[SG])

        def layer_consts(l):
            j = l - NA
            lam_init = 0.8 - 0.6 * math.exp(-0.3 * l)
            load_ln(l)
            DMA("sp", lq[:, :], lam_qk[j:j + 1, :].partition_broadcast(128), w=[lq])
            TTo("dve", lq2[:, 0, :], lq[:, 0:64], lq[:, 64:128], ALU.mult, r=[lq], w=[lq2])
            TTo("dve", lq2[:, 1, :], lq[:, 128:192], lq[:, 192:256], ALU.mult, r=[lq], w=[lq2])
            RED(le[:, :], lq2[:, :, :], r=[lq2], w=[le])
            ACT(le[:, :], le[:, :], AF.Exp, r=[le], w=[le])
            TTo("dve", neglam[:, :], le[:, 1:2], le[:, 0:1], ALU.subtract, r=[le], w=[neglam])
            TS("dve", neglam[:, :], neglam[:, :], -lam_init, None, ALU.add, None, r=[neglam], w=[neglam])
            DMA("sp", subg[:, :], subln_g[j:j + 1, :].rearrange("o v -> v o"), w=[subg])
            return lam_init

        def attn_post(xt, m_, lam_init, j, dst_ap, dst_buf):
            OT2 = OTf[:, :, :].rearrange("p a t -> p (a t)")
            sq2 = sqt[:, :, :].rearrange("p a t -> p (a t)")
            S.op("act", lambda e: e.activation(out=sq2, in_=OT2, func=AF.Square), r=[OTf], w=[sqt])
            for c in range(4):
                cs = slice(c * 512, (c + 1) * 512)
                pb = projR.next()
                MM(pb[:, :], onesf[:, :], sq2[:, cs], True, True, r=[onesf, sqt], w=[pb])
                TS("dve", sq2[:, cs], pb[:, :], 1.0 / 128, LN_EPS, ALU.mult, ALU.add, r=[pb], w=[sqt])
                S.op("act", lambda e, cs=cs: e.activation(out=sq2[:, cs], in_=sq2[:, cs], func=AF.Sqrt), r=[sqt], w=[sqt])
                S.op("dve", lambda e, cs=cs: e.reciprocal(out=sq2[:, cs], in_=sq2[:, cs]), r=[sqt], w=[sqt])
                TTo("dve", OT2[:, cs], OT2[:, cs], sq2[:, cs], ALU.mult, r=[OTf, sqt], w=[OTf])
                pbt = projR.next()
                for q in range(4):
                    kg = c * 4 + q
                    TR(pbt[:, q * 128:(q + 1) * 128], SG[:, kg * 128:(kg + 1) * 128], IDF(), r=[SG, mats], w=[pbt])
                TTo("dve", OT2[:, cs], OT2[:, cs], pbt[:, :], ALU.mult, r=[OTf, pbt], w=[OTf])
            TS("dve", OTb[:, :, :], OTf[:, :, :], subg[:, 0:1], 1.0 - lam_init, ALU.mult, ALU.mult, r=[OTf, subg], w=[OTb])
            for piece in range(8):
                wo_b = wo_bR.next()
                DMA("sp", wo_b[:, :, :], wb_ob[j, piece * 256:(piece + 1) * 256, :].rearrange("(c p) n -> p c n", p=128), w=[wo_b])
                for sb in range(2):
                    for hf, pbx in enumerate((pb_x0, pb_x1)):
                        MM(pbx[:, :], OTb[:, piece * 2 + sb, :], wo_b[:, sb, hf * 512:(hf + 1) * 512],
                           piece == 0 and sb == 0, piece == 7 and sb == 1, r=[OTb, wo_b], w=[pbx])
            return residual_ln(xt, m_[:, 2 * D:3 * D], m_, dst_ap, dst_buf)

        def finish_head(aO, aL, ncol, out_ap, in_m0, in_m1):
            S.op("dve", lambda e: e.reciprocal(out=rLt[:, 0:ncol], in_=aL[:, 0:ncol]), r=[aL], w=[rLt])
            TTo("dve", Ont[:, 0:ncol], aO[:, 0:ncol], rLt[:, 0:ncol], ALU.mult, r=[aO, rLt], w=[Ont])
            STT("dve", out_ap, in_m1, neglam[:, 0:1], in_m0, ALU.mult, ALU.add, r=[Ont, neglam], w=[OTf])

        KT = Tile(nc, "KT", [128, 8, T], BF16)
        QTbd = Tile(nc, "QTbd", [128, 8, 512], BF16)
        MEMSET(QTbd[:, :, :], 0.0, [QTbd])
        sc4 = Rot([PB[5], PB[6], PB[0], PB[1]])
        ET4 = Rot(ETr.items + [Tile(nc, f"ETp{i}", [128, 512], BF16) for i in range(2)])
        Vb = Tile(nc, "Vb", [128, NT, 1024], BF16)
        m_ = mod["p"]
        compute_mod(m_, "p", ada_kv_w, ada_kv_b, 2 * D, D)
        for ti in range(0 if _os.environ.get("ONLYS") else NT):
            rs = slice(ti * 128, (ti + 1) * 128)
            xt = x_t.next()
            load_x(xt, xb_p[rs, :], xbuf_p)
            modulate(xt, m_)
            DMA("sp", rtab[:, :, :], c_rope_p[rs, :].rearrange("p (a c) -> p a c", a=2), w=[rtab])
            kv_tile(xt, k_p[rs, :], v_p[rs, :],
                    lambda c, ti=ti: (KT[:, c * 4:(c + 1) * 4, ti * 128:(ti + 1) * 128], KT),
                    lambda c, ti=ti: (Vb[:, ti, c * 512:(c + 1) * 512], Vb))
        ck("kvp")
        for l in (NA, NA + 1):
            if stage < l + 1 + 1 or _os.environ.get("ONLYS"):
                break
            j = l - NA
            compute_mod(m_, "p", ada_w[l], ada_b[l:l + 1, :], 3 * D, D)
            lam_init = layer_consts(l)
            for qt in range(NT):
                rs = slice(qt * 128, (qt + 1) * 128)
                xt = x_t.next()
                load_x(xt, xb_p[rs, :], xbuf_p)
                modulate(xt, m_)
                DMA("sp", rtab[:, :, :], c_rope_p[rs, :].rearrange("p (a c) -> p a c", a=2), w=[rtab])
                qg_tile(j)
                for m in range(2):
                    pr = slice(m * 64, (m + 1) * 64)
                    ev_copy(evR.next(), QTbd[pr, :, m * 256:(m + 1) * 256].rearrange("p k (g t) -> p k g t", g=2),
                            QT[pr, :, :].rearrange("p (k g) t -> p k g t", g=2), r=[QT], w=[QTbd])
                steps = [(k, kt) for k in range(8) for kt in range(qt + 1)]
                LOOK = 2
                ets = {}; acc = {}
                for i in range(len(steps) + LOOK):
                    if i < len(steps):
                        k, kt = steps[i]
                        ps = sc4.next()
                        diag = kt == qt
                        if diag:
                            MM(ps[:, :], identb[:, :], maskb[:, :], True, False, r=[identb, maskb], w=[ps])
                        MM(ps[:, :], KT[:, k, kt * 128:(kt + 1) * 128], QTbd[:, k, :], not diag, True, r=[KT, QTbd], w=[ps])
                        et = ET4.next()
                        ACT(et[:, :], ps[:, :], AF.Exp, r=[ps], w=[et], scale=0.125)
                        ets[i] = et
                    jj = i - LOOK
                    if jj >= 0:
                        k, kt = steps[jj]
                        if kt == 0:
                            acc[k] = (accO.next(), accL.next())
                        aO, aL = acc[k]
                        et = ets.pop(jj)
                        MM(aO[:, :], Vb[:, kt, k * 128:(k + 1) * 128], et[:, :], kt == 0, kt == qt, r=[Vb, et], w=[aO])
                        MM(aL[:, :], onesb[:, :], et[:, :], kt == 0, kt == qt, r=[onesb, et], w=[aL])
                        if kt == qt:
                            finish_head(aO, aL, 512, OTf[:, 2 * k:2 * k + 2, :],
                                        Ont[:, 0:256].rearrange("p (g t) -> p g t", g=2),
                                        Ont[:, 256:512].rearrange("p (g t) -> p g t", g=2))
                if l == DEPTH - 1:
                    attn_post(xt, m_, lam_init, j, y_p[rs, :], new_out_buf())
                else:
                    attn_post(xt, m_, lam_init, j, xb_p[rs, :], xbuf_p)
            ck(f"attp{l}")

        S.barrier()
        nc.sbuf_base = mark2
        KTs = Tile(nc, "KTs", [128, 8, 128], BF16)
        Vs = Tile(nc, "Vs", [128, 1024], BF16)
        vnew = Tile(nc, "vnew", [8, 1024], BF16)
        Qbd = Tile(nc, "Qbd", [128, 16, 8, 32], BF16)
        KpgR = Rot([Tile(nc, f"Kpg{i}", [128, 1024], F32) for i in range(2)])
        VpgR = Rot([Tile(nc, f"Vpg{i}", [128, 1024], F32) for i in range(2)])
        KTbR = Rot([Tile(nc, f"KTb{i}", [128, 8, 128], BF16) for i in range(2)])
        VpbR = Rot([Tile(nc, f"Vpb{i}", [128, 1024], BF16) for i in range(2)])
        idxR = Rot([Tile(nc, f"idx{i}", [128, 1], I32) for i in range(4)])
        NPT = 16 * NPG
        pti = Tile(nc, "pti", [128, NPT], I32); ptf = Tile(nc, "ptf", [128, NPT], F32)
        iot = Tile(nc, "iot", [128, 1], F32); idxall = Tile(nc, "idxall", [128, NPT], I32)
        DMA("sp", pti[:, :], ptab.partition_broadcast(128), w=[pti])
        S.op("pool", lambda e: e.iota(iot[:, :], pattern=[[0, 1]], base=0, channel_multiplier=1,
                                      allow_small_or_imprecise_dtypes=True), w=[iot])
        ev_copy("dve", ptf[:, :], pti[:, :], r=[pti], w=[ptf])
        STT("dve", ptf[:, :], ptf[:, :], 128.0, iot[:, 0:1].to_broadcast([128, NPT]), ALU.mult, ALU.add, r=[ptf, iot], w=[ptf])
        ev_copy("dve", idxall[:, :], ptf[:, :], r=[ptf], w=[idxall])
        MEMSET(Qbd[:, :, :, :], 0.0, [Qbd])
        if _os.environ.get("DBGIDX"):
            dbg_idx = nc.dram_tensor("dbg_idx", [128, NPT], I32, kind="ExternalOutput").ap()
            DMA("sp", dbg_idx, idxall[:, :], r=[idxall], w=[new_out_buf()])

        m_ = mod["s"]
        compute_mod(m_, "s", ada_kv_w, ada_kv_b, 2 * D, D)
        xt = x_t.next()
        load_x(xt, xb_s, xbuf_s)
        modulate(xt, m_)
        DMA("sp", rtab[:, :, :], c_rope_s.rearrange("p (a c) -> p a c", a=2), w=[rtab])
        kv_tile(xt, k_s, v_s,
                lambda c: (KTs[:, c * 4:(c + 1) * 4, :], KTs),
                lambda c: (Vs[:, c * 512:(c + 1) * 512], Vs))
        ck("kvs")
        for l in (NA, NA + 1):
            if stage < l + 1 + 1:
                break
            j = l - NA
            compute_mod(m_, "s", ada_w[l], ada_b[l:l + 1, :], 3 * D, D)
            lam_init = layer_consts(l)
            xt = x_t.next()
            load_x(xt, xb_s, xbuf_s)
            modulate(xt, m_)
            qg_tile(j)
            for b in range(16):
                for m in range(2):
                    pr = slice(m * 64, (m + 1) * 64)
                    ev_copy("dve", Qbd[pr, b, :, m * 16:(m + 1) * 16].rearrange("p k (g t) -> p k g t", g=2),
                            QT[pr, :, b * 8:(b + 1) * 8].rearrange("p (k g) t -> p k g t", g=2), r=[QT], w=[Qbd])
            ck("g0")
            for b in range(16):
                aO = accO.next(); aL = accL.next()
                for jp in range(NPG + 1):
                    new = jp == NPG
                    ps = scanR.next()
                    et = ETr.next()
                    if not new:
                        it = idxR.next()
                        ev_copy("dve", it[:, :], idxall[:, b * NPG + jp:b * NPG + jp + 1], r=[idxall], w=[it])
                        kp = KpgR.next(); vp = VpgR.next()
                        S.dma("pool", lambda e, kp=kp, it=it: e.indirect_dma_start(
                            out=kp[:, :], out_offset=None, in_=cache_k[:, :],
                            in_offset=bass.IndirectOffsetOnAxis(ap=it[:, :], axis=0)), r=[it], w=[kp])
                        S.dma("pool", lambda e, vp=vp, it=it: e.indirect_dma_start(
                            out=vp[:, :], out_offset=None, in_=cache_v[:, :],
                            in_offset=bass.IndirectOffsetOnAxis(ap=it[:, :], axis=0)), r=[it], w=[vp])
                        ck("g1")
                        ktb = KTbR.next(); vpb = VpbR.next()
                        for half in range(2):
                            pbt = xR.next()
                            for q in range(4):
                                hh = half * 4 + q
                                TR(pbt[:, q * 128:(q + 1) * 128], kp[:, hh * 128:(hh + 1) * 128], IDF(), r=[kp, mats], w=[pbt])
                            ev_copy(evR.next(), ktb[:, half * 4:(half + 1) * 4, :], pbt[:, :].rearrange("p (q t) -> p q t", q=4), r=[pbt], w=[ktb])
                        ev_copy(evR.next(), vpb[:, :], vp[:, :], r=[vp], w=[vpb])
                        n = 128
                        for k in range(8):
                            MM(ps[:, k * 32:(k + 1) * 32], ktb[:, k, :], Qbd[:, b, k, :], True, True, r=[ktb, Qbd], w=[ps])
                        ACT(et[:, 0:256], ps[:, 0:256], AF.Exp, r=[ps], w=[et], scale=0.125)
                        vsrc = vpb
                    else:
                        n = 8
                        DMA("sp", vnew[:, :], Vs[b * 8:(b + 1) * 8, :], r=[Vs], w=[vnew])
                        for k in range(8):
                            MM(ps[0:8, k * 32:(k + 1) * 32], KTs[:, k, b * 8:(b + 1) * 8], Qbd[:, b, k, :], True, True, r=[KTs, Qbd], w=[ps])
                        ACT(et[0:8, 0:256], ps[0:8, 0:256], AF.Exp, r=[ps], w=[et], scale=0.125)
                        TTo("dve", et[0:8, 0:256].rearrange("p (a t) -> p a t", t=8), et[0:8, 0:256].rearrange("p (a t) -> p a t", t=8),
                            mats[0:8, 2:3, 0:8].to_broadcast([8, 32, 8]), ALU.mult, r=[et, mats], w=[et])
                        vsrc = vnew
                    ck("g2")
                    MM(aL[:, 0:256], onesb[0:n, :], et[0:n, 0:256], jp == 0, new, r=[onesb, et], w=[aL])
                    if jp == 0:
                        MM(aO[:, 0:256], zerosb[0:n, :], et[0:n, 0:256], True, False, r=[zerosb, et], w=[aO])
                    for k in range(8):
                        MM(aO[:, k * 32:(k + 1) * 32], vsrc[0:n, k * 128:(k + 1) * 128], et[0:n, k * 32:(k + 1) * 32], False, new and k == 7,
                           r=[vsrc, et], w=[aO])
                ck("g3")
                On5 = Ont[:, 0:256].rearrange("p (k m g t) -> p k m g t", k=8, m=2, g=2)
                finish_head(aO, aL, 256, OTf[:, :, b * 8:(b + 1) * 8].rearrange("p (k g) t -> p k g t", g=2),
                            On5[:, :, 0, :, :], On5[:, :, 1, :, :])
            if l == DEPTH - 1:
                attn_post(xt, m_, lam_init, j, y_s, new_out_buf())
            else:
                attn_post(xt, m_, lam_init, j, xb_s, xbuf_s)
            ck(f"atts{l}")

    try:
        for l in range(NA):
            if stage >= 1 + l and not _os.environ.get("ONLYS"):
                rwkv_layer(l)
        if stage >= 3:
            phase_b()
    except StopBuild:
        pass

    S.drain("sp")
    S.emit()
    return nc


_STAGE = 99


def kernel(x_prompt, x_sample, cache_k, cache_v, page_table, state_wkv, state_shift, c_prompt, c_sample,
           ada_w, ada_b, ln_g, ln_b, mu, w_rkvg, w0, w_decay1, w_decay2, a0, w_a1, w_a2, k_k, k_a, r_k,
           gn_g, gn_b, w_o_a, ada_kv_w, ada_kv_b, w_kv, w_qg, lam_qk, subln_g, w_o_b):
    f = lambda a: np.ascontiguousarray(np.asarray(a, dtype=np.float32))
    B, T, _ = x_prompt.shape
    DB, TS, _ = x_sample.shape
    NPHYS = cache_k.shape[0]
    NPG = page_table.shape[1]
    assert B == NCORE and DB == 16 * NCORE and TS == 8, (B, NCORE, DB)
    nc = build_program(T, NPG, NPHYS, stage=_STAGE)
    consts = make_consts(T, NPG * 128)
    shared = {
        "cache_k": f(cache_k).reshape(NPHYS * 128, 1024), "cache_v": f(cache_v).reshape(NPHYS * 128, 1024),
        "ada_w": f(ada_w), "ada_b": f(ada_b), "ln_g": f(ln_g), "ln_b": f(ln_b), "mu": f(mu),
        "w_rkvg": f(w_rkvg), "w0": f(w0), "w_decay1": f(w_decay1), "w_decay2": f(w_decay2), "a0": f(a0),
        "w_a1": f(w_a1), "w_a2": f(w_a2), "k_k": f(k_k), "k_a": f(k_a), "r_k": f(r_k).reshape(NA, E),
        "gn_g": f(gn_g), "gn_b": f(gn_b), "w_o_a": f(w_o_a), "ada_kv_w": f(ada_kv_w),
        "ada_kv_b": f(ada_kv_b).reshape(1, 2 * D), "w_kv": f(w_kv), "w_qg": f(w_qg),
        "lam_qk": f(lam_qk).reshape(2, 256), "subln_g": f(subln_g), "w_o_b": f(w_o_b),
        "c_mats": consts["mats"], "c_mask4": consts["mask4"], "c_red": consts["red"],
        "c_maskb": consts["maskb"], "c_maskb8": consts["maskb8"],
        "c_rope_p": consts["rope_p"], "c_rope_s": consts["rope_s"],
    }
    xpf, xsf = f(x_prompt), f(x_sample)
    swf, ssf = f(state_wkv), f(state_shift)
    cpf, csf = f(c_prompt), f(c_sample)
    pt = np.ascontiguousarray(np.asarray(page_table, dtype=np.int32))
    in_maps = []
    for c in range(NCORE):
        bs = slice(16 * c, 16 * (c + 1))
        m = dict(shared)
        m["xp"] = xpf[c]
        m["xs"] = xsf[bs].reshape(128, D)
        m["ptab"] = pt[bs].reshape(1, 16 * NPG)
        m["swkv"] = np.ascontiguousarray(swf[:, bs])
        m["sshift"] = np.ascontiguousarray(ssf[:, bs])
        m["cvec"] = np.concatenate([cpf[c:c + 1], csf[bs]], axis=0)
        in_maps.append(m)
    res = run_bass_kernel_spmd(nc, in_maps, core_ids=list(range(NCORE))).results
    g = lambda name: [np.asarray(r[name]) for r in res]
    if _os.environ.get("DBGIDX"):
        global _LAST_RES
        _LAST_RES = res
    y_prompt = np.stack(g("y_p"), 0).reshape(B, T, D)
    y_sample = np.concatenate(g("y_s"), 0).reshape(DB, TS, D)
    k_prompt = np.stack(g("k_p"), 0).reshape(B, T, 8, 128)
    v_prompt = np.stack(g("v_p"), 0).reshape(B, T, 8, 128)
    k_sample = np.concatenate(g("k_s"), 0).reshape(DB, TS, 8, 128)
    v_sample = np.concatenate(g("v_s"), 0).reshape(DB, TS, 8, 128)
    wkv_prompt = np.stack(g("wkv_p"), 1).reshape(NA, B, 32, 64, 64)
    shift_prompt = np.stack(g("shift_p"), 1).reshape(NA, B, D)
    wkv_sample = np.concatenate(g("wkv_s"), 1).reshape(NA, DB, 32, 64, 64)
    shift_sample = np.concatenate(g("shift_s"), 1).reshape(NA, DB, D)
    return (y_prompt.astype(np.float32), y_sample.astype(np.float32), k_prompt.astype(np.float32),
            v_prompt.astype(np.float32), k_sample.astype(np.float32), v_sample.astype(np.float32),
            wkv_prompt.astype(np.float32), shift_prompt.astype(np.float32), wkv_sample.astype(np.float32),
            shift_sample.astype(np.float32))
```

```python
import math
import numpy as np
import ml_dtypes
import concourse.bass as bass
import concourse.mybir as mybir
from concourse.bass_utils import run_bass_kernel_spmd

F32 = mybir.dt.float32
BF16 = mybir.dt.bfloat16
I32 = mybir.dt.int32
AF = mybir.ActivationFunctionType
ALU = mybir.AluOpType
AX = mybir.AxisListType

D = 1024
E = 2048
NCORE = 8
DEPTH = 4
NA = 2
ALPHA = (2 * DEPTH) ** 0.25
LN_EPS = 1e-5
GN_EPS = 64e-5
LWC = -math.exp(-0.5)
SEM_LIMIT = 30000


class Buf:
    __slots__ = ("w", "r", "const")

    def __init__(self, const=False):
        self.w = None
        self.r = []
        self.const = const


class Tile:
    def __init__(self, nc, name, shape, dtype, psum=False):
        if psum:
            self.t = nc.alloc_psum_tensor(name, list(shape), dtype)
        else:
            self.t = nc.alloc_sbuf_tensor(name, list(shape), dtype)
        self.b = Buf()
        self.shape = shape
        self.name = name

    def __getitem__(self, k):
        return self.t[k]


class Sched:
    ENGS = ("pe", "act", "dve", "pool", "sp")

    def __init__(self, nc, n_dma_sems=8):
        self.nc = nc
        self.prog = {e: [] for e in self.ENGS}
        self.sem = {}
        self.cnt = {}
        self.nsem = 0
        for e in ("pe", "act", "dve", "pool"):
            self._new_sem(e)
        self.seen = {e: {} for e in self.ENGS}
        self.dma_sems = {}
        self.dma_cnt = {}
        self.dma_rr = {}
        for q in ("sp", "act", "pool"):
            self.dma_sems[q] = [nc.alloc_semaphore(name=f"dma_{q}_{i}") for i in range(n_dma_sems)]
            self.dma_cnt[q] = [0] * n_dma_sems
            self.dma_rr[q] = 0
        self.n_instr = 0

    def _new_sem(self, e):
        self.sem[e] = self.nc.alloc_semaphore(name=f"sem_{e}_{self.nsem}")
        self.nsem += 1
        self.cnt[e] = 0

    def _need(self, e, toks):
        best = {}
        for tok in toks:
            if tok is None:
                continue
            s, v, src = tok
            if src == "pe" and e == "pe":
                continue
            if self.seen[e].get(id(s), 0) >= v:
                continue
            if id(s) not in best or best[id(s)][1] < v:
                best[id(s)] = (s, v)
        for s, v in best.values():
            self.seen[e][id(s)] = v
            self.prog[e].append(("wait", s, v))
            self.n_instr += 1

    @staticmethod
    def _deps(reads, writes):
        toks = []
        for b in reads:
            toks.append(b.w)
        for b in writes:
            toks.append(b.w)
            toks.extend(b.r)
        return toks

    @staticmethod
    def _commit(tok, reads, writes):
        for b in reads:
            if not b.const:
                b.r.append(tok)
        for b in writes:
            b.w = tok
            b.r = []

    def op(self, e, fn, r=(), w=()):
        r = [x.b if hasattr(x, "b") else x for x in r]
        w = [x.b if hasattr(x, "b") else x for x in w]
        self._need(e, self._deps(r, w))
        if self.cnt[e] >= SEM_LIMIT:
            self._new_sem(e)
        self.cnt[e] += 1
        tok = (self.sem[e], self.cnt[e], e)
        if e == "pe":
            self.last_pe = (self.sem[e], self.cnt[e])
        self.prog[e].append(("op", fn, self.sem[e]))
        self.n_instr += 1
        self._commit(tok, r, w)
        return tok

    def dma(self, q, fn, r=(), w=()):
        r = [x.b if hasattr(x, "b") else x for x in r]
        w = [x.b if hasattr(x, "b") else x for x in w]
        i = self.dma_rr[q]
        self.dma_rr[q] = (i + 1) % len(self.dma_sems[q])
        s = self.dma_sems[q][i]
        toks = self._deps(r, w)
        if self.dma_cnt[q][i] > 0:
            toks.append((s, self.dma_cnt[q][i], "dma"))
        self._need(q, toks)
        self.dma_cnt[q][i] += 16
        tok = (s, self.dma_cnt[q][i], "dma")
        self.prog[q].append(("dma", fn, s))
        self.n_instr += 1
        self._commit(tok, r, w)
        return tok

    def wait_all(self, e, bufs):
        self._need(e, [b.w for b in bufs])

    def _all_toks(self):
        toks = []
        for e in ("pe", "act", "dve", "pool"):
            if self.cnt[e] > 0:
                toks.append((self.sem[e], self.cnt[e], "x"))
        for q in self.dma_sems:
            for s, c in zip(self.dma_sems[q], self.dma_cnt[q]):
                if c > 0:
                    toks.append((s, c, "dma"))
        return toks

    def barrier(self):
        toks = self._all_toks()
        for e in self.ENGS:
            self._need(e, toks)

    def drain(self, e):
        self._need(e, self._all_toks())

    def emit(self):
        nc = self.nc
        with nc.Block() as block:
            def run(e):
                def body(eng):
                    for item in self.prog[e]:
                        if item[0] == "wait":
                            eng.wait_ge(item[1], item[2])
                        elif item[0] == "op":
                            item[1](eng).then_inc(item[2], 1)
                        else:
                            item[1](eng).then_inc(item[2], 16)
                return body
            block.tensor(run("pe"))
            block.scalar(run("act"))
            block.vector(run("dve"))
            block.gpsimd(run("pool"))
            block.sync(run("sp"))


class Rot:
    def __init__(self, items):
        self.items = items
        self.i = 0

    def next(self):
        x = self.items[self.i]
        self.i = (self.i + 1) % len(self.items)
        return x


def make_consts(T, PAST):
    c = {}
    i = np.arange(128)
    s, t = i[:, None], i[None, :]
    same8 = (s // 8) == (t // 8)
    ident = np.eye(128, dtype=np.float32)
    su = (s < t).astype(np.float32)
    ui = (s <= t).astype(np.float32)
    sl = (s > t).astype(np.float32)
    mats = np.stack([
        ident,
        su, ui, sl,
        LWC * ui,
        LWC * su,
        LWC * sl,
        LWC * ui * same8,
        LWC * su * same8,
        LWC * sl * same8,
    ], axis=1).astype(np.float32)
    c["mats"] = mats
    mask4 = np.concatenate([su, ui, su, ui], axis=1).astype(np.float32)
    c["mask4"] = mask4
    red = np.zeros((128, 17), np.float32)
    red[:, 0] = LWC
    for b in range(16):
        red[b * 8:(b + 1) * 8, 1 + b] = LWC
    c["red"] = red
    mb = np.where(s <= t, 0.0, -30000.0).astype(np.float32)
    c["maskb"] = np.concatenate([mb, mb, mb, mb], axis=1).astype(ml_dtypes.bfloat16)
    mb8 = mb[:8, :8]
    m8 = np.zeros((128, 16), np.float32)
    m8[:8, :8] = mb8
    m8[:8, 8:] = mb8
    c["maskb8"] = m8.astype(ml_dtypes.bfloat16)
    half = 8
    inv = (500000.0 ** (-np.arange(half, dtype=np.float32) * 2.0 / 16)).astype(np.float32)
    def tab(pos):
        ang = pos.astype(np.float32)[:, None] * inv[None, :]
        return np.concatenate([np.cos(ang), np.sin(ang)], axis=1).astype(np.float32)
    c["rope_p"] = tab(np.arange(T))
    c["rope_s"] = tab(PAST + (np.arange(128) % 8))
    return c


class StopBuild(Exception):
    pass


import os as _os
_STOPAT = _os.environ.get("STOPAT", "")


_DBG = [] if _os.environ.get("PEDBG") else None


def ck(name):
    if _STOPAT and name == _STOPAT:
        raise StopBuild()


def build_program(T, NPG, NPHYS, stage=99):
    NT = T // 128
    PAST = NPG * 128
    nc = bass.Bass("TRN2", target_bir_lowering=False)
    S = Sched(nc)

    def din(name, shape, dt=F32):
        return nc.dram_tensor(name, list(shape), dt, kind="ExternalInput").ap()

    def dout(name, shape, dt=F32):
        return nc.dram_tensor(name, list(shape), dt, kind="ExternalOutput").ap()

    def dscr(name, shape, dt=F32):
        return nc.dram_tensor(name, list(shape), dt, kind="Internal").ap()

    xp = din("xp", [T, D]); xs = din("xs", [128, D])
    cache_k = din("cache_k", [NPHYS * 128, 1024]); cache_v = din("cache_v", [NPHYS * 128, 1024])
    ptab = din("ptab", [1, 16 * NPG], I32)
    swkv = din("swkv", [NA, 16, 32, 64, 64]); sshift = din("sshift", [NA, 16, D])
    cvec = din("cvec", [17, D])
    ada_w = din("ada_w", [DEPTH, D, 3 * D]); ada_b = din("ada_b", [DEPTH, 3 * D])
    ln_g = din("ln_g", [DEPTH, D]); ln_b = din("ln_b", [DEPTH, D])
    mu = din("mu", [NA, 6, D])
    w_rkvg = din("w_rkvg", [NA, 4, D, E]); w0 = din("w0", [NA, E])
    w_d1 = din("w_decay1", [NA, D, 64]); w_d2 = din("w_decay2", [NA, 64, E])
    a0 = din("a0", [NA, E]); w_a1 = din("w_a1", [NA, D, 64]); w_a2 = din("w_a2", [NA, 64, E])
    k_k = din("k_k", [NA, E]); k_a = din("k_a", [NA, E]); r_k = din("r_k", [NA, E])
    gn_g = din("gn_g", [NA, E]); gn_b = din("gn_b", [NA, E])
    w_o_a = din("w_o_a", [NA, E, D])
    ada_kv_w = din("ada_kv_w", [D, 2 * D]); ada_kv_b = din("ada_kv_b", [1, 2 * D])
    w_kv = din("w_kv", [D, 2048]); w_qg = din("w_qg", [2, D, 4096])
    lam_qk = din("lam_qk", [2, 256]); subln_g = din("subln_g", [2, 128]); w_o_b = din("w_o_b", [2, E, D])
    c_mats = din("c_mats", [128, 10, 128]); c_mask4 = din("c_mask4", [128, 512]); c_red = din("c_red", [128, 17])
    c_maskb = din("c_maskb", [128, 512], BF16); c_maskb8 = din("c_maskb8", [128, 16], BF16)
    c_rope_p = din("c_rope_p", [T, 16]); c_rope_s = din("c_rope_s", [128, 16])
    y_p = dout("y_p", [T, D]); y_s = dout("y_s", [128, D])
    k_p = dout("k_p", [T, 1024]); v_p = dout("v_p", [T, 1024])
    k_s = dout("k_s", [128, 1024]); v_s = dout("v_s", [128, 1024])
    wkv_p = dout("wkv_p", [NA, 32, 64, 64]); shift_p = dout("shift_p", [NA, D])
    wkv_s = dout("wkv_s", [NA, 16, 32, 64, 64]); shift_s = dout("shift_s", [NA, 16, D])
    out_bufs = []
    xb_p = dscr("xb_p", [T, D]); xb_s = dscr("xb_s", [128, D])
    xbuf_p = Buf(); xbuf_s = Buf()
    wb_rkvg = dscr("wb_rkvg", [NA, 4, D, E], BF16); wb_oa = dscr("wb_oa", [NA, E, D], BF16)
    wb_d1 = dscr("wb_d1", [NA, D, 64], BF16); wb_d2 = dscr("wb_d2", [NA, 64, E], BF16)
    wb_a1 = dscr("wb_a1", [NA, D, 64], BF16); wb_a2 = dscr("wb_a2", [NA, 64, E], BF16)
    wb_kv = dscr("wb_kv", [D, 2048], BF16); wb_qg = dscr("wb_qg", [2, D, 4096], BF16)
    wb_ob = dscr("wb_ob", [2, E, D], BF16)
    wbufs = []

    def DMA(q, out_ap, in_ap, r=(), w=()):
        S.dma(q, lambda e: e.dma_start(out=out_ap, in_=in_ap), r=r, w=w)

    def conv(dst, src, rows, chunk=1024):
        for r0 in range(0, rows, chunk):
            r1 = min(rows, r0 + chunk)
            b_ = Buf()
            DMA("pool", dst[r0:r1], src[r0:r1], w=[b_])
            b_.const = True
            wbufs.append(b_)
    conv(wb_rkvg.rearrange("l n d e -> (l n d) e"), w_rkvg.rearrange("l n d e -> (l n d) e"), NA * 4 * D)
    conv(wb_oa.rearrange("l e d -> (l e) d"), w_o_a.rearrange("l e d -> (l e) d"), NA * E, 2048)
    conv(wb_d1.rearrange("l d e -> (l d) e"), w_d1.rearrange("l d e -> (l d) e"), NA * D, 2048)
    conv(wb_a1.rearrange("l d e -> (l d) e"), w_a1.rearrange("l d e -> (l d) e"), NA * D, 2048)
    conv(wb_d2.rearrange("l d e -> (l d) e"), w_d2.rearrange("l d e -> (l d) e"), NA * 64, 2048)
    conv(wb_a2.rearrange("l d e -> (l d) e"), w_a2.rearrange("l d e -> (l d) e"), NA * 64, 2048)
    conv(wb_kv, w_kv, D)
    conv(wb_qg.rearrange("l d e -> (l d) e"), w_qg.rearrange("l d e -> (l d) e"), 2 * D, 512)
    conv(wb_ob.rearrange("l e d -> (l e) d"), w_o_b.rearrange("l e d -> (l e) d"), 2 * E, 2048)

    mats = Tile(nc, "mats", [128, 10, 128], F32)
    S.dma("sp", lambda e: e.dma_start(out=mats[:, :, :], in_=c_mats), w=[mats])
    mask4 = Tile(nc, "mask4", [128, 512], F32)
    S.dma("sp", lambda e: e.dma_start(out=mask4[:, :], in_=c_mask4), w=[mask4])
    red = Tile(nc, "red", [128, 17], F32)
    S.dma("sp", lambda e: e.dma_start(out=red[:, :], in_=c_red), w=[red])
    maskb = Tile(nc, "maskb", [128, 512], BF16)
    S.dma("sp", lambda e: e.dma_start(out=maskb[:, :], in_=c_maskb), w=[maskb])
    maskb8 = Tile(nc, "maskb8", [128, 16], BF16)
    S.dma("sp", lambda e: e.dma_start(out=maskb8[:, :], in_=c_maskb8), w=[maskb8])
    identb = Tile(nc, "identb", [128, 128], BF16)
    S.op("dve", lambda e: e.tensor_copy(out=identb[:, :], in_=mats[:, 0, :]), r=[mats], w=[identb])
    for t_ in (mats, mask4, red, maskb, maskb8, identb):
        t_.b.const = True
    IDF = lambda n=128: mats[0:n, 0, 0:n]

    PB = [Tile(nc, f"pb{i}", [128, 512], F32, psum=True) for i in range(8)]
    pb_x0, pb_x1 = PB[0], PB[1]
    projR = Rot([PB[2], PB[3]])
    pb_tr = PB[4]
    scanR = Rot([PB[5], PB[6]])
    pb_y = PB[7]

    def ev_copy(eng, out_ap, in_ap, r, w):
        if eng == "act":
            S.op("act", lambda e: e.activation(out=out_ap, in_=in_ap, func=AF.Copy), r=r, w=w)
        else:
            S.op(eng, lambda e: e.tensor_copy(out=out_ap, in_=in_ap), r=r, w=w)

    def TTo(eng, out_ap, a, b_, op, r, w):
        S.op(eng, lambda e: e.tensor_tensor(out=out_ap, in0=a, in1=b_, op=op), r=r, w=w)

    def STT(eng, out_ap, a, sc, b_, op0, op1, r, w):
        S.op(eng, lambda e: e.scalar_tensor_tensor(out=out_ap, in0=a, scalar=sc, in1=b_, op0=op0, op1=op1), r=r, w=w)

    def TS(eng, out_ap, a, s1, s2, op0, op1, r, w):
        if op1 is None:
            S.op(eng, lambda e: e.tensor_scalar(out=out_ap, in0=a, scalar1=s1, scalar2=None, op0=op0), r=r, w=w)
        else:
            S.op(eng, lambda e: e.tensor_scalar(out=out_ap, in0=a, scalar1=s1, scalar2=s2, op0=op0, op1=op1), r=r, w=w)

    def ACT(out_ap, in_ap, func, r, w, scale=1.0, bias=0.0):
        S.op("act", lambda e: e.activation(out=out_ap, in_=in_ap, func=func, scale=scale, bias=bias), r=r, w=w)

    pe_inflight = {}

    def pe_guard(lhs_ap, bank):
        b0 = lhs_ap.base_partition(); k_ = lhs_ap.shape[0]
        rg = frozenset(range(b0 // 32, (b0 + k_ + 31) // 32))
        if len(rg) == 4:
            pe_inflight.clear()
            return
        hazard = any((not (rg & g2)) and bank in banks for g2, banks in pe_inflight.items())
        if hazard:
            if getattr(S, "last_pe", None) is not None:
                S.prog["pe"].append(("wait", S.last_pe[0], S.last_pe[1]))
                S.n_instr += 1
            pe_inflight.clear()
        pe_inflight.setdefault(rg, set()).add(bank)

    def MM(out_ap, lhsT, rhs, start, stop, r, w):
        if _DBG is not None:
            _DBG.append((w[0].name, start, stop, len(S.prog["pe"])))
        pe_guard(lhsT, w[0].name)
        S.op("pe", lambda e: e.matmul(out_ap, lhsT=lhsT, rhs=rhs, start=start, stop=stop), r=r, w=w)

    def TR(out_ap, in_ap, ident_ap, r, w):
        if _DBG is not None:
            _DBG.append((w[0].name, "T", "T", len(S.prog["pe"])))
        pe_guard(in_ap, w[0].name)
        S.op("pe", lambda e: e.transpose(out_ap, in_ap, ident_ap), r=r, w=w)

    evR = Rot(["act", "dve"])

    def RSQRT(t_, src_ap, dst_ap, eps, op):
        TS("dve", dst_ap, src_ap, eps, None, op, None, r=[t_], w=[t_])
        S.op("act", lambda e: e.activation(out=dst_ap, in_=dst_ap, func=AF.Sqrt), r=[t_], w=[t_])
        S.op("dve", lambda e: e.reciprocal(out=dst_ap, in_=dst_ap), r=[t_], w=[t_])

    csil = Tile(nc, "rows17", [17, D], F32)
    S.dma("sp", lambda e: e.dma_start(out=csil[:, :], in_=cvec), w=[csil])
    ACT(csil[:, :], csil[:, :], AF.Silu, r=[csil], w=[csil])
    scT = Tile(nc, "scT", [128, 8, 17], F32)
    for dc in range(8):
        pb = projR.next()
        TR(pb[:, 0:17], csil[0:17, dc * 128:(dc + 1) * 128], IDF(17), r=[csil, mats], w=[pb])
        ev_copy("dve", scT[:, dc, :], pb[:, 0:17], r=[pb], w=[scT])
    wpc_l = [Tile(nc, f"wpc{i}", [128, 8, 512], BF16) for i in range(3)]
    wpc = Rot(wpc_l)
    screp_mem = Tile(nc, "screp_mem", [128, 2, 8, 128], F32)

    class View:
        def __init__(self, base, fn):
            self.b = base.b
            self.fn = fn

        def __getitem__(self, k):
            return self.fn()[k]
    screp = {"p": View(screp_mem, lambda: screp_mem.t[:, 0, :, :]), "s": View(screp_mem, lambda: screp_mem.t[:, 1, :, :])}
    S.op("dve", lambda e: e.tensor_copy(out=screp["p"][:, :, :], in_=scT[:, :, 0:1].to_broadcast([128, 8, 128])),
         r=[scT], w=[screp["p"]])
    S.op("dve", lambda e: e.tensor_copy(out=screp["s"][:, :, :].rearrange("p c (b t) -> p c b t", t=8),
                                        in_=scT[:, :, 1:17].unsqueeze(3).to_broadcast([128, 8, 16, 8])),
         r=[scT], w=[screp["s"]])
    adaw_v = View(wpc_l[0], lambda: wpc_l[0].t[:, :, :].bitcast(F32))
    adab_t = Tile(nc, "adab", [128, 256], F32)

    def compute_mod(dst, grp, wsrc, bsrc, ncols, plus1_from):
        for cb in range(ncols // 256):
            wt = adaw_v; bt = adab_t
            csl = slice(cb * 256, (cb + 1) * 256)
            DMA("sp", wt[:, :, :], wsrc[:, csl].rearrange("(c p) n -> p c n", p=128), w=[wt])
            DMA("sp", bt[:, :], bsrc[:, csl].partition_broadcast(128), w=[bt])
            pb = projR.next()
            for dc in range(8):
                MM(pb[:, 0:256], screp[grp][:, dc, :], wt[:, dc, :], dc == 0, dc == 7, r=[screp[grp], wt], w=[pb])
            if cb * 256 >= plus1_from:
                STT("dve", dst[:, csl], pb[:, 0:256], 1.0, bt[:, :], ALU.add, ALU.add, r=[pb, bt], w=[dst])
            else:
                TTo("dve", dst[:, csl], pb[:, 0:256], bt[:, :], ALU.add, r=[pb, bt], w=[dst])

    x_t = Rot([Tile(nc, f"x{i}", [128, D], F32) for i in range(1)])
    h_t = Tile(nc, "h", [128, D], F32)
    lnv = {}
    lng_t = Tile(nc, "lng", [128, D], F32); lnb_t = Tile(nc, "lnb", [128, D], F32)
    st6 = Tile(nc, "st6", [128, 2, 6], F32); mv = Tile(nc, "mv", [128, 2], F32); rstd = Tile(nc, "rstd", [128, 1], F32)
    xn_t = h_t
    xo_t = Rot([Tile(nc, "xo0", [128, D], F32)])

    def load_ln(l):
        S.dma("sp", lambda e: e.dma_start(out=lng_t[:, :], in_=ln_g[l:l + 1, :].partition_broadcast(128)), w=[lng_t])
        S.dma("sp", lambda e: e.dma_start(out=lnb_t[:, :], in_=ln_b[l:l + 1, :].partition_broadcast(128)), w=[lnb_t])

    def residual_ln(xt, gate1_ap, gate_tile, dst_ap, dst_buf, also=None):
        for hf, pb in enumerate((pb_x0, pb_x1)):
            sl = slice(hf * 512, (hf + 1) * 512)
            TTo("dve", xn_t[:, sl], pb[:, :], gate1_ap[:, sl], ALU.mult, r=[pb, gate_tile], w=[xn_t])
        STT("dve", xn_t[:, :], xt[:, :], ALPHA, xn_t[:, :], ALU.mult, ALU.add, r=[xt, xn_t], w=[xn_t])
        for hf in range(2):
            S.op("dve", lambda e, hf=hf: e.bn_stats(out=st6[:, hf, :], in_=xn_t[:, hf * 512:(hf + 1) * 512]), r=[xn_t], w=[st6])
        S.op("dve", lambda e: e.bn_aggr(out=mv[:, :], in_=st6[:, :, :]), r=[st6], w=[mv])
        ev_copy("dve", rstd[:, :], mv[:, 1:2], r=[mv], w=[rstd])
        RSQRT(rstd, rstd[:, :], rstd[:, :], LN_EPS, ALU.add)
        TS("dve", xn_t[:, :], xn_t[:, :], mv[:, 0:1], rstd[:, 0:1], ALU.subtract, ALU.mult, r=[xn_t, mv, rstd], w=[xn_t])
        xo = xo_t.next()
        TTo("dve", xn_t[:, :], xn_t[:, :], lng_t[:, :], ALU.mult, r=[xn_t, lng_t], w=[xn_t])
        TTo("dve", xo[:, :], xn_t[:, :], lnb_t[:, :], ALU.add, r=[xn_t, lnb_t], w=[xo])
        if dst_ap is not None:
            S.dma("sp", lambda e: e.dma_start(out=dst_ap, in_=xo[:, :]), r=[xo], w=[dst_buf])
        return xo

    def new_out_buf():
        b = Buf()
        out_bufs.append(b)
        return b

    mod_one = Tile(nc, "mod_one", [128, 3 * D], F32)
    mod = {"p": mod_one, "s": mod_one}
    onesf = Tile(nc, "onesf", [128, 128], F32); onesb = Tile(nc, "onesb", [128, 128], BF16)
    S.op("pool", lambda e: e.memset(onesf[:, :], 1.0), w=[onesf])
    S.op("dve", lambda e: e.tensor_copy(out=onesb[:, :], in_=onesf[:, :]), r=[onesf], w=[onesb])
    zerosb = Tile(nc, "zerosb", [128, 128], BF16)
    S.op("pool", lambda e: e.memset(zerosb[:, :], 0.0), w=[zerosb])
    onesf.b.const = True; onesb.b.const = True; zerosb.b.const = True
    sbuf_mark = nc.sbuf_base
    vec = {nm: Tile(nc, f"vec_{nm}", [128, 512], F32) for nm in ("k_k", "k_a", "r_k", "gn_g", "gn_b", "w0", "a0")}
    vec_src = {"k_k": k_k, "k_a": k_a, "r_k": r_k, "gn_g": gn_g, "gn_b": gn_b, "w0": w0, "a0": a0}
    mu6 = csil
    muT = Tile(nc, "muT", [128, 6, 8], F32)
    hT = Tile(nc, "hT", [128, 8, 129], F32)
    dT = Tile(nc, "dT", [128, 8, 128], F32)
    xsn = [Tile(nc, f"xs{n}", [128, 8, 128], BF16) for n in range(6)]
    stsh = csil; stT = Tile(nc, "stT", [128, 8, 16], F32)
    wd1 = Tile(nc, "wd1", [128, 8, 64], BF16); wa1 = Tile(nc, "wa1", [128, 8, 64], BF16)
    wd2 = Tile(nc, "wd2", [64, 512], BF16); wa2 = Tile(nc, "wa2", [64, 512], BF16)
    t1T = Tile(nc, "t1T", [64, 128], BF16); t2T = Tile(nc, "t2T", [64, 128], BF16)
    wo_pc = Rot([Tile(nc, f"wopc{i}", [128, 4, D], BF16) for i in range(1)])
    W = {nm: Tile(nc, f"w_{nm}", [128, 512], F32) for nm in
         ("r", "k", "v", "sg", "sig", "asig", "kk", "km", "zb", "t0", "e0", "e1")}
    W["y"] = W["e1"]; W["y2"] = W["e0"]
    ss8 = Tile(nc, "ss8", [128, 8], F32); rn8 = Tile(nc, "rn8", [128, 8], F32); bs8 = Tile(nc, "bs8", [128, 8], F32)
    s1 = Tile(nc, "s1", [128, 8], F32); s2 = Tile(nc, "s2", [128, 8], F32); m8 = Tile(nc, "m8", [128, 8], F32); r8 = Tile(nc, "r8", [128, 8], F32)
    Bq = {nm: Tile(nc, f"b_{nm}", [128, 512], BF16) for nm in ("Rh", "Ah", "Bc", "Kc", "Kt", "Bt", "V", "out")}
    ART = Tile(nc, "ART", [128, 4, 256], BF16); BcT = Tile(nc, "BcT", [128, 4, 128], BF16); KcT = Tile(nc, "KcT", [128, 4, 128], BF16)
    outT = Tile(nc, "outT", [128, 4, 128], BF16)
    gam = Tile(nc, "gam", [128, 16, 17], F32)
    ST = Tile(nc, "ST", [128, 16, 64], F32); STb = Tile(nc, "STb", [128, 16, 64], BF16)
    QRES = []
    for qd in range(2):
        QRES.append({
            "MMq": [Tile(nc, f"MMq{qd}_{i}", [128, 512], BF16) for i in range(4)],
            "Xq": Rot([Tile(nc, f"Xq{qd}_{i}", [128, 4, 128], BF16) for i in range(2)]),
            "XTq": Rot([Tile(nc, f"XTq{qd}_{i}", [128, 4, 128], BF16) for i in range(2)]),
            "Pq": Tile(nc, f"Pq{qd}", [128, 4, 128], F32), "Pbq": Tile(nc, f"Pbq{qd}", [128, 4, 128], BF16),
            "Wbq": Tile(nc, f"Wbq{qd}", [128, 4, 64], BF16), "Ubq": Tile(nc, f"Ubq{qd}", [128, 4, 64], BF16),
            "R": Rot([PB[5], PB[6]]) if qd == 0 else Rot([PB[2], PB[3]]),
        })
    smp = Tile(nc, "smp", [8, 3, 512], BF16)
    s0ld = Tile(nc, "s0ld", [64, 8, 64], F32)
    sout = Tile(nc, "sout", [64, 8, 64], F32)
    src3 = Tile(nc, "src3", [128, 3, 512], BF16)

    def scan_chunk(n, cols, lvls, rows_fn, cbi, st_tile, stb_tile, gam_col, ybank, ART_, BcT_, KcT_):
        ART4 = ART_[:, :, :].rearrange("p s (q t) -> p s q t", q=2)

        def quad_body(quad, Rq):
            MMq, Xq, XTq, Pq, Pbq, Wbq, Ubq, scanR = (Rq[k_] for k_ in ("MMq", "Xq", "XTq", "Pq", "Pbq", "Wbq", "Ubq", "R"))
            heads = [quad * 4 + i for i in range(4)]
            hord = [(0, heads[0]), (2, heads[2]), (1, heads[1]), (3, heads[3])]
            X0 = Xq.next(); XT0 = XTq.next()
            pbT = scanR.next()
            for qi, hl in hord:
                sb, h2 = hl // 2, hl % 2
                pr = slice(h2 * 64, h2 * 64 + 64)
                MM(pbT[0:n, qi * 128:qi * 128 + n], ART4[pr, sb, 0, cols], BcT_[pr, sb, cols], True, True, r=[ART_, BcT_], w=[pbT])
            TTo("dve", XT0[0:n, :, 0:n], pbT[0:n, :].rearrange("p (q t) -> p q t", q=4)[:, :, 0:n],
                mats[0:n, 3:4, 0:n].to_broadcast([n, 4, n]), ALU.mult, r=[pbT, mats], w=[XT0])
            yield
            for qi, hl in hord:
                sb, h2 = hl // 2, hl % 2
                pr = slice(h2 * 64, h2 * 64 + 64)
                arh = ART4[pr, sb, :, cols]
                pb = scanR.next()
                MM(pb[0:n, 0:2 * n], BcT_[pr, sb, cols], arh, True, True, r=[BcT_, ART_], w=[pb])
                MM(pb[0:n, 2 * n:4 * n], KcT_[pr, sb, cols], arh, True, True, r=[KcT_, ART_], w=[pb])
                if n == 128:
                    TTo("dve", MMq[qi][:, :], pb[:, :], mask4[:, :], ALU.mult, r=[pb, mask4], w=[MMq[qi]])
                else:
                    m4 = mask4[0:n, :].rearrange("p (k t) -> p k t", k=4)[:, :, 0:n]
                    TTo("dve", MMq[qi][0:n, 0:4 * n].rearrange("p (k t) -> p k t", k=4),
                        pb[0:n, 0:4 * n].rearrange("p (k t) -> p k t", k=4), m4, ALU.mult, r=[pb, mask4], w=[MMq[qi]])
                if qi == 2:
                    yield
            for qi in range(4):
                ev_copy("act", X0[0:n, qi, 0:n], MMq[qi][0:n, 0:n], r=[MMq[qi]], w=[X0])
            TTo("dve", Pq[0:n, :, 0:n], X0[0:n, :, 0:n], mats[0:n, 0:1, 0:n].to_broadcast([n, 4, n]), ALU.add, r=[X0, mats], w=[Pq])
            ev_copy("act", Pbq[0:n, :, 0:n], Pq[0:n, :, 0:n], r=[Pq], w=[Pbq])
            yield
            Xc, XTc = X0, XT0
            for lv in range(1, lvls):
                last = lv == lvls - 1
                pbXT = scanR.next()
                for qi in range(4):
                    MM(pbXT[0:n, qi * 128:qi * 128 + n], Xc[0:n, qi, 0:n], XTc[0:n, qi, 0:n], True, True, r=[Xc, XTc], w=[pbXT])
                XTn = XTq.next()
                ev_copy("act", XTn[0:n, :, 0:n], pbXT[0:n, :].rearrange("p (q t) -> p q t", q=4)[:, :, 0:n], r=[pbXT], w=[XTn])
                if not last:
                    pbX = scanR.next()
                    for qi in range(4):
                        MM(pbX[0:n, qi * 128:qi * 128 + n], XTc[0:n, qi, 0:n], Xc[0:n, qi, 0:n], True, True, r=[Xc, XTc], w=[pbX])
                    Xn = Xq.next()
                    ev_copy("dve", Xn[0:n, :, 0:n], pbX[0:n, :].rearrange("p (q t) -> p q t", q=4)[:, :, 0:n], r=[pbX], w=[Xn])
                yield
                pbP = scanR.next()
                for qi in range(4):
                    MM(pbP[0:n, qi * 128:qi * 128 + n], XTn[0:n, qi, 0:n], Pbq[0:n, qi, 0:n], True, True, r=[XTn, Pbq], w=[pbP])
                TTo("dve", Pq[0:n, :, 0:n], Pq[0:n, :, 0:n], pbP[0:n, :].rearrange("p (q t) -> p q t", q=4)[:, :, 0:n], ALU.add, r=[Pq, pbP], w=[Pq])
                ev_copy("act", Pbq[0:n, :, 0:n], Pq[0:n, :, 0:n], r=[Pq], w=[Pbq])
                XTc = XTn
                if not last:
                    Xc = Xn
                yield
            pbW = scanR.next()
            for qi, hl in hord:
                sb, h2 = hl // 2, hl % 2
                pr = slice(h2 * 64, h2 * 64 + 64)
                hh = cbi * 4 + sb
                MM(pbW[0:n, qi * 64:(qi + 1) * 64], ART4[pr, sb, 0, cols], stb_tile[pr, hh, :], True, False, r=[ART_, stb_tile], w=[pbW])
                MM(pbW[0:n, qi * 64:(qi + 1) * 64], MMq[qi][0:n, 2 * n:3 * n], rows_fn("V", hl), False, True, r=[MMq[qi], src3, smp], w=[pbW])
            ev_copy("act", Wbq[0:n, :, :], pbW[0:n, 0:256].rearrange("p (q i) -> p q i", q=4), r=[pbW], w=[Wbq])
            yield
            pbU = scanR.next()
            for qi in range(4):
                MM(pbU[0:n, qi * 64:(qi + 1) * 64], Pbq[0:n, qi, 0:n], Wbq[0:n, qi, :], True, True, r=[Pbq, Wbq], w=[pbU])
            ev_copy("dve", Ubq[0:n, :, :], pbU[0:n, 0:256].rearrange("p (q i) -> p q i", q=4), r=[pbU], w=[Ubq])
            yield
            for qi, hl in hord:
                sb, h2 = hl // 2, hl % 2
                pr = slice(h2 * 64, h2 * 64 + 64)
                hh = cbi * 4 + sb
                yo = ybank[0:n, hl * 64:(hl + 1) * 64]
                MM(yo, ART4[pr, sb, 1, cols], stb_tile[pr, hh, :], True, False, r=[ART_, stb_tile], w=[ybank])
                MM(yo, MMq[qi][0:n, n:2 * n], Ubq[0:n, qi, :], False, False, r=[MMq[qi], Ubq], w=[ybank])
                MM(yo, MMq[qi][0:n, 3 * n:4 * n], rows_fn("V", hl), False, True, r=[MMq[qi], src3, smp], w=[ybank])
            pbS = scanR.next()
            for qi, hl in hord:
                sb, h2 = hl // 2, hl % 2
                pr = slice(h2 * 64, h2 * 64 + 64)
                so = pbS[pr, (qi // 2) * 64:(qi // 2) * 64 + 64]
                MM(so, rows_fn("Bt", hl), Ubq[0:n, qi, :], True, False, r=[src3, smp, Ubq], w=[pbS])
                MM(so, rows_fn("Kt", hl), rows_fn("V", hl), False, True, r=[src3, smp], w=[pbS])
            for sbl in range(2):
                sb = quad * 2 + sbl
                hh = cbi * 4 + sb
                STT("dve", st_tile[:, hh, :], st_tile[:, hh, :], gam_col(hh), pbS[:, sbl * 64:(sbl + 1) * 64],
                    ALU.mult, ALU.add, r=[st_tile, gam, pbS, stb_tile], w=[st_tile])

        for qd_ in range(2):
            Rq_ = dict(QRES[qd_]); Rq_["R"] = scanR
            for _ in quad_body(qd_, Rq_):
                pass
        ev_copy("act", stb_tile[:, cbi * 4:(cbi + 1) * 4, :], st_tile[:, cbi * 4:(cbi + 1) * 4, :], r=[st_tile], w=[stb_tile])

    def RED(out_ap, in_ap, r, w):
        S.op("dve", lambda e: e.tensor_reduce(out=out_ap, in_=in_ap, axis=AX.X, op=ALU.add), r=r, w=w)

    def MEMSET(ap, val, w):
        S.op("pool", lambda e: e.memset(ap, val), w=w)

    def v3(t_):
        return t_[:, :].rearrange("p (h j) -> p h j", j=64)

    def bc3(t_):
        return t_[:, :].unsqueeze(2).to_broadcast([128, 8, 64])

    def state_out(dst_fn, cbs):
        for cbo in cbs:
            pbo = scanR.next()
            for sb in range(4):
                TR(pbo[0:64, sb * 128:(sb + 1) * 128], ST[:, cbo * 4 + sb, :], IDF(), r=[ST, mats], w=[pbo])
            ev_copy("dve", sout[:, :, :], pbo[0:64, :].rearrange("i (h j) -> i h j", j=64), r=[pbo], w=[sout])
            DMA("sp", dst_fn(cbo), sout[:, :, :], r=[sout], w=[new_out_buf()])

    def rwkv_layer(l):
        compute_mod(mod["p"], "p", ada_w[l], ada_b[l:l + 1, :], 3 * D, D)
        load_ln(l)
        DMA("sp", mu6[0:6, :], mu[l], w=[mu6])
        for dc in range(8):
            pb = projR.next()
            TR(pb[:, 0:6], mu6[0:6, dc * 128:(dc + 1) * 128], IDF(6), r=[mu6, mats], w=[pb])
            ev_copy("dve", muT[:, :, dc], pb[:, 0:6], r=[pb], w=[muT])
        S.wait_all("sp", wbufs)
        DMA("sp", wd1[:, :, :], wb_d1[l].rearrange("(c p) n -> p c n", p=128), w=[wd1])
        DMA("sp", wa1[:, :, :], wb_a1[l].rearrange("(c p) n -> p c n", p=128), w=[wa1])
        MEMSET(ST[:, :, :], 0.0, [ST])
        MEMSET(STb[:, :, :], 0.0, [STb])
        MEMSET(hT[:, :, 0:1], 0.0, [hT])
        ck("setup")
        tiles = [("p", i) for i in range(NT)] + [("s", 0)]
        for grp, ti in tiles:
            smp_mode = grp == "s"
            if smp_mode:
                compute_mod(mod["s"], "s", ada_w[l], ada_b[l:l + 1, :], 3 * D, D)
            xt = x_t.next()
            if l == 0:
                DMA("sp", xt[:, :], xs if smp_mode else xp[ti * 128:(ti + 1) * 128, :], w=[xt])
            else:
                DMA("sp", xt[:, :], xb_s if smp_mode else xb_p[ti * 128:(ti + 1) * 128, :], r=[xbuf_s if smp_mode else xbuf_p], w=[xt])
            m_ = mod[grp]
            TTo("dve", h_t[:, :], xt[:, :], m_[:, D:2 * D], ALU.mult, r=[xt, m_], w=[h_t])
            TTo("dve", h_t[:, :], h_t[:, :], m_[:, 0:D], ALU.add, r=[h_t, m_], w=[h_t])
            if grp == "p" and ti == NT - 1:
                DMA("sp", shift_p[l:l + 1, :], h_t[127:128, :], r=[h_t], w=[new_out_buf()])
            if smp_mode:
                for b in range(16):
                    DMA("sp", shift_s[l, b:b + 1, :], h_t[b * 8 + 7:b * 8 + 8, :], r=[h_t], w=[new_out_buf()])
            ck("h")
            for half in range(2):
                pb = projR.next()
                for q in range(4):
                    dc = half * 4 + q
                    TR(pb[:, q * 128:(q + 1) * 128], h_t[:, dc * 128:(dc + 1) * 128], IDF(), r=[h_t, mats], w=[pb])
                ev_copy("act", hT[:, half * 4:(half + 1) * 4, 1:129], pb[:, :].rearrange("p (q t) -> p q t", q=4), r=[pb], w=[hT])
            if not smp_mode:
                TTo("dve", dT[:, :, :], hT[:, :, 0:128], hT[:, :, 1:129], ALU.subtract, r=[hT], w=[dT])
            else:
                DMA("sp", stsh[0:16, :], sshift[l], w=[stsh])
                for dc in range(8):
                    pb = projR.next()
                    TR(pb[:, 0:16], stsh[0:16, dc * 128:(dc + 1) * 128], IDF(16), r=[stsh, mats], w=[pb])
                    ev_copy("dve", stT[:, dc, :], pb[:, 0:16], r=[pb], w=[stT])
                h4 = hT[:, :, 1:129].rearrange("p c (b t) -> p c b t", t=8)
                d4 = dT[:, :, :].rearrange("p c (b t) -> p c b t", t=8)
                TTo("dve", d4[:, :, :, 1:8], h4[:, :, :, 0:7], h4[:, :, :, 1:8], ALU.subtract, r=[hT], w=[dT])
                TTo("dve", d4[:, :, :, 0:1], stT[:, :, :].unsqueeze(3), h4[:, :, :, 0:1], ALU.subtract, r=[hT, stT], w=[dT])
            for n_ in range(6):
                for dc in range(8):
                    STT("dve", xsn[n_][:, dc, :], dT[:, dc, :], muT[:, n_, dc:dc + 1], hT[:, dc, 1:129], ALU.mult, ALU.add,
                        r=[dT, muT, hT], w=[xsn[n_]])
            if not smp_mode:
                ev_copy("act", hT[:, :, 0:1], hT[:, :, 128:129], r=[hT], w=[hT])
            ck("xs")
            pb = projR.next()
            for dc in range(8):
                MM(pb[0:64, 0:128], wd1[:, dc, :], xsn[4][:, dc, :], dc == 0, dc == 7, r=[wd1, xsn[4]], w=[pb])
            ACT(t1T[:, :], pb[0:64, 0:128], AF.Tanh, r=[pb], w=[t1T])
            pb = projR.next()
            for dc in range(8):
                MM(pb[0:64, 0:128], wa1[:, dc, :], xsn[5][:, dc, :], dc == 0, dc == 7, r=[wa1, xsn[5]], w=[pb])
            ev_copy("act", t2T[:, :], pb[0:64, 0:128], r=[pb], w=[t2T])
            Lm = (7, 8, 9) if smp_mode else (4, 5, 6)
            for cb in range(4):
                cs = slice(cb * 512, (cb + 1) * 512)
                for nm, t_ in vec.items():
                    DMA("pool", t_[:, :], vec_src[nm][l:l + 1, cs].partition_broadcast(128), w=[t_])
                DMA("pool", wd2[:, :], wb_d2[l, :, cs], w=[wd2])
                DMA("pool", wa2[:, :], wb_a2[l, :, cs], w=[wa2])

                def proj(n_):
                    wt = wpc.next()
                    DMA("sp", wt[:, :, :], wb_rkvg[l, n_, :, cs].rearrange("(c p) n -> p c n", p=128), w=[wt])
                    pb_ = projR.next()
                    for dc in range(8):
                        MM(pb_[:, :], xsn[n_][:, dc, :], wt[:, dc, :], dc == 0, dc == 7, r=[xsn[n_], wt], w=[pb_])
                    return pb_
                pb = proj(0); ev_copy("act", W["r"][:, :], pb[:, :], r=[pb], w=[W["r"]])
                pb = proj(1); ev_copy("dve", W["k"][:, :], pb[:, :], r=[pb], w=[W["k"]])
                pb = proj(2); ev_copy("act", W["v"][:, :], pb[:, :], r=[pb], w=[W["v"]])
                pb = proj(3); ACT(W["sg"][:, :], pb[:, :], AF.Silu, r=[pb], w=[W["sg"]])
                pb = projR.next()
                MM(pb[:, :], t1T[:, :], wd2[:, :], True, True, r=[t1T, wd2], w=[pb])
                TTo("dve", W["t0"][:, :], pb[:, :], vec["w0"][:, :], ALU.add, r=[pb, vec["w0"]], w=[W["t0"]])
                ACT(W["sig"][:, :], W["t0"][:, :], AF.Sigmoid, r=[W["t0"]], w=[W["sig"]])
                pb = projR.next()
                MM(pb[:, :], t2T[:, :], wa2[:, :], True, True, r=[t2T, wa2], w=[pb])
                TTo("dve", W["t0"][:, :], pb[:, :], vec["a0"][:, :], ALU.add, r=[pb, vec["a0"]], w=[W["t0"]])
                ACT(W["asig"][:, :], W["t0"][:, :], AF.Sigmoid, r=[W["t0"]], w=[W["asig"]])
                ck("proj")
                pb = projR.next()
                ncol = 17 if smp_mode else 1
                for sb in range(4):
                    MM(pb[:, sb * 32:sb * 32 + ncol], W["sig"][:, sb * 128:(sb + 1) * 128], red[:, 0:ncol], True, True, r=[W["sig"], red], w=[pb])
                ACT(gam[:, cb * 4:(cb + 1) * 4, 0:ncol], pb[:, 0:128].rearrange("p (s c) -> p s c", s=4)[:, :, 0:ncol], AF.Exp, r=[pb], w=[gam])
                TTo("dve", W["kk"][:, :], W["k"][:, :], vec["k_k"][:, :], ALU.mult, r=[W["k"], vec["k_k"]], w=[W["kk"]])
                TTo("dve", W["t0"][:, :], W["kk"][:, :], W["kk"][:, :], ALU.mult, r=[W["kk"]], w=[W["t0"]])
                RED(ss8[:, :], v3(W["t0"]), r=[W["t0"]], w=[ss8])
                ev_copy("dve", rn8[:, :], ss8[:, :], r=[ss8], w=[rn8])
                RSQRT(rn8, rn8[:, :], rn8[:, :], 1e-24, ALU.max)
                TTo("dve", v3(W["kk"]), v3(W["kk"]), bc3(rn8), ALU.mult, r=[W["kk"], rn8], w=[W["kk"]])
                STT("dve", W["t0"][:, :], W["asig"][:, :], -1.0, vec["k_a"][:, :], ALU.add, ALU.mult, r=[W["asig"], vec["k_a"]], w=[W["t0"]])
                STT("dve", W["km"][:, :], W["t0"][:, :], 1.0, W["k"][:, :], ALU.add, ALU.mult, r=[W["t0"], W["k"]], w=[W["km"]])
                TTo("dve", W["zb"][:, :], W["kk"][:, :], W["asig"][:, :], ALU.mult, r=[W["kk"], W["asig"]], w=[W["zb"]])
                TTo("dve", W["t0"][:, :], W["r"][:, :], W["km"][:, :], ALU.mult, r=[W["r"], W["km"]], w=[W["t0"]])
                TTo("dve", W["t0"][:, :], W["t0"][:, :], vec["r_k"][:, :], ALU.mult, r=[W["t0"], vec["r_k"]], w=[W["t0"]])
                RED(bs8[:, :], v3(W["t0"]), r=[W["t0"]], w=[bs8])
                def cexp(dst_nm, mi, sc):
                    pb_ = projR.next()
                    MM(pb_[:, :], mats[:, mi, :], W["sig"][:, :], True, True, r=[mats, W["sig"]], w=[pb_])
                    ACT(W[dst_nm][:, :], pb_[:, :], AF.Exp, r=[pb_], w=[W[dst_nm]], scale=sc)
                cexp("e0", Lm[0], 1.0)
                TTo("dve", Bq["Rh"][:, :], W["r"][:, :], W["e0"][:, :], ALU.mult, r=[W["r"], W["e0"]], w=[Bq["Rh"]])
                cexp("e1", Lm[1], 1.0)
                STT("dve", Bq["Ah"][:, :], W["kk"][:, :], -1.0, W["e1"][:, :], ALU.mult, ALU.mult, r=[W["kk"], W["e1"]], w=[Bq["Ah"]])
                cexp("e0", Lm[0], -1.0)
                TTo("dve", Bq["Bc"][:, :], W["zb"][:, :], W["e0"][:, :], ALU.mult, r=[W["zb"], W["e0"]], w=[Bq["Bc"]])
                TTo("dve", Bq["Kc"][:, :], W["km"][:, :], W["e0"][:, :], ALU.mult, r=[W["km"], W["e0"]], w=[Bq["Kc"]])
                cexp("e1", Lm[2], 1.0)
                TTo("dve", src3[:, 0, :], W["km"][:, :], W["e1"][:, :], ALU.mult, r=[W["km"], W["e1"]], w=[src3])
                TTo("dve", src3[:, 1, :], W["zb"][:, :], W["e1"][:, :], ALU.mult, r=[W["zb"], W["e1"]], w=[src3])
                ev_copy("act", src3[:, 2, :], W["v"][:, :], r=[W["v"]], w=[src3])
                ck("elem")
                pbb = pb_tr.t[:, :].bitcast(BF16)
                for qn, nm in enumerate(("Ah", "Rh", "Bc", "Kc")):
                    for sb in range(4):
                        TR(pbb[:, sb * 128:(sb + 1) * 128], Bq[nm][:, sb * 128:(sb + 1) * 128], identb[:, :], r=[Bq[nm], identb], w=[pb_tr])
                    dst_t = ART if qn < 2 else (BcT if qn == 2 else KcT)
                    lo = 128 if qn == 1 else 0
                    ev_copy(evR.next(), dst_t[:, :, lo:lo + 128], pbb[:, 0:512].rearrange("p (s t) -> p s t", s=4), r=[pb_tr], w=[dst_t])
                ck("tr")
                rows = {"Kt": 0, "Bt": 1, "V": 2}
                if not smp_mode:
                    scan_chunk(128, slice(0, 128), 7, lambda nm, hl: src3[:, rows[nm], hl * 64:(hl + 1) * 64], cb, ST, STb,
                               lambda hh: gam[:, hh, 0:1], pb_y, ART, BcT, KcT)
                    ev_copy("act", W["y"][:, :], pb_y[:, :], r=[pb_y], w=[W["y"]])
                else:
                    for b in range(16):
                        DMA("sp", smp[:, :, :], src3[b * 8:(b + 1) * 8, :, :], r=[src3], w=[smp])
                        DMA("sp", s0ld[:, :, :], swkv[l, b, cb * 8:(cb + 1) * 8].rearrange("h i j -> i h j"), w=[s0ld])
                        pbs = scanR.next()
                        for sb in range(4):
                            TR(pbs[:, sb * 64:(sb + 1) * 64], s0ld[:, 2 * sb:2 * sb + 2, :].rearrange("i h j -> i (h j)"),
                               IDF(64), r=[s0ld, mats], w=[pbs])
                        ev_copy("dve", ST[:, cb * 4:(cb + 1) * 4, :], pbs[:, 0:256].rearrange("p (s i) -> p s i", s=4), r=[pbs], w=[ST])
                        ev_copy("act", STb[:, cb * 4:(cb + 1) * 4, :], ST[:, cb * 4:(cb + 1) * 4, :], r=[ST], w=[STb])
                        scan_chunk(8, slice(b * 8, (b + 1) * 8), 3, lambda nm, hl: smp[0:8, rows[nm], hl * 64:(hl + 1) * 64], cb, ST, STb,
                                   lambda hh, b=b: gam[:, hh, 1 + b:2 + b], pb_y, ART, BcT, KcT)
                        ev_copy("act", W["y2"][0:8, :], pb_y[0:8, :], r=[pb_y], w=[W["y2"]])
                        DMA("sp", W["y"][b * 8:(b + 1) * 8, :], W["y2"][0:8, :], r=[W["y2"]], w=[W["y"]])
                        state_out(lambda c_, b=b: wkv_s[l, b, c_ * 8:(c_ + 1) * 8].rearrange("h i j -> i h j"), [cb])
                ck("scan")
                TTo("dve", W["y2"][:, :], W["y"][:, :], W["y"][:, :], ALU.mult, r=[W["y"]], w=[W["y2"]])
                RED(s1[:, :], v3(W["y"]), r=[W["y"]], w=[s1])
                RED(s2[:, :], v3(W["y2"]), r=[W["y2"]], w=[s2])
                TS("dve", m8[:, :], s1[:, :], 1.0 / 64, None, ALU.mult, None, r=[s1], w=[m8])
                TTo("dve", r8[:, :], m8[:, :], m8[:, :], ALU.mult, r=[m8], w=[r8])
                STT("dve", r8[:, :], s2[:, :], 1.0 / 64, r8[:, :], ALU.mult, ALU.subtract, r=[s2, r8], w=[r8])
                RSQRT(r8, r8[:, :], r8[:, :], GN_EPS, ALU.add)
                TTo("dve", v3(W["y"]), v3(W["y"]), bc3(m8), ALU.subtract, r=[W["y"], m8], w=[W["y"]])
                TTo("dve", v3(W["y"]), v3(W["y"]), bc3(r8), ALU.mult, r=[W["y"], r8], w=[W["y"]])
                TTo("dve", W["y"][:, :], W["y"][:, :], vec["gn_g"][:, :], ALU.mult, r=[W["y"], vec["gn_g"]], w=[W["y"]])
                TTo("dve", W["y"][:, :], W["y"][:, :], vec["gn_b"][:, :], ALU.add, r=[W["y"], vec["gn_b"]], w=[W["y"]])
                TTo("dve", v3(W["t0"]), v3(W["v"]), bc3(bs8), ALU.mult, r=[W["v"], bs8], w=[W["t0"]])
                TTo("dve", W["y"][:, :], W["y"][:, :], W["t0"][:, :], ALU.add, r=[W["y"], W["t0"]], w=[W["y"]])
                TTo("dve", Bq["out"][:, :], W["y"][:, :], W["sg"][:, :], ALU.mult, r=[W["y"], W["sg"]], w=[Bq["out"]])
                for sb in range(4):
                    TR(pbb[:, sb * 128:(sb + 1) * 128], Bq["out"][:, sb * 128:(sb + 1) * 128], identb[:, :], r=[Bq["out"], identb], w=[pb_tr])
                ev_copy(evR.next(), outT[:, :, :], pbb[:, 0:512].rearrange("p (s t) -> p s t", s=4), r=[pb_tr], w=[outT])
                wo = wo_pc.next()
                DMA("sp", wo[:, :, :], wb_oa[l, cb * 512:(cb + 1) * 512, :].rearrange("(c p) n -> p c n", p=128), w=[wo])
                for sb in range(4):
                    for hf, pbx in enumerate((pb_x0, pb_x1)):
                        MM(pbx[:, :], outT[:, sb, :], wo[:, sb, hf * 512:(hf + 1) * 512], cb == 0 and sb == 0, cb == 3 and sb == 3,
                           r=[outT, wo], w=[pbx])
            ck("cbs")
            if not smp_mode:
                residual_ln(xt, m_[:, 2 * D:3 * D], m_, xb_p[ti * 128:(ti + 1) * 128, :], xbuf_p)
            else:
                residual_ln(xt, m_[:, 2 * D:3 * D], m_, xb_s, xbuf_s)
            ck("tile")
            if grp == "p" and ti == NT - 1:
                state_out(lambda c_: wkv_p[l, c_ * 8:(c_ + 1) * 8].rearrange("h i j -> i h j"), [0, 1, 2, 3])

    def phase_b():
        S.barrier()
        nc.sbuf_base = sbuf_mark
        QT = Tile(nc, "QT", [128, 16, 128], BF16)
        ETr = Rot([Tile(nc, f"ET{i}", [128, 512], BF16) for i in range(2)])
        OTf = Tile(nc, "OTf", [128, 16, 128], F32)
        sqt = Tile(nc, "sqt", [128, 16, 128], F32)
        OTb = Tile(nc, "OTb", [128, 16, 128], BF16)
        SG = Tile(nc, "SG", [128, 2048], F32)
        Wq = Rot([Tile(nc, f"Wq{i}", [128, 512], F32) for i in range(2)])
        hTb = Tile(nc, "hTb", [128, 8, 128], BF16)
        rLt = Tile(nc, "rLt", [128, 512], F32); Ont = Tile(nc, "Ont", [128, 512], F32)
        wo_bR = Rot([Tile(nc, f"wo_b{i}", [128, 2, D], BF16) for i in range(2)])
        rtab = Tile(nc, "rtab", [128, 2, 8], F32)
        rp = [Tile(nc, f"rp{i}", [128, 8, 8], F32) for i in range(4)]
        lq = Tile(nc, "lq", [128, 256], F32); lq2 = Tile(nc, "lq2", [128, 2, 64], F32)
        le = Tile(nc, "le", [128, 2], F32); neglam = Tile(nc, "neglam", [128, 1], F32)
        subg = Tile(nc, "subg", [128, 1], F32)
        accO = Rot([PB[7], PB[4]]); accL = Rot([PB[2], PB[3]]); xR = Rot([PB[0], PB[1]])
        mark2 = nc.sbuf_base

        def load_x(xt, src_ap, buf):
            DMA("sp", xt[:, :], src_ap, r=[buf], w=[xt])

        def modulate(xt, m_):
            TTo("dve", h_t[:, :], xt[:, :], m_[:, D:2 * D], ALU.mult, r=[xt, m_], w=[h_t])
            TTo("dve", h_t[:, :], h_t[:, :], m_[:, 0:D], ALU.add, r=[h_t, m_], w=[h_t])

        def make_hT():
            for half in range(2):
                pb = projR.next()
                for q in range(4):
                    dc = half * 4 + q
                    TR(pb[:, q * 128:(q + 1) * 128], h_t[:, dc * 128:(dc + 1) * 128], IDF(), r=[h_t, mats], w=[pb])
                ev_copy("act", hTb[:, half * 4:(half + 1) * 4, :], pb[:, :].rearrange("p (q t) -> p q t", q=4), r=[pb], w=[hTb])

        def projB(wdram):
            wt = wpc.next()
            DMA("sp", wt[:, :, :], wdram.rearrange("(c p) n -> p c n", p=128), w=[wt])
            pb_ = projR.next()
            for dc in range(8):
                MM(pb_[:, :], hTb[:, dc, :], wt[:, dc, :], dc == 0, dc == 7, r=[hTb, wt], w=[pb_])
            return pb_

        def rope(Wt):
            v = Wt[:, :].rearrange("p (g d) -> p g d", d=64)
            x1, x2 = v[:, :, 0:8], v[:, :, 8:16]
            cos = rtab[:, 0:1, :].to_broadcast([128, 8, 8]); sin = rtab[:, 1:2, :].to_broadcast([128, 8, 8])
            TTo("dve", rp[0][:, :, :], x1, cos, ALU.mult, r=[Wt, rtab], w=[rp[0]])
            TTo("dve", rp[1][:, :, :], x2, sin, ALU.mult, r=[Wt, rtab], w=[rp[1]])
            TTo("dve", rp[2][:, :, :], x1, sin, ALU.mult, r=[Wt, rtab], w=[rp[2]])
            TTo("dve", rp[3][:, :, :], x2, cos, ALU.mult, r=[Wt, rtab], w=[rp[3]])
            TTo("dve", x1, rp[0][:, :, :], rp[1][:, :, :], ALU.subtract, r=[rp[0], rp[1]], w=[Wt])
            TTo("dve", x2, rp[3][:, :, :], rp[2][:, :, :], ALU.add, r=[rp[3], rp[2]], w=[Wt])

        def kv_tile(xt, rows_k, rows_v, KT_dst_fn, V_dst_fn):
            make_hT()
            for c in range(4):
                pb = projB(wb_kv[:, c * 512:(c + 1) * 512])
                Wt = Wq.next()
                ev_copy("act", Wt[:, :], pb[:, :], r=[pb], w=[Wt])
                if c < 2:
                    rope(Wt)
                    DMA("sp", rows_k[:, c * 512:(c + 1) * 512], Wt[:, :], r=[Wt], w=[new_out_buf()])
                    pbt = projR.next()
                    for q in range(4):
                        TR(pbt[:, q * 128:(q + 1) * 128], Wt[:, q * 128:(q + 1) * 128], IDF(), r=[Wt, mats], w=[pbt])
                    kd, kb = KT_dst_fn(c)
                    ev_copy("dve", kd, pbt[:, :].rearrange("p (q t) -> p q t", q=4), r=[pbt], w=[kb])
                else:
                    DMA("sp", rows_v[:, (c - 2) * 512:(c - 1) * 512], Wt[:, :], r=[Wt], w=[new_out_buf()])
                    vd, vb = V_dst_fn(c - 2)
                    ev_copy("dve", vd, Wt[:, :], r=[Wt], w=[vb])

        def qg_tile(j):
            make_hT()
            for c in range(4):
                pb = projB(wb_qg[j, :, c * 512:(c + 1) * 512])
                Wt = Wq.next()
                ev_copy("act", Wt[:, :], pb[:, :], r=[pb], w=[Wt])
                rope(Wt)
                pbt = projR.next()
                for q in range(4):
                    TR(pbt[:, q * 128:(q + 1) * 128], Wt[:, q * 128:(q + 1) * 128], IDF(), r=[Wt, mats], w=[pbt])
                ev_copy("dve", QT[:, c * 4:(c + 1) * 4, :], pbt[:, :].rearrange("p (q t) -> p q t", q=4), r=[pbt], w=[QT])
            for c in range(4):
                pb = projB(wb_qg[j, :, 2048 + c * 512:2048 + (c + 1) * 512])
                ACT(SG[:, c * 512:(c + 1) * 512], pb[:, :], AF.Silu, r=[pb], w=[SG])

        def layer_consts(l):
            j = l - NA
            lam_init = 0.8 - 0.6 * math.exp(-0.3 * l)
            load_ln(l)
            DMA("sp", lq[:, :], lam_qk[j:j + 1, :].partition_broadcast(128), w=[lq])
            TTo("dve", lq2[:, 0, :], lq[:, 0:64], lq[:, 64:128], ALU.mult, r=[lq], w=[lq2])
            TTo("dve", lq2[:, 1, :], lq[:, 128:192], lq[:, 192:256], ALU.mult, r=[lq], w=[lq2])
            RED(le[:, :], lq2[:, :, :], r=[lq2], w=[le])
            ACT(le[:, :], le[:, :], AF.Exp, r=[le], w=[le])
            TTo("dve", neglam[:, :], le[:, 1:2], le[:, 0:1], ALU.subtract, r=[le], w=[neglam])
            TS("dve", neglam[:, :], neglam[:, :], -lam_init, None, ALU.add, None, r=[neglam], w=[neglam])
            DMA("sp", subg[:, :], subln_g[j:j + 1, :].rearrange("o v -> v o"), w=[subg])
            return lam_init

        def attn_post(xt, m_, lam_init, j, dst_ap, dst_buf):
            OT2 = OTf[:, :, :].rearrange("p a t -> p (a t)")
            sq2 = sqt[:, :, :].rearrange("p a t -> p (a t)")
            S.op("act", lambda e: e.activation(out=sq2, in_=OT2, func=AF.Square), r=[OTf], w=[sqt])
            for c in range(4):
                cs = slice(c * 512, (c + 1) * 512)
                pb = projR.next()
                MM(pb[:, :], onesf[:, :], sq2[:, cs], True, True, r=[onesf, sqt], w=[pb])
                TS("dve", sq2[:, cs], pb[:, :], 1.0 / 128, LN_EPS, ALU.mult, ALU.add, r=[pb], w=[sqt])
                S.op("act", lambda e, cs=cs: e.activation(out=sq2[:, cs], in_=sq2[:, cs], func=AF.Sqrt), r=[sqt], w=[sqt])
                S.op("dve", lambda e, cs=cs: e.reciprocal(out=sq2[:, cs], in_=sq2[:, cs]), r=[sqt], w=[sqt])
                TTo("dve", OT2[:, cs], OT2[:, cs], sq2[:, cs], ALU.mult, r=[OTf, sqt], w=[OTf])
                pbt = projR.next()
                for q in range(4):
                    kg = c * 4 + q
                    TR(pbt[:, q * 128:(q + 1) * 128], SG[:, kg * 128:(kg + 1) * 128], IDF(), r=[SG, mats], w=[pbt])
                TTo("dve", OT2[:, cs], OT2[:, cs], pbt[:, :], ALU.mult, r=[OTf, pbt], w=[OTf])
            TS("dve", OTb[:, :, :], OTf[:, :, :], subg[:, 0:1], 1.0 - lam_init, ALU.mult, ALU.mult, r=[OTf, subg], w=[OTb])
            for piece in range(8):
                wo_b = wo_bR.next()
                DMA("sp", wo_b[:, :, :], wb_ob[j, piece * 256:(piece + 1) * 256, :].rearrange("(c p) n -> p c n", p=128), w=[wo_b])
                for sb in range(2):
                    for hf, pbx in enumerate((pb_x0, pb_x1)):
                        MM(pbx[:, :], OTb[:, piece * 2 + sb, :], wo_b[:, sb, hf * 512:(hf + 1) * 512],
                           piece == 0 and sb == 0, piece == 7 and sb == 1, r=[OTb, wo_b], w=[pbx])
            return residual_ln(xt, m_[:, 2 * D:3 * D], m_, dst_ap, dst_buf)

        def finish_head(aO, aL, ncol, out_ap, in_m0, in_m1):
            S.op("dve", lambda e: e.reciprocal(out=rLt[:, 0:ncol], in_=aL[:, 0:ncol]), r=[aL], w=[rLt])
            TTo("dve", Ont[:, 0:ncol], aO[:, 0:ncol], rLt[:, 0:ncol], ALU.mult, r=[aO, rLt], w=[Ont])
            STT("dve", out_ap, in_m1, neglam[:, 0:1], in_m0, ALU.mult, ALU.add, r=[Ont, neglam], w=[OTf])

        KT = Tile(nc, "KT", [128, 8, T], BF16)
        QTbd = Tile(nc, "QTbd", [128, 8, 512], BF16)
        MEMSET(QTbd[:, :, :], 0.0, [QTbd])
        sc4 = Rot([PB[5], PB[6], PB[0], PB[1]])
        ET4 = Rot(ETr.items + [Tile(nc, f"ETp{i}", [128, 512], BF16) for i in range(2)])
        Vb = Tile(nc, "Vb", [128, NT, 1024], BF16)
        m_ = mod["p"]
        compute_mod(m_, "p", ada_kv_w, ada_kv_b, 2 * D, D)
        for ti in range(0 if _os.environ.get("ONLYS") else NT):
            rs = slice(ti * 128, (ti + 1) * 128)
            xt = x_t.next()
            load_x(xt, xb_p[rs, :], xbuf_p)
            modulate(xt, m_)
            DMA("sp", rtab[:, :, :], c_rope_p[rs, :].rearrange("p (a c) -> p a c", a=2), w=[rtab])
            kv_tile(xt, k_p[rs, :], v_p[rs, :],
                    lambda c, ti=ti: (KT[:, c * 4:(c + 1) * 4, ti * 128:(ti + 1) * 128], KT),
                    lambda c, ti=ti: (Vb[:, ti, c * 512:(c + 1) * 512], Vb))
        ck("kvp")
        for l in (NA, NA + 1):
            if stage < l + 1 + 1 or _os.environ.get("ONLYS"):
                break
            j = l - NA
            compute_mod(m_, "p", ada_w[l], ada_b[l:l + 1, :], 3 * D, D)
            lam_init = layer_consts(l)
            for qt in range(NT):
                rs = slice(qt * 128, (qt + 1) * 128)
                xt = x_t.next()
                load_x(xt, xb_p[rs, :], xbuf_p)
                modulate(xt, m_)
                DMA("sp", rtab[:, :, :], c_rope_p[rs, :].rearrange("p (a c) -> p a c", a=2), w=[rtab])
                qg_tile(j)
                for m in range(2):
                    pr = slice(m * 64, (m + 1) * 64)
                    ev_copy(evR.next(), QTbd[pr, :, m * 256:(m + 1) * 256].rearrange("p k (g t) -> p k g t", g=2),
                            QT[pr, :, :].rearrange("p (k g) t -> p k g t", g=2), r=[QT], w=[QTbd])
                steps = [(k, kt) for k in range(8) for kt in range(qt + 1)]
                LOOK = 2
                ets = {}; acc = {}
                for i in range(len(steps) + LOOK):
                    if i < len(steps):
                        k, kt = steps[i]
                        ps = sc4.next()
                        diag = kt == qt
                        if diag:
                            MM(ps[:, :], identb[:, :], maskb[:, :], True, False, r=[identb, maskb], w=[ps])
                        MM(ps[:, :], KT[:, k, kt * 128:(kt + 1) * 128], QTbd[:, k, :], not diag, True, r=[KT, QTbd], w=[ps])
                        et = ET4.next()
                        ACT(et[:, :], ps[:, :], AF.Exp, r=[ps], w=[et], scale=0.125)
                        ets[i] = et
                    jj = i - LOOK
                    if jj >= 0:
                        k, kt = steps[jj]
                        if kt == 0:
                            acc[k] = (accO.next(), accL.next())
                        aO, aL = acc[k]
                        et = ets.pop(jj)
                        MM(aO[:, :], Vb[:, kt, k * 128:(k + 1) * 128], et[:, :], kt == 0, kt == qt, r=[Vb, et], w=[aO])
                        MM(aL[:, :], onesb[:, :], et[:, :], kt == 0, kt == qt, r=[onesb, et], w=[aL])
                        if kt == qt:
                            finish_head(aO, aL, 512, OTf[:, 2 * k:2 * k + 2, :],
                                        Ont[:, 0:256].rearrange("p (g t) -> p g t", g=2),
                                        Ont[:, 256:512].rearrange("p (g t) -> p g t", g=2))
                if l == DEPTH - 1:
                    attn_post(xt, m_, lam_init, j, y_p[rs, :], new_out_buf())
                else:
                    attn_post(xt, m_, lam_init, j, xb_p[rs, :], xbuf_p)
            ck(f"attp{l}")

        S.barrier()
        nc.sbuf_base = mark2
        KTs = Tile(nc, "KTs", [128, 8, 128], BF16)
        Vs = Tile(nc, "Vs", [128, 1024], BF16)
        vnewR = Rot([Tile(nc, f"vnew{i}", [8, 1024], BF16) for i in range(2)])
        ETs = Rot([Tile(nc, f"ETs{i}", [128, 256], BF16) for i in range(3)])
        Qbd = Tile(nc, "Qbd", [128, 16, 8, 32], BF16)
        KpgR = Rot([Tile(nc, f"Kpg{i}", [128, 1024], F32) for i in range(3)])
        VpgR = Rot([Tile(nc, f"Vpg{i}", [128, 1024], F32) for i in range(3)])
        KTbR = Rot([Tile(nc, f"KTb{i}", [128, 8, 128], BF16) for i in range(3)])
        VpbR = Rot([Tile(nc, f"Vpb{i}", [128, 1024], BF16) for i in range(4)])
        idxR = Rot([Tile(nc, f"idx{i}", [128, 1], I32) for i in range(6)])
        NPT = 16 * NPG
        pti = Tile(nc, "pti", [128, NPT], I32); ptf = Tile(nc, "ptf", [128, NPT], F32)
        iot = Tile(nc, "iot", [128, 1], F32); idxall = Tile(nc, "idxall", [128, NPT], I32)
        DMA("sp", pti[:, :], ptab.partition_broadcast(128), w=[pti])
        S.op("pool", lambda e: e.iota(iot[:, :], pattern=[[0, 1]], base=0, channel_multiplier=1,
                                      allow_small_or_imprecise_dtypes=True), w=[iot])
        ev_copy("dve", ptf[:, :], pti[:, :], r=[pti], w=[ptf])
        STT("dve", ptf[:, :], ptf[:, :], 128.0, iot[:, 0:1].to_broadcast([128, NPT]), ALU.mult, ALU.add, r=[ptf, iot], w=[ptf])
        ev_copy("dve", idxall[:, :], ptf[:, :], r=[ptf], w=[idxall])
        MEMSET(Qbd[:, :, :, :], 0.0, [Qbd])
        if _os.environ.get("DBGIDX"):
            dbg_idx = nc.dram_tensor("dbg_idx", [128, NPT], I32, kind="ExternalOutput").ap()
            DMA("sp", dbg_idx, idxall[:, :], r=[idxall], w=[new_out_buf()])

        m_ = mod["s"]
        compute_mod(m_, "s", ada_kv_w, ada_kv_b, 2 * D, D)
        xt = x_t.next()
        load_x(xt, xb_s, xbuf_s)
        modulate(xt, m_)
        DMA("sp", rtab[:, :, :], c_rope_s.rearrange("p (a c) -> p a c", a=2), w=[rtab])
        kv_tile(xt, k_s, v_s,
                lambda c: (KTs[:, c * 4:(c + 1) * 4, :], KTs),
                lambda c: (Vs[:, c * 512:(c + 1) * 512], Vs))
        ck("kvs")
        for l in (NA, NA + 1):
            if stage < l + 1 + 1:
                break
            j = l - NA
            compute_mod(m_, "s", ada_w[l], ada_b[l:l + 1, :], 3 * D, D)
            lam_init = layer_consts(l)
            xt = x_t.next()
            load_x(xt, xb_s, xbuf_s)
            modulate(xt, m_)
            qg_tile(j)
            for b in range(16):
                for m in range(2):
                    pr = slice(m * 64, (m + 1) * 64)
                    ev_copy("dve", Qbd[pr, b, :, m * 16:(m + 1) * 16].rearrange("p k (g t) -> p k g t", g=2),
                            QT[pr, :, b * 8:(b + 1) * 8].rearrange("p (k g) t -> p k g t", g=2), r=[QT], w=[Qbd])
            ck("g0")
            steps = [(b, jp) for b in range(16) for jp in range(NPG + 1)]
            NS = len(steps)

            def stA(i):
                b, jp = steps[i]
                if jp == NPG:
                    vn = vnewR.next()
                    DMA("sp", vn[:, :], Vs[b * 8:(b + 1) * 8, :], r=[Vs], w=[vn])
                    return (None, vn)
                it = idxR.next()
                ev_copy("dve", it[:, :], idxall[:, b * NPG + jp:b * NPG + jp + 1], r=[idxall], w=[it])
                kp = KpgR.next(); vp = VpgR.next()
                S.dma("pool", lambda e, kp=kp, it=it: e.indirect_dma_start(
                    out=kp[:, :], out_offset=None, in_=cache_k[:, :],
                    in_offset=bass.IndirectOffsetOnAxis(ap=it[:, :], axis=0)), r=[it], w=[kp])
                S.dma("pool", lambda e, vp=vp, it=it: e.indirect_dma_start(
                    out=vp[:, :], out_offset=None, in_=cache_v[:, :],
                    in_offset=bass.IndirectOffsetOnAxis(ap=it[:, :], axis=0)), r=[it], w=[vp])
                ktb = KTbR.next(); vpb = VpbR.next()
                for half in range(2):
                    pbt = xR.next()
                    for q in range(4):
                        hh = half * 4 + q
                        TR(pbt[:, q * 128:(q + 1) * 128], kp[:, hh * 128:(hh + 1) * 128], IDF(), r=[kp, mats], w=[pbt])
                    ev_copy(evR.next(), ktb[:, half * 4:(half + 1) * 4, :], pbt[:, :].rearrange("p (q t) -> p q t", q=4), r=[pbt], w=[ktb])
                ev_copy(evR.next(), vpb[:, :], vp[:, :], r=[vp], w=[vpb])
                return (ktb, vpb)

            def stB(i, ktb):
                b, jp = steps[i]
                ps = scanR.next()
                et = ETs.next()
                if jp < NPG:
                    for k in range(8):
                        MM(ps[:, k * 32:(k + 1) * 32], ktb[:, k, :], Qbd[:, b, k, :], True, True, r=[ktb, Qbd], w=[ps])
                    ACT(et[:, 0:256], ps[:, 0:256], AF.Exp, r=[ps], w=[et], scale=0.125)
                else:
                    for k in range(8):
                        MM(ps[0:8, k * 32:(k + 1) * 32], KTs[:, k, b * 8:(b + 1) * 8], Qbd[:, b, k, :], True, True, r=[KTs, Qbd], w=[ps])
                    ACT(et[0:8, 0:256], ps[0:8, 0:256], AF.Exp, r=[ps], w=[et], scale=0.125)
                    TTo("dve", et[0:8, 0:256].rearrange("p (a t) -> p a t", t=8), et[0:8, 0:256].rearrange("p (a t) -> p a t", t=8),
                        mats[0:8, 2:3, 0:8].to_broadcast([8, 32, 8]), ALU.mult, r=[et, mats], w=[et])
                return et

            accs = {}

            def stC(i, et, vsrc):
                b, jp = steps[i]
                new = jp == NPG
                n = 8 if new else 128
                if jp == 0:
                    accs[b] = (accO.next(), accL.next())
                aO, aL = accs[b]
                MM(aL[:, 0:256], onesb[0:n, :], et[0:n, 0:256], jp == 0, new, r=[onesb, et], w=[aL])
                if jp == 0:
                    MM(aO[:, 0:256], zerosb[0:n, :], et[0:n, 0:256], True, False, r=[zerosb, et], w=[aO])
                for k in range(8):
                    MM(aO[:, k * 32:(k + 1) * 32], vsrc[0:n, k * 128:(k + 1) * 128], et[0:n, k * 32:(k + 1) * 32], False, new and k == 7,
                       r=[vsrc, et], w=[aO])
                if new:
                    On5 = Ont[:, 0:256].rearrange("p (k m g t) -> p k m g t", k=8, m=2, g=2)
                    finish_head(aO, aL, 256, OTf[:, :, b * 8:(b + 1) * 8].rearrange("p (k g) t -> p k g t", g=2),
                                On5[:, :, 0, :, :], On5[:, :, 1, :, :])

            resA = {}; resB = {}
            resA[0] = stA(0)
            for i in range(NS + 1):
                if i + 1 < NS:
                    resA[i + 1] = stA(i + 1)
                if i < NS:
                    resB[i] = stB(i, resA[i][0])
                if i - 1 >= 0:
                    stC(i - 1, resB.pop(i - 1), resA.pop(i - 1)[1])
            if l == DEPTH - 1:
                attn_post(xt, m_, lam_init, j, y_s, new_out_buf())
            else:
                attn_post(xt, m_, lam_init, j, xb_s, xbuf_s)
            ck(f"atts{l}")

    try:
        for l in range(NA):
            if stage >= 1 + l and not _os.environ.get("ONLYS"):
                rwkv_layer(l)
        if stage >= 3:
            phase_b()
    except StopBuild:
        pass

    S.drain("sp")
    S.emit()
    return nc


_STAGE = 99


def kernel(x_prompt, x_sample, cache_k, cache_v, page_table, state_wkv, state_shift, c_prompt, c_sample,
           ada_w, ada_b, ln_g, ln_b, mu, w_rkvg, w0, w_decay1, w_decay2, a0, w_a1, w_a2, k_k, k_a, r_k,
           gn_g, gn_b, w_o_a, ada_kv_w, ada_kv_b, w_kv, w_qg, lam_qk, subln_g, w_o_b):
    f = lambda a: np.ascontiguousarray(np.asarray(a, dtype=np.float32))
    B, T, _ = x_prompt.shape
    DB, TS, _ = x_sample.shape
    NPHYS = cache_k.shape[0]
    NPG = page_table.shape[1]
    assert B == NCORE and DB == 16 * NCORE and TS == 8, (B, NCORE, DB)
    nc = build_program(T, NPG, NPHYS, stage=_STAGE)
    consts = make_consts(T, NPG * 128)
    shared = {
        "cache_k": f(cache_k).reshape(NPHYS * 128, 1024), "cache_v": f(cache_v).reshape(NPHYS * 128, 1024),
        "ada_w": f(ada_w), "ada_b": f(ada_b), "ln_g": f(ln_g), "ln_b": f(ln_b), "mu": f(mu),
        "w_rkvg": f(w_rkvg), "w0": f(w0), "w_decay1": f(w_decay1), "w_decay2": f(w_decay2), "a0": f(a0),
        "w_a1": f(w_a1), "w_a2": f(w_a2), "k_k": f(k_k), "k_a": f(k_a), "r_k": f(r_k).reshape(NA, E),
        "gn_g": f(gn_g), "gn_b": f(gn_b), "w_o_a": f(w_o_a), "ada_kv_w": f(ada_kv_w),
        "ada_kv_b": f(ada_kv_b).reshape(1, 2 * D), "w_kv": f(w_kv), "w_qg": f(w_qg),
        "lam_qk": f(lam_qk).reshape(2, 256), "subln_g": f(subln_g), "w_o_b": f(w_o_b),
        "c_mats": consts["mats"], "c_mask4": consts["mask4"], "c_red": consts["red"],
        "c_maskb": consts["maskb"], "c_maskb8": consts["maskb8"],
        "c_rope_p": consts["rope_p"], "c_rope_s": consts["rope_s"],
    }
    xpf, xsf = f(x_prompt), f(x_sample)
    swf, ssf = f(state_wkv), f(state_shift)
    cpf, csf = f(c_prompt), f(c_sample)
    pt = np.ascontiguousarray(np.asarray(page_table, dtype=np.int32))
    in_maps = []
    for c in range(NCORE):
        bs = slice(16 * c, 16 * (c + 1))
        m = dict(shared)
        m["xp"] = xpf[c]
        m["xs"] = xsf[bs].reshape(128, D)
        m["ptab"] = pt[bs].reshape(1, 16 * NPG)
        m["swkv"] = np.ascontiguousarray(swf[:, bs])
        m["sshift"] = np.ascontiguousarray(ssf[:, bs])
        m["cvec"] = np.concatenate([cpf[c:c + 1], csf[bs]], axis=0)
        in_maps.append(m)
    res = run_bass_kernel_spmd(nc, in_maps, core_ids=list(range(NCORE))).results
    g = lambda name: [np.asarray(r[name]) for r in res]
    if _os.environ.get("DBGIDX"):
        global _LAST_RES
        _LAST_RES = res
    y_prompt = np.stack(g("y_p"), 0).reshape(B, T, D)
    y_sample = np.concatenate(g("y_s"), 0).reshape(DB, TS, D)
    k_prompt = np.stack(g("k_p"), 0).reshape(B, T, 8, 128)
    v_prompt = np.stack(g("v_p"), 0).reshape(B, T, 8, 128)
    k_sample = np.concatenate(g("k_s"), 0).reshape(DB, TS, 8, 128)
    v_sample = np.concatenate(g("v_s"), 0).reshape(DB, TS, 8, 128)
    wkv_prompt = np.stack(g("wkv_p"), 1).reshape(NA, B, 32, 64, 64)
    shift_prompt = np.stack(g("shift_p"), 1).reshape(NA, B, D)
    wkv_sample = np.concatenate(g("wkv_s"), 1).reshape(NA, DB, 32, 64, 64)
    shift_sample = np.concatenate(g("shift_s"), 1).reshape(NA, DB, D)
    return (y_prompt.astype(np.float32), y_sample.astype(np.float32), k_prompt.astype(np.float32),
            v_prompt.astype(np.float32), k_sample.astype(np.float32), v_sample.astype(np.float32),
            wkv_prompt.astype(np.float32), shift_prompt.astype(np.float32), wkv_sample.astype(np.float32),
            shift_sample.astype(np.float32))
```
